# Optimizing a Trainium2 kernel written in Bass

```python
import math
import jax, jax.numpy as jnp
from jax import lax
import numpy as np

D_MODEL = 1024
BATCH = 2
SEQ = 8192
DEPTH = 1

N_META = 16
RWKV_HEADS = 8
RWKV_HEAD_DIM = 64
RWKV_DIM = RWKV_HEADS * RWKV_HEAD_DIM
DECAY_LORA = 64
AAA_LORA = 64
GATE_LORA = 128
RWKV_GN_EPS = RWKV_HEAD_DIM * 1e-5
ATTN_Q_HEADS = 8
ATTN_KV_HEADS = 2
ATTN_HEAD_DIM = 64
ATTN_GROUP = ATTN_Q_HEADS // ATTN_KV_HEADS
ATTN_DIM = ATTN_Q_HEADS * ATTN_HEAD_DIM
KV_DIM = ATTN_KV_HEADS * ATTN_HEAD_DIM
WINDOW = 128
BLOCK = 128
REL_BUCKETS = 32
REL_MAX_EXACT = 16
REL_MAX_DIST = 128
D_FF = 2816
CONV_WIDTH = 3
NORM_EPS = 1e-6
MASK_VALUE = -1e30

RWKV_IN = 3 * RWKV_DIM + DECAY_LORA + AAA_LORA + GATE_LORA
ATTN_IN = ATTN_DIM + 2 * KV_DIM
IN_DIM = RWKV_IN + ATTN_IN
MIX_DIM = RWKV_DIM + ATTN_DIM

kernel_name = "hymba_rwkv7_swa_convffn_layer"


def rms_norm(x, w, eps=NORM_EPS):
    xf = x.astype(jnp.float32)
    y = xf * lax.rsqrt(jnp.mean(xf * xf, axis=-1, keepdims=True) + eps)
    return (y * w.astype(jnp.float32)).astype(x.dtype)


def shift_right(x, n):
    return jnp.pad(x, ((0, 0), (n, 0), (0, 0)))[:, : x.shape[1]]


def t5_bucket(d):
    d = jnp.maximum(d, 0)
    df = jnp.maximum(d, REL_MAX_EXACT).astype(jnp.float32)
    large = REL_MAX_EXACT + (jnp.log(df / REL_MAX_EXACT) / math.log(REL_MAX_DIST / REL_MAX_EXACT)
                             * (REL_BUCKETS - REL_MAX_EXACT)).astype(jnp.int32)
    large = jnp.minimum(large, REL_BUCKETS - 1)
    return jnp.where(d < REL_MAX_EXACT, d, large)


def rwkv7_step(S, inp):
    r, w, k, v, a, b = inp
    sa = jnp.einsum('bhij,bhj->bhi', S, a)
    S = S * w[:, :, None, :] + sa[..., None] * b[:, :, None, :] + v[..., None] * k[:, :, None, :]
    y = jnp.einsum('bhij,bhj->bhi', S, r)
    return S, y


def rwkv7_mixer(p, mu, w0, w_dec_up, a0, a_up, g_up, k_k, k_a, r_k, lnx_w, lnx_b):
    B, L, _ = p.shape
    H, N = RWKV_HEADS, RWKV_HEAD_DIM
    p = p + (shift_right(p, 1) - p) * mu
    r, k, v, wd, ad, gd = jnp.split(
        p, [RWKV_DIM, 2 * RWKV_DIM, 3 * RWKV_DIM, 3 * RWKV_DIM + DECAY_LORA,
            3 * RWKV_DIM + DECAY_LORA + AAA_LORA], axis=-1)
    w_log = -jax.nn.softplus(-(w0 + jnp.tanh(wd) @ w_dec_up)) - 0.5
    decay = jnp.exp(-jnp.exp(w_log.astype(jnp.float32)))
    a = jax.nn.sigmoid(a0 + ad @ a_up)
    g = jax.nn.sigmoid(gd) @ g_up
    heads = lambda t: t.reshape(B, L, H, N).astype(jnp.float32)
    kk = heads(k * k_k)
    kk = kk / jnp.maximum(jnp.sqrt(jnp.sum(kk * kk, axis=-1, keepdims=True)), 1e-12)
    k = k * (1.0 + (a - 1.0) * k_a)
    r_h, k_h, v_h, a_h, w_h = heads(r), heads(k), heads(v), heads(a), heads(decay)
    xs = tuple(jnp.moveaxis(t, 1, 0) for t in (r_h, w_h, k_h, v_h, -kk, kk * a_h))
    S0 = jnp.zeros((B, H, N, N), jnp.float32)
    _, y = lax.scan(rwkv7_step, S0, xs)
    y = jnp.moveaxis(y, 0, 1)
    mean = jnp.mean(y, axis=-1, keepdims=True)
    var = jnp.mean(jnp.square(y - mean), axis=-1, keepdims=True)
    y = ((y - mean) * lax.rsqrt(var + RWKV_GN_EPS)).reshape(B, L, RWKV_DIM)
    y = y * lnx_w.astype(jnp.float32) + lnx_b.astype(jnp.float32)
    bonus = jnp.sum(r_h * k_h * r_k.astype(jnp.float32), axis=-1, keepdims=True) * v_h
    y = (y + bonus.reshape(B, L, RWKV_DIM)) * g.astype(jnp.float32)
    return y.astype(p.dtype)


def swa_mixer(p, q_norm_w, k_norm_w, rel_bias, sinks):
    B, L, _ = p.shape
    q, k, v = jnp.split(p, [ATTN_DIM, ATTN_DIM + KV_DIM], axis=-1)
    q = rms_norm(q.reshape(B, L, ATTN_Q_HEADS, ATTN_HEAD_DIM), q_norm_w)
    k = rms_norm(k.reshape(B, L, ATTN_KV_HEADS, ATTN_HEAD_DIM), k_norm_w)
    v = v.reshape(B, L, ATTN_KV_HEADS, ATTN_HEAD_DIM)
    meta_k, meta_v = k[:, :N_META], v[:, :N_META]
    pad = BLOCK - N_META
    padf = lambda t: jnp.pad(t, ((0, 0), (pad, 0), (0, 0), (0, 0)))
    q, k, v = padf(q), padf(k), padf(v)
    nb = (L + pad) // BLOCK
    qb = q.reshape(B, nb, BLOCK, ATTN_KV_HEADS, ATTN_GROUP, ATTN_HEAD_DIM)

    def band(t):
        t = t.reshape(B, nb, BLOCK, ATTN_KV_HEADS, ATTN_HEAD_DIM)
        prev = jnp.pad(t, ((0, 0), (1, 0), (0, 0), (0, 0), (0, 0)))[:, :nb]
        return jnp.concatenate([prev, t], axis=2)

    kband, vband = band(k), band(v)
    scale = ATTN_HEAD_DIM ** -0.5
    s_meta = jnp.einsum('bnqhgd,bmhd->bhgnqm', qb, meta_k)
    s_band = jnp.einsum('bnqhgd,bnkhd->bhgnqk', qb, kband)
    qpos = jnp.arange(nb)[:, None] * BLOCK + jnp.arange(BLOCK)[None, :] - pad
    kpos = (jnp.arange(nb)[:, None] - 1) * BLOCK + jnp.arange(2 * BLOCK)[None, :] - pad
    d_band = qpos[:, :, None] - kpos[:, None, :]
    mask_band = (kpos[:, None, :] >= N_META) & (d_band >= 0) & (d_band < WINDOW)
    d_meta = qpos[:, :, None] - jnp.arange(N_META)[None, None, :]
    mask_meta = d_meta >= 0
    d_all = jnp.concatenate([d_meta, d_band], axis=-1)
    mask_all = jnp.concatenate([mask_meta, mask_band], axis=-1)
    bias = rel_bias[t5_bucket(d_all)]
    bias = jnp.moveaxis(bias, -1, 0).reshape(ATTN_KV_HEADS, ATTN_GROUP, nb, BLOCK, -1)
    s = jnp.concatenate([s_meta, s_band], axis=-1).astype(jnp.float32) * scale + bias.astype(jnp.float32)
    s = jnp.where(mask_all, s, MASK_VALUE)
    sink = sinks.astype(jnp.float32).reshape(ATTN_KV_HEADS, ATTN_GROUP, 1, 1, 1)
    m = jnp.maximum(jnp.max(s, axis=-1, keepdims=True), sink)
    e = jnp.exp(s - m)
    prob = (e / (jnp.sum(e, axis=-1, keepdims=True) + jnp.exp(sink - m))).astype(v.dtype)
    o = (jnp.einsum('bhgnqm,bmhd->bnqhgd', prob[..., :N_META], meta_v)
         + jnp.einsum('bhgnqk,bnkhd->bnqhgd', prob[..., N_META:], vband))
    return o.reshape(B, nb * BLOCK, ATTN_DIM)[:, pad:]


def conv_ffn(u, w_up, conv_w, conv_b, w_down):
    hid = u @ w_up
    hid = (hid * conv_w[0] + shift_right(hid, 1) * conv_w[1]
           + shift_right(hid, 2) * conv_w[2] + conv_b)
    gate, val = jnp.split(hid, 2, axis=-1)
    return (jax.nn.silu(gate) * val) @ w_down


def setup_inputs(seed: int = 0) -> dict:
    key = jax.random.key(seed)
    ks = jax.random.split(key, 32)
    nrm = lambda k, shape, s: jax.random.normal(k, shape, jnp.float32) * s
    near_one = lambda k, shape: 1.0 + 0.02 * jax.random.normal(k, shape, jnp.float32)
    return {
        "x": nrm(ks[0], (BATCH, SEQ, D_MODEL), 1.0),
        "meta_tokens": nrm(ks[1], (N_META, D_MODEL), 1.0),
        "rel_bias": nrm(ks[2], (REL_BUCKETS, ATTN_Q_HEADS), 0.5),
        "norm1_w": near_one(ks[3], (DEPTH, D_MODEL)),
        "w_in": nrm(ks[4], (DEPTH, D_MODEL, IN_DIM), D_MODEL ** -0.5),
        "shift_mu": jax.random.uniform(ks[5], (DEPTH, RWKV_IN), jnp.float32, 0.0, 1.0),
        "decay_w0": jax.random.uniform(ks[6], (DEPTH, RWKV_DIM), jnp.float32, -5.0, -1.0),
        "decay_up": nrm(ks[7], (DEPTH, DECAY_LORA, RWKV_DIM), 0.1 * DECAY_LORA ** -0.5),
        "aaa_a0": nrm(ks[8], (DEPTH, RWKV_DIM), 0.1),
        "aaa_up": nrm(ks[9], (DEPTH, AAA_LORA, RWKV_DIM), 0.5 * AAA_LORA ** -0.5),
        "gate_up": nrm(ks[10], (DEPTH, GATE_LORA, RWKV_DIM), GATE_LORA ** -0.5),
        "k_k": 0.85 + nrm(ks[11], (DEPTH, RWKV_DIM), 0.02),
        "k_a": near_one(ks[12], (DEPTH, RWKV_DIM)),
        "r_k": nrm(ks[13], (DEPTH, RWKV_HEADS, RWKV_HEAD_DIM), 0.1),
        "lnx_w": near_one(ks[14], (DEPTH, RWKV_DIM)),
        "lnx_b": nrm(ks[15], (DEPTH, RWKV_DIM), 0.02),
        "q_norm_w": near_one(ks[16], (DEPTH, ATTN_HEAD_DIM)),
        "k_norm_w": near_one(ks[17], (DEPTH, ATTN_HEAD_DIM)),
        "sinks": nrm(ks[18], (DEPTH, ATTN_Q_HEADS), 1.0),
        "w_out": nrm(ks[19], (DEPTH, MIX_DIM, D_MODEL), MIX_DIM ** -0.5),
        "norm2_w": near_one(ks[20], (DEPTH, D_MODEL)),
        "w_up": nrm(ks[21], (DEPTH, D_MODEL, 2 * D_FF), D_MODEL ** -0.5),
        "conv_w": nrm(ks[22], (DEPTH, CONV_WIDTH, 2 * D_FF), CONV_WIDTH ** -0.5),
        "conv_b": nrm(ks[23], (DEPTH, 2 * D_FF), 0.02),
        "w_down": nrm(ks[24], (DEPTH, D_FF, D_MODEL), D_FF ** -0.5),
    }


def reference(x, meta_tokens, rel_bias, norm1_w, w_in, shift_mu, decay_w0, decay_up, aaa_a0,
              aaa_up, gate_up, k_k, k_a, r_k, lnx_w, lnx_b, q_norm_w, k_norm_w, sinks, w_out,
              norm2_w, w_up, conv_w, conv_b, w_down):
    B = x.shape[0]
    meta = jnp.broadcast_to(meta_tokens[None].astype(x.dtype), (B, N_META, D_MODEL))
    h = jnp.concatenate([meta, x], axis=1)
    for layer in range(DEPTH):
        u = rms_norm(h, norm1_w[layer])
        p = u @ w_in[layer]
        y_rwkv = rwkv7_mixer(p[..., :RWKV_IN], shift_mu[layer], decay_w0[layer], decay_up[layer],
                             aaa_a0[layer], aaa_up[layer], gate_up[layer], k_k[layer], k_a[layer],
                             r_k[layer], lnx_w[layer], lnx_b[layer])
        y_attn = swa_mixer(p[..., RWKV_IN:], q_norm_w[layer], k_norm_w[layer], rel_bias, sinks[layer])
        h = h + jnp.concatenate([y_rwkv, y_attn], axis=-1) @ w_out[layer]
        u = rms_norm(h, norm2_w[layer])
        h = h + conv_ffn(u, w_up[layer], conv_w[layer], conv_b[layer], w_down[layer])
    return h[:, N_META:]
```

```python
import contextlib
import numpy as np
import concourse.bass as bass
import concourse.mybir as mybir
from concourse.bass_utils import run_bass_kernel_spmd

F32 = mybir.dt.float32
BF16 = mybir.dt.bfloat16
ALU = mybir.AluOpType
AF = mybir.ActivationFunctionType

D = 1024
NCS = 2
SBW = NCS * 128
NCH = 68
NSB = NCH // NCS
NTOK = NCH * 128
HALO_CH = 51
KV_CH = 50
OWN0 = 52
NPROJ = 21
DFF = 2816
NFC = 22
CW = 0.6065306597126334


class TB:
    def __init__(self, h, name):
        self.h = h
        self.name = name
        self.last_w = None
        self.reads = {}

    def __getitem__(self, idx):
        return V(self.h[idx], self)


class V:
    def __init__(self, ap, buf):
        self.ap = ap
        self.buf = buf


class Op:
    __slots__ = ("eng", "fns", "reads", "writes", "dma", "cost", "preds", "succs", "idx", "prio", "npred", "fin", "pos", "func", "dr", "why", "tag", "alt", "dr2")

    def __init__(self, eng, fn, reads, writes, dma, cost, func=None):
        self.eng = eng
        self.fns = [fn]
        self.reads = reads
        self.writes = writes
        self.dma = dma
        self.cost = cost
        self.preds = set()
        self.succs = []
        self.func = func
        self.alt = None


import os as _osx
XLAT = float(_osx.environ.get('KXLAT', '600'))


class Sched:
    def __init__(self, nc, es):
        self.nc = nc
        self.es = es
        self.engs = ["pe", "act", "dve", "pool", "sp"]
        self.sem = {}
        self.cnt = {}
        self.known = {e: {} for e in self.engs}
        self.prog = {e: [] for e in self.engs}
        for e in self.engs:
            self.sem[e] = es.enter_context(nc.semaphore("s_" + e))
            self.cnt[e] = 0
        self.ops = []
        self.open_mm = None
        self.final = []

    def dma_sem(self, name):
        key = "d_" + name
        if key not in self.sem:
            self.sem[key] = self.es.enter_context(self.nc.semaphore(key))
            self.cnt[key] = 0
        return key

    def op(self, eng, fn, reads, writes, dma=None, cost=100.0, append_to=None, func=None):
        if append_to is not None:
            o = append_to
            o.fns.append(fn)
            o.cost += cost
            for b in reads:
                if b not in o.reads:
                    o.reads.append(b)
            for b in writes:
                if b not in o.writes:
                    o.writes.append(b)
        else:
            o = Op(eng, fn, list(reads), list(writes), dma, cost, func)
            o.idx = len(self.ops)
            self.ops.append(o)
        return o

    def _build_deps(self, ops):
        for o in ops:
            for b in o.reads:
                if b.last_w is not None and b.last_w is not o:
                    o.preds.add(b.last_w)
            for b in o.writes:
                if b.last_w is not None and b.last_w is not o:
                    o.preds.add(b.last_w)
                for r in b.reads.values():
                    if r is not o:
                        o.preds.add(r)
            for b in o.reads:
                b.reads[id(o)] = o
            for b in o.writes:
                b.last_w = o
                b.reads = {}

    def flush(self, barrier=True):
        import heapq
        ops = self.ops
        self.ops = []
        self.open_mm = None
        if not ops:
            return
        self._build_deps(ops)
        inset = set(id(o) for o in ops)
        for o in ops:
            o.preds = set(p for p in o.preds if id(p) in inset)
            o.npred = len(o.preds)
            for p in o.preds:
                p.succs.append(o)
        for o in reversed(ops):
            m = 0.0
            for s in o.succs:
                if s.prio > m:
                    m = s.prio
            o.prio = m + o.cost + (XLAT if o.succs else 0.0)
        ready = {e: [] for e in self.engs}
        def push_ready(o):
            if o.alt is None:
                dr = 0.0
                for p in o.preds:
                    t = p.fin + (0.0 if p.eng == o.eng else XLAT)
                    if t > dr:
                        dr = t
                o.dr = dr
                ready[o.eng].append(o)
            else:
                o.dr2 = {}
                for e2 in o.alt:
                    dr = 0.0
                    for p in o.preds:
                        t = p.fin + (0.0 if p.eng == e2 else XLAT)
                        if t > dr:
                            dr = t
                    o.dr2[e2] = dr
                    ready[e2].append(o)
        for o in ops:
            if o.npred == 0:
                push_ready(o)
        free = {e: 0.0 for e in self.engs}
        lastfunc = {"act": None}
        order = {e: [] for e in self.engs}
        nleft = len(ops)
        while nleft:
            best = None
            for e in self.engs:
                rl = ready[e]
                if not rl:
                    continue
                fe = free[e]
                mn = min((o.dr if o.alt is None else o.dr2[e]) for o in rl)
                if mn < fe:
                    mn = fe
                pick = None
                for o in rl:
                    d_ = o.dr if o.alt is None else o.dr2[e]
                    st_ = d_ if d_ > fe else fe
                    if st_ <= mn + 150.0:
                        pen = 0.0
                        if e == "act" and o.func is not None and lastfunc["act"] is not None and o.func != lastfunc["act"]:
                            pen = 1300.0
                        sc = o.prio - pen
                        if pick is None or sc > pick[2]:
                            pick = (st_, o, sc)
                if best is None or pick[0] < best[0]:
                    best = (pick[0], pick[1], e)
            st_, o, e = best
            ready[e].remove(o)
            if o.alt is not None:
                for e2 in o.alt:
                    if e2 != e:
                        ready[e2].remove(o)
                o.eng = e
                o.fns = [o.alt[e]]
                o.dr = o.dr2[e]
            if o.dr >= free[e]:
                bp = None
                for p in o.preds:
                    t = p.fin + (0.0 if p.eng == e else XLAT)
                    if bp is None or t > bp[0]:
                        bp = (t, p)
                o.why = ("dep", bp[1] if bp else None)
            else:
                o.why = ("eng", order[e][-1] if order[e] else None)
            issue = o.cost
            if o.dma is not None:
                o.fin = st_ + o.cost
                free[e] = st_ + 60.0 * len(o.fns)
            else:
                o.fin = st_ + issue
                free[e] = o.fin
            if e == "act" and o.func is not None:
                lastfunc["act"] = o.func
            order[e].append(o)
            nleft -= 1
            for s in o.succs:
                s.npred -= 1
                if s.npred == 0:
                    push_ready(s)
        self.est = max(free.values())
        self.busy = {e: sum(o.cost for o in order[e]) for e in self.engs}
        import os as _os
        if _os.environ.get("KCRIT"):
            last = max(ops, key=lambda o: o.fin)
            cnt_ = {}
            o = last
            n_ = 0
            while o is not None and n_ < 200000:
                kind, p = o.why
                key_ = (kind, o.eng, (o.writes[0].name if o.writes else "-")[:6])
                cnt_[key_] = cnt_.get(key_, 0.0) + o.cost + (XLAT if kind == "dep" and p is not None and p.eng != o.eng else 0.0)
                o = p
                n_ += 1
            for kk__, vv in sorted(cnt_.items(), key=lambda kv: -kv[1])[:40]:
                print("[crit]", kk__, int(vv))
        print('[sched] region est', int(self.est), 'busy', {e: int(v) for e, v in self.busy.items()}, 'nops', {e: len(order[e]) for e in self.engs})
        for e in self.engs:
            for o in order[e]:
                key = o.dma if o.dma is not None else e
                step = 16 * len(o.fns) if o.dma is not None else 1
                self.cnt[key] += step
                o.pos = (key, self.cnt[key])
        for e in self.engs:
            for o in order[e]:
                need = {}
                for p in o.preds:
                    k_, v_ = p.pos
                    if k_ == "pe" and e == "pe":
                        continue
                    if need.get(k_, 0) < v_:
                        need[k_] = v_
                for k_, v_ in need.items():
                    if self.known[e].get(k_, 0) >= v_:
                        continue
                    self.prog[e].append(("wait", k_, v_))
                    self.known[e][k_] = v_
                n = len(o.fns)
                for i_, fn in enumerate(o.fns):
                    if o.dma is not None:
                        self.prog[e].append(("op", fn, o.pos[0], 16))
                    else:
                        self.prog[e].append(("op", fn, o.pos[0] if i_ == n - 1 else None, 1))
        for o in ops:
            o.preds = None
            o.succs = None
        if barrier:
            for e in self.engs:
                for key, v in self.cnt.items():
                    if v > 0 and key != e and self.known[e].get(key, 0) < v:
                        self.prog[e].append(("wait", key, v))
                        self.known[e][key] = v
        for o in ops:
            for b in o.reads:
                b.reads = {}
            for b in o.writes:
                b.last_w = None
                b.reads = {}

    def emit(self, block):
        sems = self.sem

        def run(e, name):
            for it in self.prog[name]:
                if it[0] == "wait":
                    e.wait_ge(sems[it[1]], it[2])
                else:
                    ins = it[1](e)
                    if it[2] is not None:
                        ins.then_inc(sems[it[2]], it[3])
            if name == "sp":
                for k_, v_ in self.cnt.items():
                    if k_.startswith("d_") and v_ > 0:
                        e.wait_ge(sems[k_], v_)

        @block.tensor
        def _(e):
            run(e, "pe")

        @block.scalar
        def _(e):
            run(e, "act")

        @block.vector
        def _(e):
            run(e, "dve")

        @block.gpsimd
        def _(e):
            run(e, "pool")

        @block.sync
        def _(e):
            run(e, "sp")


class Sub:
    def __init__(self, tb, off, width):
        self.tb = tb
        self.off = off
        self.width = width
        self.name = tb.name

    def __getitem__(self, idx):
        rows, cols = idx
        a = 0 if cols.start is None else cols.start
        b_ = self.width if cols.stop is None else cols.stop
        return self.tb[rows, self.off + a:self.off + b_]


def bufs(*vs):
    out = []
    for v in vs:
        if isinstance(v, V) and v.buf not in out:
            out.append(v.buf)
    return out


def A(x):
    return x.ap if isinstance(x, V) else x


def fsz(v):
    n = 1
    for s in v.ap.shape[1:]:
        n *= int(s)
    return n


class K:
    def __init__(self, s):
        self.s = s

    def mm(self, out, lhsT, rhs, start=True, stop=True):
        n = fsz(out)
        c = max(105.0, 60.0 + n * 0.42)
        if lhsT.ap.dtype == F32:
            c *= 4
        fn = lambda e: e.matmul(out.ap, lhsT.ap, rhs.ap, start=start, stop=stop)
        if start:
            o = self.s.op("pe", fn, bufs(lhsT, rhs), bufs(out), cost=c)
            self.s.open_mm = None if stop else o
        else:
            o = self.s.op("pe", fn, bufs(lhsT, rhs), bufs(out), cost=c, append_to=self.s.open_mm)
            if stop:
                self.s.open_mm = None

    def tr(self, out, in_, ident):
        self.s.op("pe", lambda e: e.transpose(out.ap, in_.ap, ident.ap), bufs(in_, ident), bufs(out), cost=130.0)

    def act(self, out, in_, func, bias=0.0, scale=1.0, accum=None, eng="act"):
        kw = {}
        if accum is not None:
            kw["accum_out"] = accum.ap
        fam = {AF.Exp: "exp", AF.Tanh: "exp", AF.Sigmoid: "sig", AF.Sqrt: "sqrt", AF.Silu: "silu"}.get(func)
        self.s.op("act", lambda e: e.activation(out=out.ap, in_=in_.ap, func=func, bias=A(bias), scale=A(scale), **kw),
                  bufs(in_, bias, scale), bufs(out, accum), cost=200.0 + fsz(out) * 0.75, func=fam)

    def raw(self, eng, fn, reads, writes, n):
        self.s.op(eng, fn, reads, writes, cost=160.0 + n * 0.9)

    def tt(self, out, a, b, op, eng="dve"):
        self.s.op(eng, lambda e: e.tensor_tensor(out=out.ap, in0=a.ap, in1=b.ap, op=op), bufs(a, b), bufs(out),
                  cost=120.0 + fsz(out) * 0.85)

    def ts(self, out, a, s1, s2, op0, op1=None, eng="dve"):
        c = 120.0 + fsz(out) * 0.6
        if op1 is None:
            self.s.op(eng, lambda e: e.tensor_scalar(out=out.ap, in0=a.ap, scalar1=A(s1), scalar2=None, op0=op0),
                      bufs(a, s1), bufs(out), cost=c)
        else:
            self.s.op(eng, lambda e: e.tensor_scalar(out=out.ap, in0=a.ap, scalar1=A(s1), scalar2=A(s2), op0=op0, op1=op1),
                      bufs(a, s1, s2), bufs(out), cost=c)

    def stt(self, out, a, sc, b, op0, op1):
        self.s.op("dve", lambda e: e.scalar_tensor_tensor(out=out.ap, in0=a.ap, scalar=A(sc), in1=b.ap, op0=op0, op1=op1),
                  bufs(a, sc, b), bufs(out), cost=120.0 + fsz(out) * 0.85)

    def cp(self, out, in_, eng="dve"):
        if eng == "act":
            o = self.s.op("act", lambda e: e.activation(out=out.ap, in_=in_.ap, func=AF.Copy), bufs(in_), bufs(out),
                          cost=200.0 + fsz(out) * 0.75)
            o.alt = {"act": o.fns[0], "dve": (lambda e: e.tensor_copy(out=out.ap, in_=in_.ap))}
        else:
            self.s.op(eng, lambda e: e.tensor_copy(out=out.ap, in_=in_.ap), bufs(in_), bufs(out), cost=120.0 + fsz(out) * 0.6)

    def recip(self, out, in_):
        self.s.op("dve", lambda e: e.reciprocal(out=out.ap, in_=in_.ap), bufs(in_), bufs(out), cost=150.0 + fsz(out) * 5.5)

    def scan(self, out, ones, x):
        self.s.op("dve", lambda e: e.tensor_tensor_scan(out=out.ap, data0=ones.ap, data1=x.ap, initial=0.0,
                                                        op0=ALU.mult, op1=ALU.add), bufs(ones, x), bufs(out), cost=150.0 + fsz(out) * 2.1)

    def memset(self, out, val, eng="dve"):
        self.s.op(eng, lambda e: e.memset(out.ap, val), [], bufs(out), cost=100.0 + fsz(out) * 0.3)

    def raw_dma(self, out_tb, in_tb, fn, name, eng, nbytes):
        key = self.s.dma_sem(name)
        return self.s.op(eng, fn, [in_tb], [out_tb], dma=key, cost=2200.0 + nbytes / 60.0)

    def dma(self, out, in_, name, eng="sp", group=None):
        key = self.s.dma_sem(name)
        nbytes = 128 * fsz(out) * 4
        c = 2200.0 + nbytes / 60.0
        fn = lambda e: e.dma_start(out=out.ap, in_=in_.ap)
        if group is not None:
            return self.s.op(eng, fn, bufs(in_), bufs(out), dma=key, cost=nbytes / 120.0, append_to=group)
        return self.s.op(eng, fn, bufs(in_), bufs(out), dma=key, cost=c)


def build_program(debug=False, trunc=None):
    import os
    trunc = trunc or os.environ.get('KTRUNC', '')
    kstep = int(os.environ.get('KSTEP', '99'))
    kvar = os.environ.get('KVAR', '')
    nc = bass.Bass("TRN2", target_bir_lowering=False)
    es = contextlib.ExitStack()
    S = Sched(nc, es)
    k = K(S)
    W_ = SBW

    def dram(name, shape, dt=F32, kind="ExternalInput"):
        return TB(nc.dram_tensor(name, shape, dt, kind=kind).ap(), name)

    def sb(name, shape, dt=F32):
        return TB(es.enter_context(nc.sbuf_tensor(name, shape, dt)), name)

    def ps(name, shape, dt=F32):
        return TB(es.enter_context(nc.psum_tensor(name, shape, dt)), name)

    xl = dram("xl", [NTOK, D])
    xmeta = dram("xmeta", [128, D])
    w_in_d = dram("w_in", [D, NPROJ * 128])
    w_out_d = dram("w_out", [D, D])
    w_up_d = dram("w_up", [NFC, D, 256])
    w_down_d = dram("w_down", [DFF, D])
    lut_d = dram("lut", [128, 512])
    gup_d = dram("gup", [128, 512])
    pc_d = dram("pcols", [128, 64])
    n1_d = dram("n1w", [128, D])
    n2_d = dram("n2w", [128, D])
    cw_d = dram("convw", [128, 44 * 3])
    cb_d = dram("convb", [128, 44])
    cst_d = dram("consts", [128, 448])
    msk_d = dram("masks", [128, 3 * 512])
    bias_d = dram("abias", [3, 2, 128, 2 * 512])
    biasm_d = dram("abiasm", [3, 2, 16, 512])
    snk_d = dram("sinkx", [128, 4])
    out_d = dram("out", [2048, D], kind="ExternalOutput")
    hmid_d = dram("hmid", [17 * 128, D], kind="Internal")

    pc = sb("pc", [128, 64])
    pc2 = sb("pc2", [128, 32])
    nw = sb("nw_s", [128, D])
    ident = sb("ident", [128, 128], BF16)
    mhalf = sb("mhalf", [128, 1])
    bones = sb("bones", [128, 128])
    irep2 = sb("irep2", [128, 64])

    MU, KK_, KA, W0, A0, RK, LW, LB, QW, KW = 0, 14, 18, 22, 26, 30, 34, 38, 42, 43
    OMU, OMKA, ESK = 0, 14, 18

    NFB = 6
    pbanks = [ps("pb%d" % i, [128, 512]) for i in range(NFB)]
    pbh = [ps("pbf%d" % i, [128, 1024], BF16) for i in range(2)]
    rot = [0]

    NPB = 2

    npb_cur = [NPB]

    def bank():
        b = pbanks[rot[0] % npb_cur[0]]
        rot[0] += 1
        return b
    crot = [0]

    def cbank():
        b = pbanks[NPB + crot[0] % (NFB - NPB)]
        crot[0] += 1
        return b
    trot = [0]

    def tbank():
        b = pbh[trot[0] % 2]
        trot[0] += 1
        return b

    esP = contextlib.ExitStack()

    def sbp(name, shape, dt=F32):
        return TB(esP.enter_context(nc.sbuf_tensor(name, shape, dt)), name)
    w_in = sbp("w_in_bf", [128, 8, NPROJ * 128], BF16)
    yrw = sbp("yrw", [128, 17, 4, 128], BF16)
    ones64 = sbp("ones64", [128, 64], BF16)
    kmeta = sbp("kmeta", [128, 2, 16], BF16)
    vmeta = sbp("vmeta", [16, 128], BF16)
    esA = contextlib.ExitStack()

    def sba(name, shape, dt=F32):
        return TB(esA.enter_context(nc.sbuf_tensor(name, shape, dt)), name)

    lut = sba("lut_bf", [128, 512], BF16)
    gup = sba("gup_bf", [128, 512], BF16)
    irep8 = sba("irep8", [128, 512], BF16)
    msu = sba("msu", [128, 512], BF16)
    msl = sba("msl", [128, 512], BF16)
    mui = sba("mui", [128, 512], BF16)
    ones_t = sba("ones_t", [128, 128])
    rst = sba("rst", [128, 2 * W_])
    carry = sba("carry", [128, 14])
    Sst = [sba("S%d" % i, [128, 256], BF16) for i in range(2)]
    irep2x4 = sba("irep2x4", [128, 256])
    xb = [sba("xb%d" % i, [128, D]) for i in range(2)]
    utok = sba("utok", [128, D], BF16)
    st = [sba("st%d" % i, [128, 4]) for i in range(4)]
    uT = sba("uT", [128, 8, W_ + 1], BF16)
    tmp = [sba("tmp%d" % i, [128, W_]) for i in range(2)]
    def pp_set(alloc, sfx):
        wide = {nm: [alloc("pp%s%d%s" % (nm, g, sfx), [128, 2 * W_]) for g in range(2)] for nm in ("R", "K", "V")}
        lst = []
        for nm in ("R", "K", "V"):
            for hp in range(4):
                lst.append(Sub(wide[nm][hp // 2], (hp % 2) * W_, W_))
        lst.append(alloc("pp12" + sfx, [128, W_]))
        lst.append(alloc("pp13" + sfx, [128, W_]))
        return lst, wide
    pp, ppw = pp_set(sba, "")
    lo_bf = sba("lo_bf", [128, W_], BF16)
    snk = sba("snk", [128, 4])

    t1 = [sba("t1_%d" % i, [128, W_]) for i in range(3)]
    TW = {nm: sba(nm, [128, 2 * W_]) for nm in ("lwW", "alW", "kkW", "kfW", "clW", "g1W", "xA", "xB", "xC")}
    def wide_heads(alloc, nm, sfx):
        w_ = [alloc("%sW%d%s" % (nm, g, sfx), [128, 2 * W_], BF16) for g in range(2)]
        return [Sub(w_[hp // 2], (hp % 2) * W_, W_) for hp in range(4)], w_

    def sb_set(alloc, sfx):
        d_ = {}
        for nm in ("At", "Bt", "Kt", "vbf"):
            d_[nm], d_[nm + "W"] = wide_heads(alloc, nm, sfx)
        d_["gC"] = alloc("gC" + sfx, [128, 4, NCS])
        return d_
    SBS = [sb_set(sba, "")]
    CS = []
    for ci_ in range(NCS):
        s_ = {}
        for nm in ("A_tok", "Bh_tok", "Kh_tok", "V_tok", "TA", "W1", "Ul"):
            s_[nm] = sba("%s_%d" % (nm, ci_), [128, 512], BF16)
        for nm in ("N", "L", "TT"):
            s_[nm] = [[sba("%s%d_%d_%d" % (nm, ci_, i, j), [128, 512], BF16) for j in range(2)] for i in range(2)]
        for nm in ("AkT",):
            s_[nm] = [sba("%s%d_%d" % (nm, ci_, j), [128, 512], BF16) for j in range(2)]
        s_["McT"] = sba("McT_%d" % ci_, [128, 512], BF16)
        CS.append(s_)

    esA1 = contextlib.ExitStack()

    def sba1(name, shape, dt=F32):
        return TB(esA1.enter_context(nc.sbuf_tensor(name, shape, dt)), name)
    SBS.append(sb_set(sba1, "_b"))
    pp_b, ppw_b = pp_set(sba1, "_b")
    PPS = [(pp, ppw), (pp_b, ppw_b)]
    print('[mem] phase A1 free bytes/partition:', nc.sbuf_bytes_remaining)
    g_ = k.dma(lut[:, :], lut_d[:, :], "wc", eng="pool")
    k.dma(gup[:, :], gup_d[:, :], "wc", eng="pool", group=g_)
    import os as _os2
    nosplit = bool(_os2.environ.get("KNOSPLIT"))
    for kk_ in range(8):
        if nosplit:
            k.dma(w_in[:, kk_, :], w_in_d[kk_ * 128:(kk_ + 1) * 128, :], "wc", eng="pool", group=g_)
        else:
            k.dma(w_in[:, kk_, 512:1664], w_in_d[kk_ * 128:(kk_ + 1) * 128, 512:1664], "wc", eng="pool", group=g_)
    g_ = k.dma(pc[:, :], pc_d[:, :], "c0")
    k.dma(nw[:, :], n1_d[:, :], "c0", group=g_)
    k.dma(bones[:, :], cst_d[:, 128:256], "c0", group=g_)
    k.dma(irep2[:, :], cst_d[:, 384:448], "c0", group=g_)
    for i in range(4):
        k.dma(irep2x4[:, i * 64:(i + 1) * 64], cst_d[:, 384:448], "c0", group=g_)
    k.dma(snk[:, :], snk_d[:, :], "c0", group=g_)
    g_ = k.dma(ident[:, :], cst_d[:, 0:128], "c1", eng="pool")
    for i in range(4):
        k.dma(irep8[:, i * 128:(i + 1) * 128], cst_d[:, 0:128], "c1", eng="pool", group=g_)
    k.dma(msu[:, :], msk_d[:, 0:512], "c1", eng="pool", group=g_)
    k.dma(msl[:, :], msk_d[:, 512:1024], "c1", eng="pool", group=g_)
    k.dma(mui[:, :], msk_d[:, 1024:1536], "c1", eng="pool", group=g_)
    g_ = None
    for kk_ in range(8):
        if nosplit:
            break
        r_ = k.dma(w_in[:, kk_, 0:512], w_in_d[kk_ * 128:(kk_ + 1) * 128, 0:512], "wc_b", eng="pool", group=g_)
        g_ = g_ or r_
        k.dma(w_in[:, kk_, 1664:NPROJ * 128], w_in_d[kk_ * 128:(kk_ + 1) * 128, 1664:NPROJ * 128], "wc_b", eng="pool", group=g_)
    k.memset(ones_t[:, :], 1.0)
    k.memset(mhalf[:, :], -0.5)
    k.memset(rst[:, :], 1.0)
    for i in range(2 * W_ // 128):
        k.memset(rst[:, i * 128:i * 128 + 1], 0.0)
    k.memset(ones64[:, :], 1.0)
    k.raw("dve", lambda e: e.memset(uT.h[:, :, :], 0.0), [], [uT], 2056)
    k.memset(Sst[0][:, :], 0.0)
    for cs__ in CS:
        k.memset(cs__["McT"][:, :], 0.0)
    k.ts(pc2[:, OMU:OMU + 14], pc[:, MU:MU + 14], -1.0, 1.0, ALU.mult, ALU.add)
    k.ts(pc2[:, OMKA:OMKA + 4], pc[:, KA:KA + 4], -1.0, 1.0, ALU.mult, ALU.add)
    k.ts(pc2[:, 22:26], pc[:, W0:W0 + 4], 0.5, None, ALU.mult)
    k.ts(pc2[:, 26:30], pc[:, A0:A0 + 4], 0.5, None, ALU.mult)
    k.act(pc2[:, ESK:ESK + 4], snk[:, :], AF.Exp)
    def rstd_from_ss(out, ss_view, inv_n, eps):
        k.ts(out, ss_view, inv_n, eps, ALU.mult, ALU.add)
        k.act(out, out, AF.Sqrt)
        k.recip(out, out)

    def norm_block(x_t, ut, wt, si, stl):
        s_ = stl[si % 4]
        k.act(ut[:, :], x_t[:, :], AF.Square, accum=s_[:, 0:1])
        k.ts(s_[:, 2:3], s_[:, 0:1], 1.0 / D, 1e-6, ALU.mult, ALU.add)
        k.tt(s_[:, 1:2], s_[:, 2:3], mhalf[:, :], ALU.pow, eng="pool")
        k.stt(ut[:, :], x_t[:, :], s_[:, 1:2], wt[:, :], ALU.mult, ALU.mult)

    def transpose_block(ut, dst, col0, src_cols=None):
        for hlf in range(2):
            pt_ = tbank()
            for d4 in range(4):
                dk = hlf * 4 + d4
                k.tr(pt_[:, d4 * 128:(d4 + 1) * 128], ut[:, dk * 128:(dk + 1) * 128], ident[:, :])
            if src_cols is None:
                k.raw("act", lambda e, pt_=pt_, hlf=hlf: e.activation(
                    out=dst.h[:, hlf * 4:(hlf + 1) * 4, col0:col0 + 128],
                    in_=pt_.h[:, 0:512].rearrange("p (k t) -> p k t", k=4), func=AF.Copy), [pt_], [dst], 512)
            else:
                a, b_ = src_cols
                k.raw("act", lambda e, pt_=pt_, hlf=hlf, a=a, b_=b_: e.activation(
                    out=dst.h[:, hlf * 4:(hlf + 1) * 4, col0:col0 + (b_ - a)],
                    in_=pt_.h[:, 0:512].rearrange("p (k t) -> p k t", k=4)[:, :, a:b_], func=AF.Copy), [pt_], [dst], 16)

    def project(cc, n=W_, shift=False):
        b = bank()
        lo_ = 0 if shift else 1
        n_ = n + 1 if shift else n
        for dk in range(8):
            k.mm(b[:, 0:n_], w_in[:, dk, cc * 128:(cc + 1) * 128], uT[:, dk, lo_:lo_ + n_], start=(dk == 0), stop=(dk == 7))
        return b

    def kv_norm_k(psb, dst, n):
        k.act(t1[0][:, 0:n], psb[:, 0:n], AF.Square)
        b2 = bank()
        k.mm(b2[:, 0:n], bones[:, :], t1[0][:, 0:n])
        rstd_from_ss(t1[1][:, 0:n], b2[:, 0:n], 1.0 / 64, 1e-6)
        k.stt(dst, psb[:, 0:n], pc[:, KW:KW + 1], t1[1][:, 0:n], ALU.mult, ALU.mult)

    k.dma(xb[0][:, :], xmeta[:, :], "xb0")
    norm_block(xb[0], utok, nw, 0, st)
    transpose_block(utok, uT, 1)
    for g in range(2):
        b = project(18 + g, 128)
        kv_norm_k(b, t1[2][:, 0:128], 128)
        k.cp(kmeta[:, g, :], t1[2][:, 0:16])
    b = project(20, 128)
    k.cp(lo_bf[:, 0:128], b[:, 0:128], eng="act")
    pt_ = tbank()
    k.tr(pt_[:, 0:128], lo_bf[:, 0:128], ident[:, :])
    k.cp(vmeta[:, :], pt_[0:16, 0:128])

    xcnt = [1]
    nsb_run = NSB
    if trunc.startswith('nsb'):
        nsb_run = int(trunc[3:])
    if trunc == 'meta':
        nsb_run = 0
    for sbi in range(nsb_run):
        ch0 = sbi * NCS
        full = ch0 >= 48
        par = sbi % 2
        if ch0 == 48:
            S.flush()
            print('[sched] modelled phase A1 (ns):', S.est)
            esA1.close()

            def sba2(name, shape, dt=F32):
                return TB(esA.enter_context(nc.sbuf_tensor(name, shape, dt)), name)
            sg_bf = sba2("sg_bf", [128, W_], BF16)
            gq = [sba2("gq%d" % i, [128, W_], BF16) for i in range(4)]
            Rt, RtW = wide_heads(sba2, "Rt", "")
            bon, bonW = wide_heads(sba2, "bon", "")
            t2 = [sba2("t2_%d" % i, [128, 512]) for i in range(2)]
            GTm = [sba2("GT%d" % i, [128, 512], BF16) for i in range(2)]
            for i in range(2):
                k.memset(GTm[i][:, :], 0.0)
            yt = sba2("yt", [128, 512])
            for cs__ in CS:
                for nm in ("RbT", "RkT"):
                    cs__[nm] = [sba2("%s%s_%d" % (nm, cs__["McT"].name[-2:], j), [128, 512], BF16) for j in range(2)]
            print('[mem] phase A2 free bytes/partition:', nc.sbuf_bytes_remaining)
            SBS = SBS[:1]
            PPS = PPS[:1]
            print('[mem] phase A2 free after temps:', nc.sbuf_bytes_remaining)
        sbs_ = SBS[sbi % len(SBS)]
        At, Bt, Kt, vbf, gC = (sbs_[n_] for n_ in ("At", "Bt", "Kt", "vbf", "gC"))
        AtW, BtW, KtW, vbfW = (sbs_[n_ + "W"] for n_ in ("At", "Bt", "Kt", "vbf"))
        pp, ppw = PPS[sbi % len(PPS)]
        k.raw("dve", lambda e: e.tensor_copy(out=uT.h[:, :, 0:1], in_=uT.h[:, :, W_:W_ + 1]), [uT], [uT], 8)
        for c4 in range(NCS):
            ch = ch0 + c4
            xt_ = xb[xcnt[0] % 2]
            k.dma(xt_[:, :], xl[ch * 128:(ch + 1) * 128, :], "xb%d" % (xcnt[0] % 2))
            norm_block(xt_, utok, nw, xcnt[0], st)
            transpose_block(utok, uT, 1 + c4 * 128)
            xcnt[0] += 1
        if kstep < 1:
            continue
        chunks = list(range(0, 14)) if full else list(range(4, 13))
        for cc in chunks:
            b = project(cc, shift=True)
            tm = tmp[cc % len(tmp)]
            k.act(tm[:, 0:W_], b[:, 0:W_], AF.Copy, scale=pc[:, MU + cc:MU + cc + 1])
            k.stt(pp[cc][:, :], b[:, 1:W_ + 1], pc2[:, OMU + cc:OMU + cc + 1], tm[:, 0:W_], ALU.mult, ALU.add)
        if kstep < 2:
            continue
        k.act(lo_bf[0:64, :], pp[12][0:64, :], AF.Tanh)
        k.cp(lo_bf[64:128, :], pp[12][64:128, :])
        if full:
            k.act(t1[0][:, :], pp[13][:, :], AF.Tanh, scale=0.5)
            k.ts(sg_bf[:, :], t1[0][:, :], 0.5, 0.5, ALU.mult, ALU.add)
        lwW, alW, kkW, kfW, clW, g1W, xA, xB, xC = (TW[n_] for n_ in ("lwW", "alW", "kkW", "kfW", "clW", "g1W", "xA", "xB", "xC"))
        for g2 in range(2):
            kW, vW, rW = ppw["K"][g2], ppw["V"][g2], ppw["R"][g2]
            for i2 in range(2):
                hp = 2 * g2 + i2
                cs = slice(i2 * W_, (i2 + 1) * W_)
                b = bank()
                k.mm(b[:, 0:W_], lut[0:64, hp * 128:(hp + 1) * 128], lo_bf[0:64, :])
                k.act(lwW[:, cs], b[:, 0:W_], AF.Tanh, bias=pc2[:, 22 + hp:23 + hp], scale=0.5)
                b = bank()
                k.mm(b[:, 0:W_], lut[64:128, hp * 128:(hp + 1) * 128], lo_bf[64:128, :])
                k.act(alW[:, cs], b[:, 0:W_], AF.Tanh, bias=pc2[:, 26 + hp:27 + hp], scale=0.5)
                if full:
                    b = bank()
                    k.mm(b[:, 0:W_], gup[:, hp * 128:(hp + 1) * 128], sg_bf[:, :])
                    k.cp(gq[hp][:, :], b[:, 0:W_], eng="act")
                k.ts(xA[:, cs], kW[:, cs], pc[:, KK_ + hp:KK_ + hp + 1], None, ALU.mult)
            k.ts(lwW[:, :], lwW[:, :], 0.5, 0.5, ALU.mult, ALU.add)
            k.ts(alW[:, :], alW[:, :], 0.5, 0.5, ALU.mult, ALU.add)
            k.act(xB[:, :], xA[:, :], AF.Square)
            b = bank()
            k.mm(b[:, :], bones[:, :], xB[:, :])
            k.act(xB[:, :], b[:, :], AF.Sqrt)
            k.ts(xB[:, :], xB[:, :], 1e-12, None, ALU.max)
            k.recip(xB[:, :], xB[:, :])
            k.tt(kkW[:, :], xA[:, :], xB[:, :], ALU.mult)
            k.tt(xC[:, :], kkW[:, :], alW[:, :], ALU.mult)
            for i2 in range(2):
                hp = 2 * g2 + i2
                cs = slice(i2 * W_, (i2 + 1) * W_)
                k.ts(xA[:, cs], alW[:, cs], pc[:, KA + hp:KA + hp + 1], pc2[:, OMKA + hp:OMKA + hp + 1], ALU.mult, ALU.add)
            k.tt(kfW[:, :], kW[:, :], xA[:, :], ALU.mult)
            k.scan(clW[:, :], rst[:, :], lwW[:, :])
            k.tt(xA[:, :], clW[:, :], lwW[:, :], ALU.subtract)
            k.act(g1W[:, :], clW[:, :], AF.Exp, scale=-CW)
            k.act(xB[:, :], xA[:, :], AF.Exp, scale=-CW)
            k.act(alW[:, :], clW[:, :], AF.Exp, scale=CW)
            k.stt(AtW[g2][:, :], kkW[:, :], -1.0, xB[:, :], ALU.mult, ALU.mult)
            k.tt(BtW[g2][:, :], xC[:, :], alW[:, :], ALU.mult)
            k.tt(KtW[g2][:, :], kfW[:, :], alW[:, :], ALU.mult)
            k.raw("dve", lambda e, gC=gC, g1W=g1W, g2=g2: e.tensor_copy(
                out=gC.h[:, 2 * g2:2 * g2 + 2, :],
                in_=g1W.h[:, :].rearrange("p (a c) -> p a c", c=128)[:, :, 127:128].rearrange("p (i c4) o -> p i (c4 o)", i=2)),
                [g1W], [gC], 4)
            k.cp(vbfW[g2][:, :], vW[:, :], eng="act")
            if full:
                k.tt(RtW[g2][:, :], rW[:, :], g1W[:, :], ALU.mult)
                for i2 in range(2):
                    hp = 2 * g2 + i2
                    cs = slice(i2 * W_, (i2 + 1) * W_)
                    k.stt(xA[:, cs], rW[:, cs], pc[:, RK + hp:RK + hp + 1], kfW[:, cs], ALU.mult, ALU.mult)
                b = bank()
                k.mm(b[:, :], bones[:, :], xA[:, :])
                k.tt(bonW[g2][:, :], b[:, :], vW[:, :], ALU.mult)
        if kstep < 3:
            continue
        CH = [ch0 + c4 for c4 in range(NCS)]
        SL = [slice(c4 * 128, (c4 + 1) * 128) for c4 in range(NCS)]
        OUT = [ch >= HALO_CH for ch in CH]
        for c4 in range(NCS):
            cs_ = CS[c4]
            for src, dst in ((At, cs_["A_tok"]), (Bt, cs_["Bh_tok"]), (Kt, cs_["Kh_tok"]), (vbf, cs_["V_tok"])):
                pt_ = tbank()
                for hp in range(4):
                    k.tr(pt_[:, hp * 128:(hp + 1) * 128], src[hp][:, SL[c4]], ident[:, :])
                k.cp(dst[:, :], pt_[:, 0:512], eng="act")
        if kstep < 4:
            continue
        cur = [0] * NCS
        for c4 in range(NCS):
            cs_ = CS[c4]
            sl = SL[c4]
            pn = [cbank(), cbank()]
            pl = [cbank(), cbank()]
            for h in range(8):
                hp, hf = h // 2, h % 2
                rows = slice(64 * hf, 64 * hf + 64)
                cs = slice(hp * 128, hp * 128 + 128)
                k.mm(pn[hf][:, cs], Bt[hp][rows, sl], At[hp][rows, sl])
                k.mm(pl[hf][:, cs], At[hp][rows, sl], Bt[hp][rows, sl])
            for j in range(2):
                k.tt(cs_["N"][0][j][:, :], pn[j][:, :], msu[:, :], ALU.mult)
                k.tt(cs_["L"][0][j][:, :], pl[j][:, :], msl[:, :], ALU.mult)
                k.tt(cs_["TT"][0][j][:, :], cs_["N"][0][j][:, :], irep8[:, :], ALU.add)
            pa = [cbank(), cbank()]
            for h in range(8):
                hp, hf = h // 2, h % 2
                rows = slice(64 * hf, 64 * hf + 64)
                cs = slice(hp * 128, hp * 128 + 128)
                k.mm(pa[hf][:, cs], Kt[hp][rows, sl], At[hp][rows, sl])
            for j in range(2):
                k.tt(cs_["AkT"][j][:, :], pa[j][:, :], msu[:, :], ALU.mult)
            if OUT[c4]:
                for (lt, dst) in ((Bt, cs_["RbT"]), (Kt, cs_["RkT"])):
                    pr_ = [cbank(), cbank()]
                    for h in range(8):
                        hp, hf = h // 2, h % 2
                        rows = slice(64 * hf, 64 * hf + 64)
                        cs = slice(hp * 128, hp * 128 + 128)
                        k.mm(pr_[hf][:, cs], lt[hp][rows, sl], Rt[hp][rows, sl])
                    for j in range(2):
                        k.tt(dst[j][:, :], pr_[j][:, :], mui[:, :], ALU.mult)
        if kstep < 5:
            continue
        for lev in range(6):
            for c4 in range(NCS):
                cs_ = CS[c4]
                c_ = cur[c4]
                n_ = 1 - c_
                Nb_, Lb_, TT_ = cs_["N"], cs_["L"], cs_["TT"]
                for j in range(2):
                    b = cbank()
                    for i in range(4):
                        cs = slice(i * 128, (i + 1) * 128)
                        k.mm(b[:, cs], Nb_[c_][j][:, cs], Lb_[c_][j][:, cs])
                    k.cp(Lb_[n_][j][:, :], b[:, :], eng="act")
                if lev < 5:
                    for j in range(2):
                        b = cbank()
                        for i in range(4):
                            cs = slice(i * 128, (i + 1) * 128)
                            k.mm(b[:, cs], Lb_[c_][j][:, cs], Nb_[c_][j][:, cs])
                        k.cp(Nb_[n_][j][:, :], b[:, :], eng="act")
                for j in range(2):
                    b = cbank()
                    for i in range(4):
                        cs = slice(i * 128, (i + 1) * 128)
                        k.mm(b[:, cs], Lb_[n_][j][:, cs], TT_[c_][j][:, cs])
                    k.tt(TT_[n_][j][:, :], b[:, :], TT_[c_][j][:, :], ALU.add)
                cur[c4] = n_
        if kstep < 6:
            continue

        def hcs(h):
            return slice((h // 2) * 128, (h // 2) * 128 + 128)

        def h64(h):
            return slice(h * 64, (h + 1) * 64)
        for c4 in range(NCS):
            cs_ = CS[c4]
            TT = cs_["TT"][cur[c4]]
            A_tok, Bh_tok, V_tok = cs_["A_tok"], cs_["Bh_tok"], cs_["V_tok"]
            AkT, TA, W1, Ul, McT = cs_["AkT"], cs_["TA"], cs_["W1"], cs_["Ul"], cs_["McT"]
            b = cbank()
            for h in range(8):
                k.mm(b[:, h64(h)], TT[h % 2][:, hcs(h)], A_tok[:, h64(h)])
            k.cp(TA[:, :], b[:, :], eng="act")
            b = cbank()
            for h in range(8):
                k.mm(b[:, h64(h)], AkT[h % 2][:, hcs(h)], V_tok[:, h64(h)])
            k.cp(W1[:, :], b[:, :], eng="act")
            b = cbank()
            for h in range(8):
                k.mm(b[:, h64(h)], TT[h % 2][:, hcs(h)], W1[:, h64(h)])
            k.cp(Ul[:, :], b[:, :], eng="act")
            b = cbank()
            for h in range(8):
                hp, hf = h // 2, h % 2
                k.mm(b[64 * hf:64 * hf + 64, hp * 128 + 64 * hf:hp * 128 + 64 * hf + 64], TA[:, h64(h)], Bh_tok[:, h64(h)])
            for hf in range(2):
                rs_ = slice(64 * hf, 64 * hf + 64)
                k.raw("dve", lambda e, McT=McT, b=b, rs_=rs_, hf=hf: e.tensor_tensor(
                    out=McT.h[rs_, :].rearrange("p (a c) -> p a c", c=128)[:, :, 64 * hf:64 * hf + 64],
                    in0=b.h[rs_, :].rearrange("p (a c) -> p a c", c=128)[:, :, 64 * hf:64 * hf + 64],
                    in1=irep2x4.h[rs_, :].rearrange("p (a c) -> p a c", c=64), op=ALU.add), [b, irep2x4], [McT], 256)
        for c4 in range(NCS):
            cs_ = CS[c4]
            ch = CH[c4]
            sl = SL[c4]
            do_out = OUT[c4]
            A_tok, Bh_tok, Kh_tok, V_tok = cs_["A_tok"], cs_["Bh_tok"], cs_["Kh_tok"], cs_["V_tok"]
            RbT, RkT, TA, Ul, McT = cs_.get("RbT"), cs_.get("RkT"), cs_["TA"], cs_["Ul"], cs_["McT"]
            S0 = Sst[ch % 2]
            S1 = Sst[(ch + 1) % 2]
            if do_out:
                b = cbank()
                for h in range(8):
                    hp, hf = h // 2, h % 2
                    k.mm(b[64 * hf:64 * hf + 64, hp * 128:(hp + 1) * 128], TA[:, h64(h)], RbT[h % 2][:, hcs(h)])
                for hp in range(4):
                    for hf in range(2):
                        rs_ = slice(64 * hf, 64 * hf + 64)
                        k.tt(GTm[hf][rs_, hp * 128:(hp + 1) * 128], b[rs_, hp * 128:(hp + 1) * 128], Rt[hp][rs_, sl], ALU.add)
                by = cbank()
                for h in range(8):
                    hp, hf = h // 2, h % 2
                    o = by[64 * hf:64 * hf + 64, hp * 128:(hp + 1) * 128]
                    k.mm(o, S0[:, hp * 64:(hp + 1) * 64], GTm[hf][:, hp * 128:(hp + 1) * 128], start=True, stop=False)
                    k.mm(o, Ul[:, h64(h)], RbT[hf][:, hcs(h)], start=False, stop=False)
                    k.mm(o, V_tok[:, h64(h)], RkT[hf][:, hcs(h)], start=False, stop=True)
                k.cp(yt[:, :], by[:, :], eng="act")
            bs = cbank()
            for h in range(8):
                hp, hf = h // 2, h % 2
                o = bs[64 * hf:64 * hf + 64, hp * 64:(hp + 1) * 64]
                k.mm(o, McT[:, hp * 128 + 64 * hf:hp * 128 + 64 * hf + 64], S0[:, hp * 64:(hp + 1) * 64], start=True, stop=False)
                k.mm(o, Bh_tok[:, h64(h)], Ul[:, h64(h)], start=False, stop=False)
                k.mm(o, Kh_tok[:, h64(h)], V_tok[:, h64(h)], start=False, stop=True)
            for hp in range(4):
                cs = slice(hp * 64, (hp + 1) * 64)
                k.act(S1[:, cs], bs[:, cs], AF.Copy, scale=gC[:, hp, c4:c4 + 1])
            if not do_out:
                continue
            yc, ysq = t2[0], t2[1]
            b = bank()
            k.mm(b[:, :], bones[:, :], yt[:, :])
            k.stt(yc[:, :], b[:, :], -1.0 / 64, yt[:, :], ALU.mult, ALU.add)
            k.act(ysq[:, :], yc[:, :], AF.Square)
            b = bank()
            k.mm(b[:, :], bones[:, :], ysq[:, :])
            rstd_from_ss(ysq[:, :], b[:, :], 1.0 / 64, 64 * 1e-5)
            k.tt(yc[:, :], yc[:, :], ysq[:, :], ALU.mult)
            for hp in range(4):
                cs = slice(hp * 128, (hp + 1) * 128)
                k.ts(yc[:, cs], yc[:, cs], pc[:, LW + hp:LW + hp + 1], pc[:, LB + hp:LB + hp + 1], ALU.mult, ALU.add)
                k.tt(yc[:, cs], yc[:, cs], bon[hp][:, sl], ALU.add)
                k.tt(yrw[:, ch - HALO_CH, hp, :], yc[:, cs], gq[hp][:, sl], ALU.mult)

    S.flush()
    print('[sched] modelled phase A (ns):', S.est)
    esA.close()

    esC = contextlib.ExitStack()

    def sbc(name, shape, dt=F32):
        return TB(esC.enter_context(nc.sbuf_tensor(name, shape, dt)), name)
    npb_cur[0] = 2
    w_out = sbc("w_out_bf", [128, 8, D], BF16)
    xbc = [sbc("xbc%d" % i, [128, D]) for i in range(4)]
    xrc = [sbc("xrc%d" % i, [128, D]) for i in range(3)]
    utc = [sbc("utc%d" % i, [128, D], BF16) for i in range(2)]
    stc = [sbc("stc%d" % i, [128, 4]) for i in range(4)]
    uTc = [sbc("uTc%d" % i, [128, 8, W_], BF16) for i in range(2)]
    tc_ = [[sbc("tc%d_%d" % (i, j), [128, W_]) for j in range(2)] for i in range(3)]
    qn = [[[sbc("qn%d_%d_%d" % (r_, i, hf), [128, W_], BF16) for hf in range(2)] for i in range(4)] for r_ in range(2)]
    kd = [[sbc("kd%d_%d" % (g, i), [128, W_], BF16) for i in range(2)] for g in range(2)]
    vat = [sbc("vat%d" % i, [128, W_], BF16) for i in range(2)]
    vtok = [sbc("vtok%d" % i, [128, 128], BF16) for i in range(4)]
    ycat = [sbc("ycat%d" % i, [128, 4, 128], BF16) for i in range(2)]
    pT = [sbc("pT%d" % i, [128, 512], BF16) for i in range(6)]
    scf = [sbc("scf%d" % i, [128, 512]) for i in range(3)]
    rdn = [sbc("rdn%d" % i, [128, 512]) for i in range(2)]
    abias = [[sbc("abias_s%d_%d" % (v_, g), [128, 1024]) for g in range(2)] for v_ in range(2)]
    abiasm = [[sbc("abiasm_s%d_%d" % (v_, g), [16, 512]) for g in range(2)] for v_ in range(2)]
    print('[mem] attention pass free bytes/partition:', nc.sbuf_bytes_remaining)
    g_ = None
    for kk_ in range(8):
        r_ = k.dma(w_out[:, kk_, :], w_out_d[kk_ * 128:(kk_ + 1) * 128, :], "wo", eng="pool", group=g_)
        g_ = g_ or r_
    for r_ in range(2):
        for hp_ in range(4):
            for hf_ in range(2):
                k.memset(qn[r_][hp_][hf_][:, :], 0.0)
    for g in range(2):
        k.dma(abias[1][g][:, :], bias_d[2, g, :, :], "abg%d" % g)
        k.dma(abiasm[1][g][:, :], biasm_d[2, g, :, :], "abmg%d" % g)
    xc = [0]
    tcn = [0]

    def projc(uTt, cc):
        b = bank()
        for dk in range(8):
            k.mm(b[:, 0:W_], w_in[:, dk, cc * 128:(cc + 1) * 128], uTt[:, dk, 0:W_], start=(dk == 0), stop=(dk == 7))
        return b

    def ttmp():
        t_ = tc_[tcn[0] % 3]
        tcn[0] += 1
        return t_
    for sbi in range(KV_CH // NCS, NSB):
        ch0 = sbi * NCS
        par = sbi % 2
        uTt = uTc[par]
        xts = []
        for c4 in range(NCS):
            ch = ch0 + c4
            xt_ = xbc[xc[0] % 4]
            k.dma(xt_[:, :], xl[ch * 128:(ch + 1) * 128, :], "xbc%d" % (xc[0] % 4))
            ut = utc[xc[0] % 2]
            norm_block(xt_, ut, nw, xc[0], stc)
            transpose_block(ut, uTt, c4 * 128)
            xts.append(xt_)
            xc[0] += 1
        qs = qn[par]
        for hp in range(4):
            b = projc(uTt, 14 + hp)
            ta, tb = ttmp()
            k.act(ta[:, :], b[:, 0:W_], AF.Square)
            b2 = bank()
            k.mm(b2[:, 0:W_], bones[:, :], ta[:, :])
            rstd_from_ss(tb[:, :], b2[:, 0:W_], 1.0 / 64, 1e-6)
            for hf in range(2):
                rs = slice(64 * hf, 64 * hf + 64)
                k.stt(qs[hp][hf][rs, :], b[rs, 0:W_], pc[rs, QW:QW + 1], tb[rs, :], ALU.mult, ALU.mult)
        for g in range(2):
            b = projc(uTt, 18 + g)
            ta, tb = ttmp()
            k.act(ta[:, :], b[:, 0:W_], AF.Square)
            b2 = bank()
            k.mm(b2[:, 0:W_], bones[:, :], ta[:, :])
            rstd_from_ss(tb[:, :], b2[:, 0:W_], 1.0 / 64, 1e-6)
            k.stt(kd[g][par][:, :], b[:, 0:W_], pc[:, KW:KW + 1], tb[:, :], ALU.mult, ALU.mult)
        b = projc(uTt, 20)
        k.cp(vat[par][:, :], b[:, 0:W_], eng="act")
        for c4 in range(NCS):
            ch = ch0 + c4
            sl = slice(c4 * 128, (c4 + 1) * 128)
            pt_ = tbank()
            k.tr(pt_[:, 0:128], vat[par][:, sl], ident[:, :])
            k.cp(vtok[ch % 4][:, :], pt_[:, 0:128])
            if ch < HALO_CH:
                continue
            var = 0 if ch == HALO_CH else (1 if ch == OWN0 else 2)
            prev_par = par if c4 > 0 else 1 - par
            psl = slice(((c4 - 1) % NCS) * 128, ((c4 - 1) % NCS) * 128 + 128)
            bo = bd = None
            yc_ = ycat[ch % 2]
            rden = rdn[ch % 2]
            for g in range(2):
                if var == 2:
                    ab, abm = abias[1][g], abiasm[1][g]
                else:
                    ab, abm = abias[0][g], abiasm[0][g]
                    k.dma(ab[:, :], bias_d[var, g, :, :], "abs%d" % g)
                    k.dma(abm[:, :], biasm_d[var, g, :, :], "abms%d" % g)
                pts = []
                for grp in range(3):
                    bsc = cbank()
                    nk = 16 if grp == 2 else 128
                    for hh in range(4):
                        h = 4 * g + hh
                        hp, hf = h // 2, h % 2
                        if grp == 0:
                            kl = kd[g][prev_par][:, psl]
                        elif grp == 1:
                            kl = kd[g][par][:, sl]
                        else:
                            kl = kmeta[:, g, :]
                        k.mm(bsc[0:nk, hh * 128:(hh + 1) * 128], kl, qs[hp][hf][:, sl])
                    bsrc = abm[:, :] if grp == 2 else ab[:, grp * 512:(grp + 1) * 512]
                    sc_f = scf[grp]
                    k.stt(sc_f[0:nk, :], bsc[0:nk, :], 0.125, bsrc, ALU.mult, ALU.add)
                    pt = pT[g * 3 + grp]
                    k.act(pt[0:nk, :], sc_f[0:nk, :], AF.Exp)
                    pts.append((pt, nk))
                if bo is None:
                    bo = bank()
                    bd = bank()
                vprev = vtok[(ch - 1) % 4]
                vcur = vtok[ch % 4]
                for hh in range(4):
                    h = 4 * g + hh
                    hp, hf = h // 2, h % 2
                    o = bo[64 * hf:64 * hf + 64, hp * 128:(hp + 1) * 128]
                    od = bd[64 * hf:64 * hf + 64, hp * 128:(hp + 1) * 128]
                    cs = slice(hh * 128, (hh + 1) * 128)
                    vs = [vprev[:, g * 64:(g + 1) * 64], vcur[:, g * 64:(g + 1) * 64], vmeta[:, g * 64:(g + 1) * 64]]
                    for grp in range(3):
                        pt, nk = pts[grp]
                        k.mm(o, vs[grp], pt[0:nk, cs], start=(grp == 0), stop=(grp == 2))
                    for grp in range(3):
                        pt, nk = pts[grp]
                        k.mm(od, ones64[0:nk, :], pt[0:nk, cs], start=(grp == 0), stop=(grp == 2))
            for hp in range(4):
                cs = slice(hp * 128, (hp + 1) * 128)
                k.ts(rden[:, cs], bd[:, cs], pc2[:, ESK + hp:ESK + hp + 1], None, ALU.add)
            k.recip(rden[:, :], rden[:, :])
            for hp in range(4):
                k.tt(yc_[:, hp, :], bo[:, hp * 128:(hp + 1) * 128], rden[:, hp * 128:(hp + 1) * 128], ALU.mult)
            ci = ch - HALO_CH
            xr = xrc[ci % 3]
            for half in range(2):
                b = cbank()
                for kc in range(8):
                    lhs = yrw[:, ci, kc, :] if kc < 4 else yc_[:, kc - 4, :]
                    k.mm(b[:, :], lhs, w_out[:, kc, half * 512:(half + 1) * 512], start=(kc == 0), stop=(kc == 7))
                k.tt(xr[:, half * 512:(half + 1) * 512], b[:, :], xts[c4][:, half * 512:(half + 1) * 512], ALU.add)
            k.dma(hmid_d[ci * 128:(ci + 1) * 128, :], xr[:, :], "xrc%d" % (ci % 3))
    S.flush()
    print('[sched] modelled attention pass (ns):', S.est)
    esC.close()
    esP.close()
    if trunc:
        with nc.Block() as block:
            S.emit(block)
        es.close()
        return nc

    def sbb(name, shape, dt=F32):
        return TB(es.enter_context(nc.sbuf_tensor(name, shape, dt)), name)

    npb_cur[0] = NFB
    wu = [sbb("wu%d" % i, [128, 8, 256], BF16) for i in range(NFC)]
    halo_sb = sbb("halo_sb", [128, 44, 2])
    w_dn = sbb("w_dn_bf", [128, NFC, D], BF16)
    cw = sbb("cw", [128, 44 * 3])
    cb = sbb("cb", [128, 44])
    u2T = sbb("u2T", [128, 8, 514], BF16)
    u2c = sbb("u2c", [128, 8, 2], BF16)
    actT = sbb("actT", [128, NFC, 512], BF16)
    hb = [sbb("hb%d" % i, [128, D]) for i in range(2)]
    u2tok = sbb("u2tok", [128, D], BF16)
    st2 = [sbb("st2_%d" % i, [128, 4]) for i in range(4)]
    c0 = [sbb("c0_%d" % i, [128, 512]) for i in range(4)]
    wukey = [TB(None, "wukey%d" % i) for i in range(3)]
    for fc in range(NFC):
        g_ = None
        for kk_ in range(8):
            r_ = k.dma(wu[fc][:, kk_, :], w_up_d[fc, kk_ * 128:(kk_ + 1) * 128, :], "wu%d" % (fc % 3), eng="pool", group=g_)
            g_ = g_ or r_
        g_.writes.append(wukey[fc % 3])
    g_ = None
    for fc in range(NFC):
        r_ = k.dma(w_dn[:, fc, :], w_down_d[fc * 128:(fc + 1) * 128, :], "wd", eng="pool", group=g_)
        g_ = g_ or r_
    g_ = k.dma(nw[:, :], n2_d[:, :], "c2")
    k.dma(cw[:, :], cw_d[:, :], "c2", group=g_)
    k.dma(cb[:, :], cb_d[:, :], "c2", group=g_)

    print('[mem] phase B free bytes/partition:', nc.sbuf_bytes_remaining)
    nidx = [0]

    def norm2_block(ci, col0, src_cols=None):
        h_ = hb[nidx[0] % 2]
        k.dma(h_[:, :], hmid_d[ci * 128:(ci + 1) * 128, :], "hb%d" % (nidx[0] % 2))
        norm_block(h_, u2tok, nw, nidx[0], st2)
        transpose_block(u2tok, u2T, col0, src_cols)
        nidx[0] += 1

    norm2_block(0, 0, (126, 128))
    for qi in range(4):
        if qi > 0:
            k.raw("dve", lambda e: e.tensor_copy(out=u2T.h[:, :, 0:2], in_=u2c.h[:, :, :]), [u2c], [u2T], 16)
        for c4 in range(4):
            norm2_block(1 + qi * 4 + c4, 2 + c4 * 128)
        k.raw("dve", lambda e: e.tensor_copy(out=u2c.h[:, :, :], in_=u2T.h[:, :, 512:514]), [u2T], [u2c], 16)
        hq = True
        for fc in range(NFC):
            pg = [bank() if hq else None, bank()]
            pv = [bank() if hq else None, bank()]
            for (pb_, col, wo_) in ((pg, fc, 0), (pv, NFC + fc, 128)):
                if hq:
                    for dk in range(8):
                        k.mm(pb_[0][:, 510:512], wu[fc][:, dk, wo_:wo_ + 128], u2T[:, dk, 0:2], start=(dk == 0), stop=(dk == 7))
                for dk in range(8):
                    k.mm(pb_[1][:, :], wu[fc][:, dk, wo_:wo_ + 128], u2T[:, dk, 2:514], start=(dk == 0), stop=(dk == 7))
            res = []
            for idx, (pb_, col) in enumerate(((pg, fc), (pv, NFC + fc))):
                cA = c0[idx * 2]
                cB = c0[idx * 2 + 1]
                w0_, w1_, w2_ = (cw[:, col * 3 + j:col * 3 + j + 1] for j in range(3))
                if hq:
                    h1, h2 = pb_[0][:, 511:512], pb_[0][:, 510:512]
                else:
                    h1, h2 = halo_sb[:, col, 1:2], halo_sb[:, col, 0:2]
                k.act(cA[:, :], pb_[1][:, :], AF.Identity, bias=cb[:, col:col + 1], scale=w0_)
                k.stt(cB[:, 1:512], pb_[1][:, 0:511], w1_, cA[:, 1:512], ALU.mult, ALU.add)
                k.stt(cB[:, 0:1], h1, w1_, cA[:, 0:1], ALU.mult, ALU.add)
                k.stt(cA[:, 2:512], pb_[1][:, 0:510], w2_, cB[:, 2:512], ALU.mult, ALU.add)
                k.stt(cA[:, 0:2], h2, w2_, cB[:, 0:2], ALU.mult, ALU.add)
                if qi < 3 and not hq:
                    k.cp(halo_sb[:, col, :], pb_[1][:, 510:512])
                res.append(cA)
            k.act(c0[1][:, :], res[0][:, :], AF.Silu)
            k.tt(actT[:, fc, :], c0[1][:, :], res[1][:, :], ALU.mult)
        for c4 in range(4):
            ci = 1 + qi * 4 + c4
            h_ = hb[nidx[0] % 2]
            hname = "hb%d" % (nidx[0] % 2)
            k.dma(h_[:, :], hmid_d[ci * 128:(ci + 1) * 128, :], hname)
            nidx[0] += 1
            for half in range(2):
                b = bank()
                for fc in range(NFC):
                    k.mm(b[:, :], actT[:, fc, c4 * 128:(c4 + 1) * 128], w_dn[:, fc, half * 512:(half + 1) * 512],
                         start=(fc == 0), stop=(fc == NFC - 1))
                k.tt(h_[:, half * 512:(half + 1) * 512], b[:, :], h_[:, half * 512:(half + 1) * 512], ALU.add)
            r0 = (qi * 4 + c4) * 128
            k.dma(out_d[r0:r0 + 128, :], h_[:, :], hname)

    S.flush()
    print("[sched] modelled time (ns): phase-sum", getattr(S, "est", None))
    with nc.Block() as block:
        S.emit(block)
    es.close()
    return nc


def _t5_bucket(d):
    d = np.maximum(d, 0)
    df = np.maximum(d, 16).astype(np.float32)
    large = 16 + (np.log(df / np.float32(16)) / np.float32(np.log(128 / 16)) * np.float32(16)).astype(np.int32)
    large = np.minimum(large, 31)
    return np.where(d < 16, d, large)


_PROG = {}


def kernel(x, meta_tokens, rel_bias, norm1_w, w_in, shift_mu, decay_w0, decay_up, aaa_a0, aaa_up, gate_up,
           k_k, k_a, r_k, lnx_w, lnx_b, q_norm_w, k_norm_w, sinks, w_out, norm2_w, w_up, conv_w, conv_b, w_down):
    f = np.float32
    x = np.asarray(x, f)
    meta = np.asarray(meta_tokens, f)
    rel_bias = np.asarray(rel_bias, f)
    W = np.asarray(w_in, f)[0]
    cols = []
    for base in (0, 512, 1024):
        cols += list(range(base, base + 512))
    cols += list(range(1536, 1664))
    cols += list(range(1664, 1792))
    cols += list(range(1792, 2304))
    kbase = 2304
    cols += list(range(kbase, kbase + 64)) * 2
    cols += list(range(kbase + 64, kbase + 128)) * 2
    cols += list(range(2432, 2560))
    w_in_p = np.ascontiguousarray(W[:, cols])
    mu = np.asarray(shift_mu, f)[0]
    mu_p = mu[cols[:14 * 128]]

    def colmaj(v):
        return np.ascontiguousarray(np.asarray(v, f).reshape(-1, 128).T)
    pcols = np.zeros((128, 64), f)
    pcols[:, 0:14] = colmaj(mu_p)
    pcols[:, 14:18] = colmaj(k_k[0])
    pcols[:, 18:22] = colmaj(k_a[0])
    pcols[:, 22:26] = colmaj(decay_w0[0])
    pcols[:, 26:30] = colmaj(aaa_a0[0])
    pcols[:, 30:34] = colmaj(np.asarray(r_k, f)[0].reshape(-1))
    pcols[:, 34:38] = colmaj(lnx_w[0])
    pcols[:, 38:42] = colmaj(lnx_b[0])
    pcols[:, 42] = np.tile(np.asarray(q_norm_w, f)[0], 2)
    pcols[:, 43] = np.tile(np.asarray(k_norm_w, f)[0], 2)
    lut = np.concatenate([np.asarray(decay_up, f)[0], np.asarray(aaa_up, f)[0]], 0)
    gup = np.asarray(gate_up, f)[0]
    n1 = np.ascontiguousarray(np.broadcast_to(np.asarray(norm1_w, f)[0][None], (128, D)))
    n2 = np.ascontiguousarray(np.broadcast_to(np.asarray(norm2_w, f)[0][None], (128, D)))
    cwv = np.asarray(conv_w, f)[0]
    convw = np.zeros((128, 44 * 3), f)
    for j in range(3):
        convw[:, j::3] = colmaj(cwv[j])
    convb = colmaj(conv_b[0])
    consts = np.zeros((128, 448), f)
    consts[:, 0:128] = np.eye(128, dtype=f)
    bo = np.zeros((128, 128), f)
    bo[:64, :64] = 1
    bo[64:, 64:] = 1
    consts[:, 128:256] = bo
    consts[:, 384:448] = np.concatenate([np.eye(64, dtype=f)] * 2, 0)
    ii = np.arange(128)
    su = (ii[:, None] < ii[None, :]).astype(f)
    sl_ = (ii[:, None] > ii[None, :]).astype(f)
    ui = (ii[:, None] <= ii[None, :]).astype(f)
    masks = np.concatenate([np.tile(su, (1, 4)), np.tile(sl_, (1, 4)), np.tile(ui, (1, 4))], 1)
    sk = np.asarray(sinks, f)[0]
    sinkx = np.zeros((128, 4), f)
    for hp in range(4):
        sinkx[:64, hp] = sk[2 * hp]
        sinkx[64:, hp] = sk[2 * hp + 1]
    xmeta = np.zeros((128, D), f)
    xmeta[:16] = meta
    NEG = f(-30000.0)

    def bias_tables(q):
        ab = np.zeros((3, 2, 128, 1024), f)
        abm = np.zeros((3, 2, 16, 512), f)
        kk = ii[:, None]
        qq = ii[None, :]
        d_prev = qq + 128 - kk
        d_cur = qq - kk
        for var in range(3):
            for g in range(2):
                for hh in range(4):
                    h = 4 * g + hh
                    bp = np.where((d_prev < 128), rel_bias[_t5_bucket(d_prev), h], NEG)
                    bc = np.where((d_cur >= 0), rel_bias[_t5_bucket(d_cur), h], NEG)
                    bm = np.full((16, 128), rel_bias[31, h], f)
                    if q == 0:
                        if var == 0:
                            bp = np.full((128, 128), NEG, f)
                            bc = np.full((128, 128), NEG, f)
                            mq = qq - 112
                            mk = np.arange(16)[:, None]
                            dm = mq - mk
                            bm = np.where(dm >= 0, rel_bias[_t5_bucket(dm), h], NEG)
                            bm = np.where(mq < 0, f(0.0), bm).astype(f)
                        elif var == 1:
                            bp = np.full((128, 128), NEG, f)
                            dm = (16 + qq) - np.arange(16)[:, None]
                            bm = rel_bias[_t5_bucket(dm), h].astype(f)
                    ab[var, g, :, hh * 128:(hh + 1) * 128] = bp
                    ab[var, g, :, 512 + hh * 128:512 + (hh + 1) * 128] = bc
                    abm[var, g, :, hh * 128:(hh + 1) * 128] = bm
        return ab, abm

    if "nc" not in _PROG:
        _PROG["nc"] = build_program()
    nc = _PROG["nc"]
    shared = {
        "xmeta": xmeta, "w_in": w_in_p, "w_out": np.ascontiguousarray(np.asarray(w_out, f)[0]),
        "w_up": np.ascontiguousarray(np.concatenate([np.asarray(w_up, f)[0][:, :DFF].reshape(D, NFC, 128),
                                                         np.asarray(w_up, f)[0][:, DFF:].reshape(D, NFC, 128)], axis=2).transpose(1, 0, 2)), "w_down": np.ascontiguousarray(np.asarray(w_down, f)[0]),
        "lut": np.ascontiguousarray(lut), "gup": np.ascontiguousarray(gup), "pcols": pcols, "n1w": n1, "n2w": n2,
        "convw": convw, "convb": convb, "consts": consts, "masks": np.ascontiguousarray(masks), "sinkx": sinkx,
    }
    tabs = {q: bias_tables(q) for q in range(4)}
    in_maps = []
    for c in range(8):
        b, q = c // 4, c % 4
        xloc = np.zeros((NTOK, D), f)
        nreal = 2048 * (q + 1)
        off = NTOK - 16 - nreal
        xloc[off:off + 16] = meta
        xloc[off + 16:] = x[b, :nreal]
        m = dict(shared)
        m["xl"] = xloc
        m["abias"], m["abiasm"] = tabs[q]
        in_maps.append(m)
    if _PROG.get("maps_only"):
        return in_maps
    res = run_bass_kernel_spmd(nc, in_maps, core_ids=list(range(8)))
    out = np.zeros((2, 8192, D), f)
    for c in range(8):
        b, q = c // 4, c % 4
        out[b, 2048 * q:2048 * (q + 1)] = np.asarray(res.results[c]["out"], f)
    return out
```

```python
import contextlib
import numpy as np
import concourse.bass as bass
import concourse.mybir as mybir
from concourse.bass_utils import run_bass_kernel_spmd

F32 = mybir.dt.float32
BF16 = mybir.dt.bfloat16
ALU = mybir.AluOpType
AF = mybir.ActivationFunctionType

D = 1024
NCS = 2
SBW = NCS * 128
NCH = 68
NSB = NCH // NCS
NTOK = NCH * 128
HALO_CH = 51
KV_CH = 50
OWN0 = 52
NPROJ = 21
DFF = 2816
NFC = 22
CW = 0.6065306597126334


class TB:
    def __init__(self, h, name):
        self.h = h
        self.name = name
        self.last_w = None
        self.reads = {}

    def __getitem__(self, idx):
        return V(self.h[idx], self)


class V:
    def __init__(self, ap, buf):
        self.ap = ap
        self.buf = buf


class Op:
    __slots__ = ("eng", "fns", "reads", "writes", "dma", "cost", "preds", "succs", "idx", "prio", "npred", "fin", "pos", "func", "dr", "why", "tag", "alt", "dr2")

    def __init__(self, eng, fn, reads, writes, dma, cost, func=None):
        self.eng = eng
        self.fns = [fn]
        self.reads = reads
        self.writes = writes
        self.dma = dma
        self.cost = cost
        self.preds = set()
        self.succs = []
        self.func = func
        self.alt = None


import os as _osx
XLAT = float(_osx.environ.get('KXLAT', '600'))


class Sched:
    def __init__(self, nc, es):
        self.nc = nc
        self.es = es
        self.engs = ["pe", "act", "dve", "pool", "sp"]
        self.sem = {}
        self.cnt = {}
        self.known = {e: {} for e in self.engs}
        self.prog = {e: [] for e in self.engs}
        for e in self.engs:
            self.sem[e] = es.enter_context(nc.semaphore("s_" + e))
            self.cnt[e] = 0
        self.ops = []
        self.open_mm = None
        self.final = []

    def dma_sem(self, name):
        key = "d_" + name
        if key not in self.sem:
            self.sem[key] = self.es.enter_context(self.nc.semaphore(key))
            self.cnt[key] = 0
        return key

    def op(self, eng, fn, reads, writes, dma=None, cost=100.0, append_to=None, func=None):
        if append_to is not None:
            o = append_to
            o.fns.append(fn)
            o.cost += cost
            for b in reads:
                if b not in o.reads:
                    o.reads.append(b)
            for b in writes:
                if b not in o.writes:
                    o.writes.append(b)
        else:
            o = Op(eng, fn, list(reads), list(writes), dma, cost, func)
            o.idx = len(self.ops)
            self.ops.append(o)
        return o

    def _build_deps(self, ops):
        for o in ops:
            for b in o.reads:
                if b.last_w is not None and b.last_w is not o:
                    o.preds.add(b.last_w)
            for b in o.writes:
                if b.last_w is not None and b.last_w is not o:
                    o.preds.add(b.last_w)
                for r in b.reads.values():
                    if r is not o:
                        o.preds.add(r)
            for b in o.reads:
                b.reads[id(o)] = o
            for b in o.writes:
                b.last_w = o
                b.reads = {}

    def flush(self, barrier=True):
        import heapq
        ops = self.ops
        self.ops = []
        self.open_mm = None
        if not ops:
            return
        self._build_deps(ops)
        inset = set(id(o) for o in ops)
        for o in ops:
            o.preds = set(p for p in o.preds if id(p) in inset)
            o.npred = len(o.preds)
            for p in o.preds:
                p.succs.append(o)
        for o in reversed(ops):
            m = 0.0
            for s in o.succs:
                if s.prio > m:
                    m = s.prio
            o.prio = m + o.cost + (XLAT if o.succs else 0.0)
        ready = {e: [] for e in self.engs}
        def push_ready(o):
            if o.alt is None:
                dr = 0.0
                for p in o.preds:
                    t = p.fin + (0.0 if p.eng == o.eng else XLAT)
                    if t > dr:
                        dr = t
                o.dr = dr
                ready[o.eng].append(o)
            else:
                o.dr2 = {}
                for e2 in o.alt:
                    dr = 0.0
                    for p in o.preds:
                        t = p.fin + (0.0 if p.eng == e2 else XLAT)
                        if t > dr:
                            dr = t
                    o.dr2[e2] = dr
                    ready[e2].append(o)
        for o in ops:
            if o.npred == 0:
                push_ready(o)
        free = {e: 0.0 for e in self.engs}
        lastfunc = {"act": None}
        order = {e: [] for e in self.engs}
        nleft = len(ops)
        while nleft:
            best = None
            for e in self.engs:
                rl = ready[e]
                if not rl:
                    continue
                fe = free[e]
                mn = min((o.dr if o.alt is None else o.dr2[e]) for o in rl)
                if mn < fe:
                    mn = fe
                pick = None
                for o in rl:
                    d_ = o.dr if o.alt is None else o.dr2[e]
                    st_ = d_ if d_ > fe else fe
                    if st_ <= mn + 150.0:
                        pen = 0.0
                        if e == "act" and o.func is not None and lastfunc["act"] is not None and o.func != lastfunc["act"]:
                            pen = 1300.0
                        sc = o.prio - pen
                        if pick is None or sc > pick[2]:
                            pick = (st_, o, sc)
                if best is None or pick[0] < best[0]:
                    best = (pick[0], pick[1], e)
            st_, o, e = best
            ready[e].remove(o)
            if o.alt is not None:
                for e2 in o.alt:
                    if e2 != e:
                        ready[e2].remove(o)
                o.eng = e
                o.fns = [o.alt[e]]
                o.dr = o.dr2[e]
            if o.dr >= free[e]:
                bp = None
                for p in o.preds:
                    t = p.fin + (0.0 if p.eng == e else XLAT)
                    if bp is None or t > bp[0]:
                        bp = (t, p)
                o.why = ("dep", bp[1] if bp else None)
            else:
                o.why = ("eng", order[e][-1] if order[e] else None)
            issue = o.cost
            if o.dma is not None:
                o.fin = st_ + o.cost
                free[e] = st_ + 60.0 * len(o.fns)
            else:
                o.fin = st_ + issue
                free[e] = o.fin
            if e == "act" and o.func is not None:
                lastfunc["act"] = o.func
            order[e].append(o)
            nleft -= 1
            for s in o.succs:
                s.npred -= 1
                if s.npred == 0:
                    push_ready(s)
        self.est = max(free.values())
        self.busy = {e: sum(o.cost for o in order[e]) for e in self.engs}
        import os as _os
        if _os.environ.get("KCRIT"):
            last = max(ops, key=lambda o: o.fin)
            cnt_ = {}
            o = last
            n_ = 0
            while o is not None and n_ < 200000:
                kind, p = o.why
                key_ = (kind, o.eng, (o.writes[0].name if o.writes else "-")[:6])
                cnt_[key_] = cnt_.get(key_, 0.0) + o.cost + (XLAT if kind == "dep" and p is not None and p.eng != o.eng else 0.0)
                o = p
                n_ += 1
            for kk__, vv in sorted(cnt_.items(), key=lambda kv: -kv[1])[:40]:
                print("[crit]", kk__, int(vv))
        print('[sched] region est', int(self.est), 'busy', {e: int(v) for e, v in self.busy.items()}, 'nops', {e: len(order[e]) for e in self.engs})
        for e in self.engs:
            for o in order[e]:
                key = o.dma if o.dma is not None else e
                step = 16 * len(o.fns) if o.dma is not None else 1
                self.cnt[key] += step
                o.pos = (key, self.cnt[key])
        for e in self.engs:
            for o in order[e]:
                need = {}
                for p in o.preds:
                    k_, v_ = p.pos
                    if k_ == "pe" and e == "pe":
                        continue
                    if need.get(k_, 0) < v_:
                        need[k_] = v_
                for k_, v_ in need.items():
                    if self.known[e].get(k_, 0) >= v_:
                        continue
                    self.prog[e].append(("wait", k_, v_))
                    self.known[e][k_] = v_
                n = len(o.fns)
                for i_, fn in enumerate(o.fns):
                    if o.dma is not None:
                        self.prog[e].append(("op", fn, o.pos[0], 16))
                    else:
                        self.prog[e].append(("op", fn, o.pos[0] if i_ == n - 1 else None, 1))
        for o in ops:
            o.preds = None
            o.succs = None
        if barrier:
            for e in self.engs:
                for key, v in self.cnt.items():
                    if v > 0 and key != e and self.known[e].get(key, 0) < v:
                        self.prog[e].append(("wait", key, v))
                        self.known[e][key] = v
        for o in ops:
            for b in o.reads:
                b.reads = {}
            for b in o.writes:
                b.last_w = None
                b.reads = {}

    def emit(self, block):
        sems = self.sem

        def run(e, name):
            for it in self.prog[name]:
                if it[0] == "wait":
                    e.wait_ge(sems[it[1]], it[2])
                else:
                    ins = it[1](e)
                    if it[2] is not None:
                        ins.then_inc(sems[it[2]], it[3])
            if name == "sp":
                for k_, v_ in self.cnt.items():
                    if k_.startswith("d_") and v_ > 0:
                        e.wait_ge(sems[k_], v_)

        @block.tensor
        def _(e):
            run(e, "pe")

        @block.scalar
        def _(e):
            run(e, "act")

        @block.vector
        def _(e):
            run(e, "dve")

        @block.gpsimd
        def _(e):
            run(e, "pool")

        @block.sync
        def _(e):
            run(e, "sp")


class Sub:
    def __init__(self, tb, off, width):
        self.tb = tb
        self.off = off
        self.width = width
        self.name = tb.name

    def __getitem__(self, idx):
        rows, cols = idx
        a = 0 if cols.start is None else cols.start
        b_ = self.width if cols.stop is None else cols.stop
        return self.tb[rows, self.off + a:self.off + b_]


def bufs(*vs):
    out = []
    for v in vs:
        if isinstance(v, V) and v.buf not in out:
            out.append(v.buf)
    return out


def A(x):
    return x.ap if isinstance(x, V) else x


def fsz(v):
    n = 1
    for s in v.ap.shape[1:]:
        n *= int(s)
    return n


class K:
    def __init__(self, s):
        self.s = s

    def mm(self, out, lhsT, rhs, start=True, stop=True):
        n = fsz(out)
        c = max(160.0, 100.0 + n * 0.42)
        if lhsT.ap.dtype == F32:
            c *= 4
        fn = lambda e: e.matmul(out.ap, lhsT.ap, rhs.ap, start=start, stop=stop)
        if start:
            o = self.s.op("pe", fn, bufs(lhsT, rhs), bufs(out), cost=c)
            self.s.open_mm = None if stop else o
        else:
            o = self.s.op("pe", fn, bufs(lhsT, rhs), bufs(out), cost=c, append_to=self.s.open_mm)
            if stop:
                self.s.open_mm = None

    def tr(self, out, in_, ident):
        self.s.op("pe", lambda e: e.transpose(out.ap, in_.ap, ident.ap), bufs(in_, ident), bufs(out), cost=130.0)

    def act(self, out, in_, func, bias=0.0, scale=1.0, accum=None, eng="act"):
        kw = {}
        if accum is not None:
            kw["accum_out"] = accum.ap
        fam = {AF.Exp: "exp", AF.Tanh: "exp", AF.Sigmoid: "sig", AF.Sqrt: "sqrt", AF.Silu: "silu"}.get(func)
        self.s.op("act", lambda e: e.activation(out=out.ap, in_=in_.ap, func=func, bias=A(bias), scale=A(scale), **kw),
                  bufs(in_, bias, scale), bufs(out, accum), cost=200.0 + fsz(out) * 0.75, func=fam)

    def raw(self, eng, fn, reads, writes, n):
        self.s.op(eng, fn, reads, writes, cost=160.0 + n * 0.9)

    def tt(self, out, a, b, op, eng="dve"):
        self.s.op(eng, lambda e: e.tensor_tensor(out=out.ap, in0=a.ap, in1=b.ap, op=op), bufs(a, b), bufs(out),
                  cost=120.0 + fsz(out) * 0.85)

    def ts(self, out, a, s1, s2, op0, op1=None, eng="dve"):
        c = 120.0 + fsz(out) * 0.6
        if op1 is None:
            self.s.op(eng, lambda e: e.tensor_scalar(out=out.ap, in0=a.ap, scalar1=A(s1), scalar2=None, op0=op0),
                      bufs(a, s1), bufs(out), cost=c)
        else:
            self.s.op(eng, lambda e: e.tensor_scalar(out=out.ap, in0=a.ap, scalar1=A(s1), scalar2=A(s2), op0=op0, op1=op1),
                      bufs(a, s1, s2), bufs(out), cost=c)

    def stt(self, out, a, sc, b, op0, op1):
        self.s.op("dve", lambda e: e.scalar_tensor_tensor(out=out.ap, in0=a.ap, scalar=A(sc), in1=b.ap, op0=op0, op1=op1),
                  bufs(a, sc, b), bufs(out), cost=120.0 + fsz(out) * 0.85)

    def cp(self, out, in_, eng="dve"):
        if eng == "act":
            o = self.s.op("act", lambda e: e.activation(out=out.ap, in_=in_.ap, func=AF.Copy), bufs(in_), bufs(out),
                          cost=200.0 + fsz(out) * 0.75)
            o.alt = {"act": o.fns[0], "dve": (lambda e: e.tensor_copy(out=out.ap, in_=in_.ap))}
        else:
            self.s.op(eng, lambda e: e.tensor_copy(out=out.ap, in_=in_.ap), bufs(in_), bufs(out), cost=120.0 + fsz(out) * 0.6)

    def recip(self, out, in_):
        self.s.op("dve", lambda e: e.reciprocal(out=out.ap, in_=in_.ap), bufs(in_), bufs(out), cost=150.0 + fsz(out) * 5.5)

    def scan(self, out, ones, x):
        self.s.op("dve", lambda e: e.tensor_tensor_scan(out=out.ap, data0=ones.ap, data1=x.ap, initial=0.0,
                                                        op0=ALU.mult, op1=ALU.add), bufs(ones, x), bufs(out), cost=150.0 + fsz(out) * 2.1)

    def memset(self, out, val, eng="dve"):
        self.s.op(eng, lambda e: e.memset(out.ap, val), [], bufs(out), cost=100.0 + fsz(out) * 0.3)

    def raw_dma(self, out_tb, in_tb, fn, name, eng, nbytes):
        key = self.s.dma_sem(name)
        return self.s.op(eng, fn, [in_tb], [out_tb], dma=key, cost=2200.0 + nbytes / 60.0)

    def dma(self, out, in_, name, eng="sp", group=None):
        key = self.s.dma_sem(name)
        nbytes = 128 * fsz(out) * 4
        c = 2200.0 + nbytes / 60.0
        fn = lambda e: e.dma_start(out=out.ap, in_=in_.ap)
        if group is not None:
            return self.s.op(eng, fn, bufs(in_), bufs(out), dma=key, cost=nbytes / 120.0, append_to=group)
        return self.s.op(eng, fn, bufs(in_), bufs(out), dma=key, cost=c)


def build_program(debug=False, trunc=None):
    import os
    trunc = trunc or os.environ.get('KTRUNC', '')
    kstep = int(os.environ.get('KSTEP', '99'))
    kvar = os.environ.get('KVAR', '')
    nc = bass.Bass("TRN2", target_bir_lowering=False)
    es = contextlib.ExitStack()
    S = Sched(nc, es)
    k = K(S)
    W_ = SBW

    def dram(name, shape, dt=F32, kind="ExternalInput"):
        return TB(nc.dram_tensor(name, shape, dt, kind=kind).ap(), name)

    def sb(name, shape, dt=F32):
        return TB(es.enter_context(nc.sbuf_tensor(name, shape, dt)), name)

    def ps(name, shape, dt=F32):
        return TB(es.enter_context(nc.psum_tensor(name, shape, dt)), name)

    xl = dram("xl", [NTOK, D])
    xmeta = dram("xmeta", [128, D])
    w_in_d = dram("w_in", [D, NPROJ * 128])
    w_out_d = dram("w_out", [D, D])
    w_up_d = dram("w_up", [NFC, D, 256])
    w_down_d = dram("w_down", [DFF, D])
    lut_d = dram("lut", [128, 512])
    gup_d = dram("gup", [128, 512])
    pc_d = dram("pcols", [128, 64])
    n1_d = dram("n1w", [128, D])
    n2_d = dram("n2w", [128, D])
    cw_d = dram("convw", [128, 44 * 3])
    cb_d = dram("convb", [128, 44])
    cst_d = dram("consts", [128, 448])
    msk_d = dram("masks", [128, 3 * 512])
    bias_d = dram("abias", [3, 2, 128, 2 * 512])
    biasm_d = dram("abiasm", [3, 2, 16, 512])
    snk_d = dram("sinkx", [128, 4])
    out_d = dram("out", [2048, D], kind="ExternalOutput")
    hmid_d = dram("hmid", [17 * 128, D], kind="Internal")

    pc = sb("pc", [128, 64])
    pc2 = sb("pc2", [128, 32])
    nw = sb("nw_s", [128, D])
    ident = sb("ident", [128, 128], BF16)
    mhalf = sb("mhalf", [128, 1])
    bones = sb("bones", [128, 128])
    irep2 = sb("irep2", [128, 64])

    MU, KK_, KA, W0, A0, RK, LW, LB, QW, KW = 0, 14, 18, 22, 26, 30, 34, 38, 42, 43
    OMU, OMKA, ESK = 0, 14, 18

    NFB = 6
    pbanks = [ps("pb%d" % i, [128, 512]) for i in range(NFB)]
    pbh = [ps("pbf%d" % i, [128, 1024], BF16) for i in range(2)]
    rot = [0]

    NPB = 2

    npb_cur = [NPB]

    def bank():
        b = pbanks[rot[0] % npb_cur[0]]
        rot[0] += 1
        return b
    crot = [0]

    def cbank():
        b = pbanks[NPB + crot[0] % (NFB - NPB)]
        crot[0] += 1
        return b
    trot = [0]

    def tbank():
        b = pbh[trot[0] % 2]
        trot[0] += 1
        return b

    esP = contextlib.ExitStack()

    def sbp(name, shape, dt=F32):
        return TB(esP.enter_context(nc.sbuf_tensor(name, shape, dt)), name)
    w_in = sbp("w_in_bf", [128, 8, NPROJ * 128], BF16)
    yrw = sbp("yrw", [128, 17, 4, 128], BF16)
    ones64 = sbp("ones64", [128, 64], BF16)
    kmeta = sbp("kmeta", [128, 2, 16], BF16)
    vmeta = sbp("vmeta", [16, 128], BF16)
    esA = contextlib.ExitStack()

    def sba(name, shape, dt=F32):
        return TB(esA.enter_context(nc.sbuf_tensor(name, shape, dt)), name)

    lut = sba("lut_bf", [128, 512], BF16)
    gup = sba("gup_bf", [128, 512], BF16)
    irep8 = sba("irep8", [128, 512], BF16)
    msu = sba("msu", [128, 512], BF16)
    msl = sba("msl", [128, 512], BF16)
    mui = sba("mui", [128, 512], BF16)
    ones_t = sba("ones_t", [128, 128])
    rst = sba("rst", [128, 2 * W_])
    carry = sba("carry", [128, 14])
    Sst = [sba("S%d" % i, [128, 256], BF16) for i in range(2)]
    irep2x4 = sba("irep2x4", [128, 256])
    xb = [sba("xb%d" % i, [128, D]) for i in range(2)]
    utok = sba("utok", [128, D], BF16)
    st = [sba("st%d" % i, [128, 4]) for i in range(4)]
    uT = sba("uT", [128, 8, W_ + 1], BF16)
    tmp = [sba("tmp%d" % i, [128, W_]) for i in range(2)]
    def pp_set(alloc, sfx):
        wide = {nm: [alloc("pp%s%d%s" % (nm, g, sfx), [128, 2 * W_]) for g in range(2)] for nm in ("R", "K", "V")}
        lst = []
        for nm in ("R", "K", "V"):
            for hp in range(4):
                lst.append(Sub(wide[nm][hp // 2], (hp % 2) * W_, W_))
        lst.append(alloc("pp12" + sfx, [128, W_]))
        lst.append(alloc("pp13" + sfx, [128, W_]))
        return lst, wide
    pp, ppw = pp_set(sba, "")
    lo_bf = sba("lo_bf", [128, W_], BF16)
    snk = sba("snk", [128, 4])

    t1 = [sba("t1_%d" % i, [128, W_]) for i in range(3)]
    TW = {nm: sba(nm, [128, 2 * W_]) for nm in ("lwW", "alW", "kkW", "kfW", "clW", "g1W", "xA", "xB", "xC")}
    def wide_heads(alloc, nm, sfx):
        w_ = [alloc("%sW%d%s" % (nm, g, sfx), [128, 2 * W_], BF16) for g in range(2)]
        return [Sub(w_[hp // 2], (hp % 2) * W_, W_) for hp in range(4)], w_

    def sb_set(alloc, sfx):
        d_ = {}
        for nm in ("At", "Bt", "Kt", "vbf"):
            d_[nm], d_[nm + "W"] = wide_heads(alloc, nm, sfx)
        d_["gC"] = alloc("gC" + sfx, [128, 4, NCS])
        return d_
    SBS = [sb_set(sba, "")]
    CS = []
    for ci_ in range(NCS):
        s_ = {}
        for nm in ("A_tok", "Bh_tok", "Kh_tok", "V_tok", "TA", "W1", "Ul"):
            s_[nm] = sba("%s_%d" % (nm, ci_), [128, 512], BF16)
        for nm in ("N", "L", "TT"):
            s_[nm] = [[sba("%s%d_%d_%d" % (nm, ci_, i, j), [128, 512], BF16) for j in range(2)] for i in range(2)]
        for nm in ("AkT",):
            s_[nm] = [sba("%s%d_%d" % (nm, ci_, j), [128, 512], BF16) for j in range(2)]
        s_["McT"] = sba("McT_%d" % ci_, [128, 512], BF16)
        CS.append(s_)

    esA1 = contextlib.ExitStack()

    def sba1(name, shape, dt=F32):
        return TB(esA1.enter_context(nc.sbuf_tensor(name, shape, dt)), name)
    SBS.append(sb_set(sba1, "_b"))
    pp_b, ppw_b = pp_set(sba1, "_b")
    PPS = [(pp, ppw), (pp_b, ppw_b)]
    print('[mem] phase A1 free bytes/partition:', nc.sbuf_bytes_remaining)
    g_ = k.dma(lut[:, :], lut_d[:, :], "wc", eng="pool")
    k.dma(gup[:, :], gup_d[:, :], "wc", eng="pool", group=g_)
    import os as _os2
    nosplit = bool(_os2.environ.get("KNOSPLIT"))
    for kk_ in range(8):
        if nosplit:
            k.dma(w_in[:, kk_, :], w_in_d[kk_ * 128:(kk_ + 1) * 128, :], "wc", eng="pool", group=g_)
        else:
            k.dma(w_in[:, kk_, 512:1664], w_in_d[kk_ * 128:(kk_ + 1) * 128, 512:1664], "wc", eng="pool", group=g_)
    g_ = k.dma(pc[:, :], pc_d[:, :], "c0")
    k.dma(nw[:, :], n1_d[:, :], "c0", group=g_)
    k.dma(bones[:, :], cst_d[:, 128:256], "c0", group=g_)
    k.dma(irep2[:, :], cst_d[:, 384:448], "c0", group=g_)
    for i in range(4):
        k.dma(irep2x4[:, i * 64:(i + 1) * 64], cst_d[:, 384:448], "c0", group=g_)
    k.dma(snk[:, :], snk_d[:, :], "c0", group=g_)
    g_ = k.dma(ident[:, :], cst_d[:, 0:128], "c1", eng="pool")
    for i in range(4):
        k.dma(irep8[:, i * 128:(i + 1) * 128], cst_d[:, 0:128], "c1", eng="pool", group=g_)
    k.dma(msu[:, :], msk_d[:, 0:512], "c1", eng="pool", group=g_)
    k.dma(msl[:, :], msk_d[:, 512:1024], "c1", eng="pool", group=g_)
    k.dma(mui[:, :], msk_d[:, 1024:1536], "c1", eng="pool", group=g_)
    g_ = None
    for kk_ in range(8):
        if nosplit:
            break
        r_ = k.dma(w_in[:, kk_, 0:512], w_in_d[kk_ * 128:(kk_ + 1) * 128, 0:512], "wc_b", eng="pool", group=g_)
        g_ = g_ or r_
        k.dma(w_in[:, kk_, 1664:NPROJ * 128], w_in_d[kk_ * 128:(kk_ + 1) * 128, 1664:NPROJ * 128], "wc_b", eng="pool", group=g_)
    k.memset(ones_t[:, :], 1.0)
    k.memset(mhalf[:, :], -0.5)
    k.memset(rst[:, :], 1.0)
    for i in range(2 * W_ // 128):
        k.memset(rst[:, i * 128:i * 128 + 1], 0.0)
    k.memset(ones64[:, :], 1.0)
    k.raw("dve", lambda e: e.memset(uT.h[:, :, :], 0.0), [], [uT], 2056)
    k.memset(Sst[0][:, :], 0.0)
    for cs__ in CS:
        k.memset(cs__["McT"][:, :], 0.0)
    k.ts(pc2[:, OMU:OMU + 14], pc[:, MU:MU + 14], -1.0, 1.0, ALU.mult, ALU.add)
    k.ts(pc2[:, OMKA:OMKA + 4], pc[:, KA:KA + 4], -1.0, 1.0, ALU.mult, ALU.add)
    k.ts(pc2[:, 22:26], pc[:, W0:W0 + 4], 0.5, None, ALU.mult)
    k.ts(pc2[:, 26:30], pc[:, A0:A0 + 4], 0.5, None, ALU.mult)
    k.act(pc2[:, ESK:ESK + 4], snk[:, :], AF.Exp)
    def rstd_from_ss(out, ss_view, inv_n, eps):
        k.ts(out, ss_view, inv_n, eps, ALU.mult, ALU.add)
        k.act(out, out, AF.Sqrt)
        k.recip(out, out)

    def norm_block(x_t, ut, wt, si, stl):
        s_ = stl[si % 4]
        k.act(ut[:, :], x_t[:, :], AF.Square, accum=s_[:, 0:1])
        k.ts(s_[:, 2:3], s_[:, 0:1], 1.0 / D, 1e-6, ALU.mult, ALU.add)
        k.tt(s_[:, 1:2], s_[:, 2:3], mhalf[:, :], ALU.pow, eng="pool")
        k.stt(ut[:, :], x_t[:, :], s_[:, 1:2], wt[:, :], ALU.mult, ALU.mult)

    def transpose_block(ut, dst, col0, src_cols=None):
        for hlf in range(2):
            pt_ = tbank()
            for d4 in range(4):
                dk = hlf * 4 + d4
                k.tr(pt_[:, d4 * 128:(d4 + 1) * 128], ut[:, dk * 128:(dk + 1) * 128], ident[:, :])
            if src_cols is None:
                k.raw("act", lambda e, pt_=pt_, hlf=hlf: e.activation(
                    out=dst.h[:, hlf * 4:(hlf + 1) * 4, col0:col0 + 128],
                    in_=pt_.h[:, 0:512].rearrange("p (k t) -> p k t", k=4), func=AF.Copy), [pt_], [dst], 512)
            else:
                a, b_ = src_cols
                k.raw("act", lambda e, pt_=pt_, hlf=hlf, a=a, b_=b_: e.activation(
                    out=dst.h[:, hlf * 4:(hlf + 1) * 4, col0:col0 + (b_ - a)],
                    in_=pt_.h[:, 0:512].rearrange("p (k t) -> p k t", k=4)[:, :, a:b_], func=AF.Copy), [pt_], [dst], 16)

    def project(cc, n=W_, shift=False):
        b = bank()
        lo_ = 0 if shift else 1
        n_ = n + 1 if shift else n
        for dk in range(8):
            k.mm(b[:, 0:n_], w_in[:, dk, cc * 128:(cc + 1) * 128], uT[:, dk, lo_:lo_ + n_], start=(dk == 0), stop=(dk == 7))
        return b

    def kv_norm_k(psb, dst, n):
        k.act(t1[0][:, 0:n], psb[:, 0:n], AF.Square)
        b2 = bank()
        k.mm(b2[:, 0:n], bones[:, :], t1[0][:, 0:n])
        rstd_from_ss(t1[1][:, 0:n], b2[:, 0:n], 1.0 / 64, 1e-6)
        k.stt(dst, psb[:, 0:n], pc[:, KW:KW + 1], t1[1][:, 0:n], ALU.mult, ALU.mult)

    k.dma(xb[0][:, :], xmeta[:, :], "xb0")
    norm_block(xb[0], utok, nw, 0, st)
    transpose_block(utok, uT, 1)
    for g in range(2):
        b = project(18 + g, 128)
        kv_norm_k(b, t1[2][:, 0:128], 128)
        k.cp(kmeta[:, g, :], t1[2][:, 0:16])
    b = project(20, 128)
    k.cp(lo_bf[:, 0:128], b[:, 0:128], eng="act")
    pt_ = tbank()
    k.tr(pt_[:, 0:128], lo_bf[:, 0:128], ident[:, :])
    k.cp(vmeta[:, :], pt_[0:16, 0:128])

    xcnt = [1]
    nsb_run = NSB
    if trunc.startswith('nsb'):
        nsb_run = int(trunc[3:])
    if trunc == 'meta':
        nsb_run = 0
    for sbi in range(nsb_run):
        ch0 = sbi * NCS
        full = ch0 >= 48
        par = sbi % 2
        if ch0 == 48:
            S.flush()
            print('[sched] modelled phase A1 (ns):', S.est)
            esA1.close()

            def sba2(name, shape, dt=F32):
                return TB(esA.enter_context(nc.sbuf_tensor(name, shape, dt)), name)
            sg_bf = sba2("sg_bf", [128, W_], BF16)
            gq = [sba2("gq%d" % i, [128, W_], BF16) for i in range(4)]
            Rt, RtW = wide_heads(sba2, "Rt", "")
            bon, bonW = wide_heads(sba2, "bon", "")
            t2 = [sba2("t2_%d" % i, [128, 512]) for i in range(2)]
            GTm = [sba2("GT%d" % i, [128, 512], BF16) for i in range(2)]
            for i in range(2):
                k.memset(GTm[i][:, :], 0.0)
            yt = sba2("yt", [128, 512])
            for cs__ in CS:
                for nm in ("RbT", "RkT"):
                    cs__[nm] = [sba2("%s%s_%d" % (nm, cs__["McT"].name[-2:], j), [128, 512], BF16) for j in range(2)]
            print('[mem] phase A2 free bytes/partition:', nc.sbuf_bytes_remaining)
            SBS = SBS[:1]
            PPS = PPS[:1]
            print('[mem] phase A2 free after temps:', nc.sbuf_bytes_remaining)
        sbs_ = SBS[sbi % len(SBS)]
        At, Bt, Kt, vbf, gC = (sbs_[n_] for n_ in ("At", "Bt", "Kt", "vbf", "gC"))
        AtW, BtW, KtW, vbfW = (sbs_[n_ + "W"] for n_ in ("At", "Bt", "Kt", "vbf"))
        pp, ppw = PPS[sbi % len(PPS)]
        k.raw("dve", lambda e: e.tensor_copy(out=uT.h[:, :, 0:1], in_=uT.h[:, :, W_:W_ + 1]), [uT], [uT], 8)
        for c4 in range(NCS):
            ch = ch0 + c4
            xt_ = xb[xcnt[0] % 2]
            k.dma(xt_[:, :], xl[ch * 128:(ch + 1) * 128, :], "xb%d" % (xcnt[0] % 2))
            norm_block(xt_, utok, nw, xcnt[0], st)
            transpose_block(utok, uT, 1 + c4 * 128)
            xcnt[0] += 1
        if kstep < 1:
            continue
        chunks = list(range(0, 14)) if full else list(range(4, 13))
        for cc in chunks:
            b = project(cc, shift=True)
            tm = tmp[cc % len(tmp)]
            k.act(tm[:, 0:W_], b[:, 0:W_], AF.Copy, scale=pc[:, MU + cc:MU + cc + 1])
            k.stt(pp[cc][:, :], b[:, 1:W_ + 1], pc2[:, OMU + cc:OMU + cc + 1], tm[:, 0:W_], ALU.mult, ALU.add)
        if kstep < 2:
            continue
        k.act(lo_bf[0:64, :], pp[12][0:64, :], AF.Tanh)
        k.cp(lo_bf[64:128, :], pp[12][64:128, :])
        if full:
            k.act(t1[0][:, :], pp[13][:, :], AF.Tanh, scale=0.5)
            k.ts(sg_bf[:, :], t1[0][:, :], 0.5, 0.5, ALU.mult, ALU.add)
        lwW, alW, kkW, kfW, clW, g1W, xA, xB, xC = (TW[n_] for n_ in ("lwW", "alW", "kkW", "kfW", "clW", "g1W", "xA", "xB", "xC"))
        for g2 in range(2):
            kW, vW, rW = ppw["K"][g2], ppw["V"][g2], ppw["R"][g2]
            for i2 in range(2):
                hp = 2 * g2 + i2
                cs = slice(i2 * W_, (i2 + 1) * W_)
                b = bank()
                k.mm(b[:, 0:W_], lut[0:64, hp * 128:(hp + 1) * 128], lo_bf[0:64, :])
                k.act(lwW[:, cs], b[:, 0:W_], AF.Tanh, bias=pc2[:, 22 + hp:23 + hp], scale=0.5)
                b = bank()
                k.mm(b[:, 0:W_], lut[64:128, hp * 128:(hp + 1) * 128], lo_bf[64:128, :])
                k.act(alW[:, cs], b[:, 0:W_], AF.Tanh, bias=pc2[:, 26 + hp:27 + hp], scale=0.5)
                if full:
                    b = bank()
                    k.mm(b[:, 0:W_], gup[:, hp * 128:(hp + 1) * 128], sg_bf[:, :])
                    k.cp(gq[hp][:, :], b[:, 0:W_], eng="act")
                k.ts(xA[:, cs], kW[:, cs], pc[:, KK_ + hp:KK_ + hp + 1], None, ALU.mult)
            k.ts(lwW[:, :], lwW[:, :], 0.5, 0.5, ALU.mult, ALU.add)
            k.ts(alW[:, :], alW[:, :], 0.5, 0.5, ALU.mult, ALU.add)
            k.act(xB[:, :], xA[:, :], AF.Square)
            b = bank()
            k.mm(b[:, :], bones[:, :], xB[:, :])
            k.act(xB[:, :], b[:, :], AF.Sqrt)
            k.ts(xB[:, :], xB[:, :], 1e-12, None, ALU.max)
            k.recip(xB[:, :], xB[:, :])
            k.tt(kkW[:, :], xA[:, :], xB[:, :], ALU.mult)
            k.tt(xC[:, :], kkW[:, :], alW[:, :], ALU.mult)
            for i2 in range(2):
                hp = 2 * g2 + i2
                cs = slice(i2 * W_, (i2 + 1) * W_)
                k.ts(xA[:, cs], alW[:, cs], pc[:, KA + hp:KA + hp + 1], pc2[:, OMKA + hp:OMKA + hp + 1], ALU.mult, ALU.add)
            k.tt(kfW[:, :], kW[:, :], xA[:, :], ALU.mult)
            k.scan(clW[:, :], rst[:, :], lwW[:, :])
            k.tt(xA[:, :], clW[:, :], lwW[:, :], ALU.subtract)
            k.act(g1W[:, :], clW[:, :], AF.Exp, scale=-CW)
            k.act(xB[:, :], xA[:, :], AF.Exp, scale=-CW)
            k.act(alW[:, :], clW[:, :], AF.Exp, scale=CW)
            k.stt(AtW[g2][:, :], kkW[:, :], -1.0, xB[:, :], ALU.mult, ALU.mult)
            k.tt(BtW[g2][:, :], xC[:, :], alW[:, :], ALU.mult)
            k.tt(KtW[g2][:, :], kfW[:, :], alW[:, :], ALU.mult)
            k.raw("dve", lambda e, gC=gC, g1W=g1W, g2=g2: e.tensor_copy(
                out=gC.h[:, 2 * g2:2 * g2 + 2, :],
                in_=g1W.h[:, :].rearrange("p (a c) -> p a c", c=128)[:, :, 127:128].rearrange("p (i c4) o -> p i (c4 o)", i=2)),
                [g1W], [gC], 4)
            k.cp(vbfW[g2][:, :], vW[:, :], eng="act")
            if full:
                k.tt(RtW[g2][:, :], rW[:, :], g1W[:, :], ALU.mult)
                for i2 in range(2):
                    hp = 2 * g2 + i2
                    cs = slice(i2 * W_, (i2 + 1) * W_)
                    k.stt(xA[:, cs], rW[:, cs], pc[:, RK + hp:RK + hp + 1], kfW[:, cs], ALU.mult, ALU.mult)
                b = bank()
                k.mm(b[:, :], bones[:, :], xA[:, :])
                k.tt(bonW[g2][:, :], b[:, :], vW[:, :], ALU.mult)
        if kstep < 3:
            continue
        CH = [ch0 + c4 for c4 in range(NCS)]
        SL = [slice(c4 * 128, (c4 + 1) * 128) for c4 in range(NCS)]
        OUT = [ch >= HALO_CH for ch in CH]
        for c4 in range(NCS):
            cs_ = CS[c4]
            for src, dst in ((At, cs_["A_tok"]), (Bt, cs_["Bh_tok"]), (Kt, cs_["Kh_tok"]), (vbf, cs_["V_tok"])):
                pt_ = tbank()
                for hp in range(4):
                    k.tr(pt_[:, hp * 128:(hp + 1) * 128], src[hp][:, SL[c4]], ident[:, :])
                k.cp(dst[:, :], pt_[:, 0:512], eng="act")
        if kstep < 4:
            continue
        cur = [0] * NCS
        for c4 in range(NCS):
            cs_ = CS[c4]
            sl = SL[c4]
            pn = [cbank(), cbank()]
            pl = [cbank(), cbank()]
            for h in range(8):
                hp, hf = h // 2, h % 2
                rows = slice(64 * hf, 64 * hf + 64)
                cs = slice(hp * 128, hp * 128 + 128)
                k.mm(pn[hf][:, cs], Bt[hp][rows, sl], At[hp][rows, sl])
                k.mm(pl[hf][:, cs], At[hp][rows, sl], Bt[hp][rows, sl])
            for j in range(2):
                k.tt(cs_["N"][0][j][:, :], pn[j][:, :], msu[:, :], ALU.mult)
                k.tt(cs_["L"][0][j][:, :], pl[j][:, :], msl[:, :], ALU.mult)
                k.tt(cs_["TT"][0][j][:, :], cs_["N"][0][j][:, :], irep8[:, :], ALU.add)
            pa = [cbank(), cbank()]
            for h in range(8):
                hp, hf = h // 2, h % 2
                rows = slice(64 * hf, 64 * hf + 64)
                cs = slice(hp * 128, hp * 128 + 128)
                k.mm(pa[hf][:, cs], Kt[hp][rows, sl], At[hp][rows, sl])
            for j in range(2):
                k.tt(cs_["AkT"][j][:, :], pa[j][:, :], msu[:, :], ALU.mult)
            if OUT[c4]:
                for (lt, dst) in ((Bt, cs_["RbT"]), (Kt, cs_["RkT"])):
                    pr_ = [cbank(), cbank()]
                    for h in range(8):
                        hp, hf = h // 2, h % 2
                        rows = slice(64 * hf, 64 * hf + 64)
                        cs = slice(hp * 128, hp * 128 + 128)
                        k.mm(pr_[hf][:, cs], lt[hp][rows, sl], Rt[hp][rows, sl])
                    for j in range(2):
                        k.tt(dst[j][:, :], pr_[j][:, :], mui[:, :], ALU.mult)
        if kstep < 5:
            continue
        for lev in range(6):
            for c4 in range(NCS):
                cs_ = CS[c4]
                c_ = cur[c4]
                n_ = 1 - c_
                Nb_, Lb_, TT_ = cs_["N"], cs_["L"], cs_["TT"]
                for j in range(2):
                    b = cbank()
                    for i in range(4):
                        cs = slice(i * 128, (i + 1) * 128)
                        k.mm(b[:, cs], Nb_[c_][j][:, cs], Lb_[c_][j][:, cs])
                    k.cp(Lb_[n_][j][:, :], b[:, :], eng="act")
                if lev < 5:
                    for j in range(2):
                        b = cbank()
                        for i in range(4):
                            cs = slice(i * 128, (i + 1) * 128)
                            k.mm(b[:, cs], Lb_[c_][j][:, cs], Nb_[c_][j][:, cs])
                        k.cp(Nb_[n_][j][:, :], b[:, :], eng="act")
                for j in range(2):
                    b = cbank()
                    for i in range(4):
                        cs = slice(i * 128, (i + 1) * 128)
                        k.mm(b[:, cs], Lb_[n_][j][:, cs], TT_[c_][j][:, cs])
                    k.tt(TT_[n_][j][:, :], b[:, :], TT_[c_][j][:, :], ALU.add)
                cur[c4] = n_
        if kstep < 6:
            continue

        def hcs(h):
            return slice((h // 2) * 128, (h // 2) * 128 + 128)

        def h64(h):
            return slice(h * 64, (h + 1) * 64)
        for c4 in range(NCS):
            cs_ = CS[c4]
            TT = cs_["TT"][cur[c4]]
            A_tok, Bh_tok, V_tok = cs_["A_tok"], cs_["Bh_tok"], cs_["V_tok"]
            AkT, TA, W1, Ul, McT = cs_["AkT"], cs_["TA"], cs_["W1"], cs_["Ul"], cs_["McT"]
            b = cbank()
            for h in range(8):
                k.mm(b[:, h64(h)], TT[h % 2][:, hcs(h)], A_tok[:, h64(h)])
            k.cp(TA[:, :], b[:, :], eng="act")
            b = cbank()
            for h in range(8):
                k.mm(b[:, h64(h)], AkT[h % 2][:, hcs(h)], V_tok[:, h64(h)])
            k.cp(W1[:, :], b[:, :], eng="act")
            b = cbank()
            for h in range(8):
                k.mm(b[:, h64(h)], TT[h % 2][:, hcs(h)], W1[:, h64(h)])
            k.cp(Ul[:, :], b[:, :], eng="act")
            b = cbank()
            for h in range(8):
                hp, hf = h // 2, h % 2
                k.mm(b[64 * hf:64 * hf + 64, hp * 128 + 64 * hf:hp * 128 + 64 * hf + 64], TA[:, h64(h)], Bh_tok[:, h64(h)])
            for hf in range(2):
                rs_ = slice(64 * hf, 64 * hf + 64)
                k.raw("dve", lambda e, McT=McT, b=b, rs_=rs_, hf=hf: e.tensor_tensor(
                    out=McT.h[rs_, :].rearrange("p (a c) -> p a c", c=128)[:, :, 64 * hf:64 * hf + 64],
                    in0=b.h[rs_, :].rearrange("p (a c) -> p a c", c=128)[:, :, 64 * hf:64 * hf + 64],
                    in1=irep2x4.h[rs_, :].rearrange("p (a c) -> p a c", c=64), op=ALU.add), [b, irep2x4], [McT], 256)
        for c4 in range(NCS):
            cs_ = CS[c4]
            ch = CH[c4]
            sl = SL[c4]
            do_out = OUT[c4]
            A_tok, Bh_tok, Kh_tok, V_tok = cs_["A_tok"], cs_["Bh_tok"], cs_["Kh_tok"], cs_["V_tok"]
            RbT, RkT, TA, Ul, McT = cs_.get("RbT"), cs_.get("RkT"), cs_["TA"], cs_["Ul"], cs_["McT"]
            S0 = Sst[ch % 2]
            S1 = Sst[(ch + 1) % 2]
            if do_out:
                b = cbank()
                for h in range(8):
                    hp, hf = h // 2, h % 2
                    k.mm(b[64 * hf:64 * hf + 64, hp * 128:(hp + 1) * 128], TA[:, h64(h)], RbT[h % 2][:, hcs(h)])
                for hp in range(4):
                    for hf in range(2):
                        rs_ = slice(64 * hf, 64 * hf + 64)
                        k.tt(GTm[hf][rs_, hp * 128:(hp + 1) * 128], b[rs_, hp * 128:(hp + 1) * 128], Rt[hp][rs_, sl], ALU.add)
                by = cbank()
                for h in range(8):
                    hp, hf = h // 2, h % 2
                    o = by[64 * hf:64 * hf + 64, hp * 128:(hp + 1) * 128]
                    k.mm(o, S0[:, hp * 64:(hp + 1) * 64], GTm[hf][:, hp * 128:(hp + 1) * 128], start=True, stop=False)
                    k.mm(o, Ul[:, h64(h)], RbT[hf][:, hcs(h)], start=False, stop=False)
                    k.mm(o, V_tok[:, h64(h)], RkT[hf][:, hcs(h)], start=False, stop=True)
                k.cp(yt[:, :], by[:, :], eng="act")
            bs = cbank()
            for h in range(8):
                hp, hf = h // 2, h % 2
                o = bs[64 * hf:64 * hf + 64, hp * 64:(hp + 1) * 64]
                k.mm(o, McT[:, hp * 128 + 64 * hf:hp * 128 + 64 * hf + 64], S0[:, hp * 64:(hp + 1) * 64], start=True, stop=False)
                k.mm(o, Bh_tok[:, h64(h)], Ul[:, h64(h)], start=False, stop=False)
                k.mm(o, Kh_tok[:, h64(h)], V_tok[:, h64(h)], start=False, stop=True)
            for hp in range(4):
                cs = slice(hp * 64, (hp + 1) * 64)
                k.act(S1[:, cs], bs[:, cs], AF.Copy, scale=gC[:, hp, c4:c4 + 1])
            if not do_out:
                continue
            yc, ysq = t2[0], t2[1]
            b = bank()
            k.mm(b[:, :], bones[:, :], yt[:, :])
            k.stt(yc[:, :], b[:, :], -1.0 / 64, yt[:, :], ALU.mult, ALU.add)
            k.act(ysq[:, :], yc[:, :], AF.Square)
            b = bank()
            k.mm(b[:, :], bones[:, :], ysq[:, :])
            rstd_from_ss(ysq[:, :], b[:, :], 1.0 / 64, 64 * 1e-5)
            k.tt(yc[:, :], yc[:, :], ysq[:, :], ALU.mult)
            for hp in range(4):
                cs = slice(hp * 128, (hp + 1) * 128)
                k.ts(yc[:, cs], yc[:, cs], pc[:, LW + hp:LW + hp + 1], pc[:, LB + hp:LB + hp + 1], ALU.mult, ALU.add)
                k.tt(yc[:, cs], yc[:, cs], bon[hp][:, sl], ALU.add)
                k.tt(yrw[:, ch - HALO_CH, hp, :], yc[:, cs], gq[hp][:, sl], ALU.mult)

    S.flush()
    print('[sched] modelled phase A (ns):', S.est)
    esA.close()

    esC = contextlib.ExitStack()

    def sbc(name, shape, dt=F32):
        return TB(esC.enter_context(nc.sbuf_tensor(name, shape, dt)), name)
    npb_cur[0] = 2
    w_out = sbc("w_out_bf", [128, 8, D], BF16)
    xbc = [sbc("xbc%d" % i, [128, D]) for i in range(4)]
    xrc = [sbc("xrc%d" % i, [128, D]) for i in range(3)]
    utc = [sbc("utc%d" % i, [128, D], BF16) for i in range(2)]
    stc = [sbc("stc%d" % i, [128, 4]) for i in range(4)]
    uTc = [sbc("uTc%d" % i, [128, 8, W_], BF16) for i in range(2)]
    tc_ = [[sbc("tc%d_%d" % (i, j), [128, W_]) for j in range(2)] for i in range(3)]
    qn = [[[sbc("qn%d_%d_%d" % (r_, i, hf), [128, W_], BF16) for hf in range(2)] for i in range(4)] for r_ in range(2)]
    kd = [[sbc("kd%d_%d" % (g, i), [128, W_], BF16) for i in range(2)] for g in range(2)]
    vat = [sbc("vat%d" % i, [128, W_], BF16) for i in range(2)]
    vtok = [sbc("vtok%d" % i, [128, 128], BF16) for i in range(4)]
    ycat = [sbc("ycat%d" % i, [128, 4, 128], BF16) for i in range(2)]
    pT = [sbc("pT%d" % i, [128, 512], BF16) for i in range(6)]
    scf = [sbc("scf%d" % i, [128, 512]) for i in range(3)]
    rdn = [sbc("rdn%d" % i, [128, 512]) for i in range(2)]
    abias = [[sbc("abias_s%d_%d" % (v_, g), [128, 1024]) for g in range(2)] for v_ in range(2)]
    abiasm = [[sbc("abiasm_s%d_%d" % (v_, g), [16, 512]) for g in range(2)] for v_ in range(2)]
    print('[mem] attention pass free bytes/partition:', nc.sbuf_bytes_remaining)
    g_ = None
    for kk_ in range(8):
        r_ = k.dma(w_out[:, kk_, :], w_out_d[kk_ * 128:(kk_ + 1) * 128, :], "wo", eng="pool", group=g_)
        g_ = g_ or r_
    for r_ in range(2):
        for hp_ in range(4):
            for hf_ in range(2):
                k.memset(qn[r_][hp_][hf_][:, :], 0.0)
    for g in range(2):
        k.dma(abias[1][g][:, :], bias_d[2, g, :, :], "abg%d" % g)
        k.dma(abiasm[1][g][:, :], biasm_d[2, g, :, :], "abmg%d" % g)
    xc = [0]
    tcn = [0]

    def projc(uTt, cc):
        b = bank()
        for dk in range(8):
            k.mm(b[:, 0:W_], w_in[:, dk, cc * 128:(cc + 1) * 128], uTt[:, dk, 0:W_], start=(dk == 0), stop=(dk == 7))
        return b

    def ttmp():
        t_ = tc_[tcn[0] % 3]
        tcn[0] += 1
        return t_
    for sbi in range(KV_CH // NCS, NSB):
        ch0 = sbi * NCS
        par = sbi % 2
        uTt = uTc[par]
        xts = []
        for c4 in range(NCS):
            ch = ch0 + c4
            xt_ = xbc[xc[0] % 4]
            k.dma(xt_[:, :], xl[ch * 128:(ch + 1) * 128, :], "xbc%d" % (xc[0] % 4))
            ut = utc[xc[0] % 2]
            norm_block(xt_, ut, nw, xc[0], stc)
            transpose_block(ut, uTt, c4 * 128)
            xts.append(xt_)
            xc[0] += 1
        qs = qn[par]
        for hp in range(4):
            b = projc(uTt, 14 + hp)
            ta, tb = ttmp()
            k.act(ta[:, :], b[:, 0:W_], AF.Square)
            b2 = bank()
            k.mm(b2[:, 0:W_], bones[:, :], ta[:, :])
            rstd_from_ss(tb[:, :], b2[:, 0:W_], 1.0 / 64, 1e-6)
            for hf in range(2):
                rs = slice(64 * hf, 64 * hf + 64)
                k.stt(qs[hp][hf][rs, :], b[rs, 0:W_], pc[rs, QW:QW + 1], tb[rs, :], ALU.mult, ALU.mult)
        for g in range(2):
            b = projc(uTt, 18 + g)
            ta, tb = ttmp()
            k.act(ta[:, :], b[:, 0:W_], AF.Square)
            b2 = bank()
            k.mm(b2[:, 0:W_], bones[:, :], ta[:, :])
            rstd_from_ss(tb[:, :], b2[:, 0:W_], 1.0 / 64, 1e-6)
            k.stt(kd[g][par][:, :], b[:, 0:W_], pc[:, KW:KW + 1], tb[:, :], ALU.mult, ALU.mult)
        b = projc(uTt, 20)
        k.cp(vat[par][:, :], b[:, 0:W_], eng="act")
        for c4 in range(NCS):
            ch = ch0 + c4
            sl = slice(c4 * 128, (c4 + 1) * 128)
            pt_ = tbank()
            k.tr(pt_[:, 0:128], vat[par][:, sl], ident[:, :])
            k.cp(vtok[ch % 4][:, :], pt_[:, 0:128])
            if ch < HALO_CH:
                continue
            var = 0 if ch == HALO_CH else (1 if ch == OWN0 else 2)
            prev_par = par if c4 > 0 else 1 - par
            psl = slice(((c4 - 1) % NCS) * 128, ((c4 - 1) % NCS) * 128 + 128)
            bo = bd = None
            yc_ = ycat[ch % 2]
            rden = rdn[ch % 2]
            for g in range(2):
                if var == 2:
                    ab, abm = abias[1][g], abiasm[1][g]
                else:
                    ab, abm = abias[0][g], abiasm[0][g]
                    k.dma(ab[:, :], bias_d[var, g, :, :], "abs%d" % g)
                    k.dma(abm[:, :], biasm_d[var, g, :, :], "abms%d" % g)
                pts = []
                for grp in range(3):
                    bsc = cbank()
                    nk = 16 if grp == 2 else 128
                    for hh in range(4):
                        h = 4 * g + hh
                        hp, hf = h // 2, h % 2
                        if grp == 0:
                            kl = kd[g][prev_par][:, psl]
                        elif grp == 1:
                            kl = kd[g][par][:, sl]
                        else:
                            kl = kmeta[:, g, :]
                        k.mm(bsc[0:nk, hh * 128:(hh + 1) * 128], kl, qs[hp][hf][:, sl])
                    bsrc = abm[:, :] if grp == 2 else ab[:, grp * 512:(grp + 1) * 512]
                    sc_f = scf[grp]
                    k.stt(sc_f[0:nk, :], bsc[0:nk, :], 0.125, bsrc, ALU.mult, ALU.add)
                    pt = pT[g * 3 + grp]
                    k.act(pt[0:nk, :], sc_f[0:nk, :], AF.Exp)
                    pts.append((pt, nk))
                if bo is None:
                    bo = bank()
                    bd = bank()
                vprev = vtok[(ch - 1) % 4]
                vcur = vtok[ch % 4]
                for hh in range(4):
                    h = 4 * g + hh
                    hp, hf = h // 2, h % 2
                    o = bo[64 * hf:64 * hf + 64, hp * 128:(hp + 1) * 128]
                    od = bd[64 * hf:64 * hf + 64, hp * 128:(hp + 1) * 128]
                    cs = slice(hh * 128, (hh + 1) * 128)
                    vs = [vprev[:, g * 64:(g + 1) * 64], vcur[:, g * 64:(g + 1) * 64], vmeta[:, g * 64:(g + 1) * 64]]
                    for grp in range(3):
                        pt, nk = pts[grp]
                        k.mm(o, vs[grp], pt[0:nk, cs], start=(grp == 0), stop=(grp == 2))
                    for grp in range(3):
                        pt, nk = pts[grp]
                        k.mm(od, ones64[0:nk, :], pt[0:nk, cs], start=(grp == 0), stop=(grp == 2))
            for hp in range(4):
                cs = slice(hp * 128, (hp + 1) * 128)
                k.ts(rden[:, cs], bd[:, cs], pc2[:, ESK + hp:ESK + hp + 1], None, ALU.add)
            k.recip(rden[:, :], rden[:, :])
            for hp in range(4):
                k.tt(yc_[:, hp, :], bo[:, hp * 128:(hp + 1) * 128], rden[:, hp * 128:(hp + 1) * 128], ALU.mult)
            ci = ch - HALO_CH
            xr = xrc[ci % 3]
            for half in range(2):
                b = cbank()
                for kc in range(8):
                    lhs = yrw[:, ci, kc, :] if kc < 4 else yc_[:, kc - 4, :]
                    k.mm(b[:, :], lhs, w_out[:, kc, half * 512:(half + 1) * 512], start=(kc == 0), stop=(kc == 7))
                k.tt(xr[:, half * 512:(half + 1) * 512], b[:, :], xts[c4][:, half * 512:(half + 1) * 512], ALU.add)
            k.dma(hmid_d[ci * 128:(ci + 1) * 128, :], xr[:, :], "xrc%d" % (ci % 3))
    S.flush()
    print('[sched] modelled attention pass (ns):', S.est)
    esC.close()
    esP.close()
    if trunc:
        with nc.Block() as block:
            S.emit(block)
        es.close()
        return nc

    def sbb(name, shape, dt=F32):
        return TB(es.enter_context(nc.sbuf_tensor(name, shape, dt)), name)

    npb_cur[0] = NFB
    wu = [sbb("wu%d" % i, [128, 8, 256], BF16) for i in range(NFC)]
    halo_sb = sbb("halo_sb", [128, 44, 2])
    w_dn = sbb("w_dn_bf", [128, NFC, D], BF16)
    cw = sbb("cw", [128, 44 * 3])
    cb = sbb("cb", [128, 44])
    u2T = sbb("u2T", [128, 8, 514], BF16)
    u2c = sbb("u2c", [128, 8, 2], BF16)
    actT = sbb("actT", [128, NFC, 512], BF16)
    hb = [sbb("hb%d" % i, [128, D]) for i in range(2)]
    u2tok = sbb("u2tok", [128, D], BF16)
    st2 = [sbb("st2_%d" % i, [128, 4]) for i in range(4)]
    c0 = [sbb("c0_%d" % i, [128, 512]) for i in range(4)]
    wukey = [TB(None, "wukey%d" % i) for i in range(3)]
    for fc in range(NFC):
        g_ = None
        for kk_ in range(8):
            r_ = k.dma(wu[fc][:, kk_, :], w_up_d[fc, kk_ * 128:(kk_ + 1) * 128, :], "wu%d" % (fc % 3), eng="pool", group=g_)
            g_ = g_ or r_
        g_.writes.append(wukey[fc % 3])
    g_ = None
    for fc in range(NFC):
        r_ = k.dma(w_dn[:, fc, :], w_down_d[fc * 128:(fc + 1) * 128, :], "wd", eng="pool", group=g_)
        g_ = g_ or r_
    g_ = k.dma(nw[:, :], n2_d[:, :], "c2")
    k.dma(cw[:, :], cw_d[:, :], "c2", group=g_)
    k.dma(cb[:, :], cb_d[:, :], "c2", group=g_)

    print('[mem] phase B free bytes/partition:', nc.sbuf_bytes_remaining)
    nidx = [0]

    def norm2_block(ci, col0, src_cols=None):
        h_ = hb[nidx[0] % 2]
        k.dma(h_[:, :], hmid_d[ci * 128:(ci + 1) * 128, :], "hb%d" % (nidx[0] % 2))
        norm_block(h_, u2tok, nw, nidx[0], st2)
        transpose_block(u2tok, u2T, col0, src_cols)
        nidx[0] += 1

    norm2_block(0, 0, (126, 128))
    for qi in range(4):
        if qi > 0:
            k.raw("dve", lambda e: e.tensor_copy(out=u2T.h[:, :, 0:2], in_=u2c.h[:, :, :]), [u2c], [u2T], 16)
        for c4 in range(4):
            norm2_block(1 + qi * 4 + c4, 2 + c4 * 128)
        k.raw("dve", lambda e: e.tensor_copy(out=u2c.h[:, :, :], in_=u2T.h[:, :, 512:514]), [u2T], [u2c], 16)
        hq = True
        for fc in range(NFC):
            pg = [bank() if hq else None, bank()]
            pv = [bank() if hq else None, bank()]
            for (pb_, col, wo_) in ((pg, fc, 0), (pv, NFC + fc, 128)):
                if hq:
                    for dk in range(8):
                        k.mm(pb_[0][:, 510:512], wu[fc][:, dk, wo_:wo_ + 128], u2T[:, dk, 0:2], start=(dk == 0), stop=(dk == 7))
                for dk in range(8):
                    k.mm(pb_[1][:, :], wu[fc][:, dk, wo_:wo_ + 128], u2T[:, dk, 2:514], start=(dk == 0), stop=(dk == 7))
            res = []
            for idx, (pb_, col) in enumerate(((pg, fc), (pv, NFC + fc))):
                cA = c0[idx * 2]
                cB = c0[idx * 2 + 1]
                w0_, w1_, w2_ = (cw[:, col * 3 + j:col * 3 + j + 1] for j in range(3))
                if hq:
                    h1, h2 = pb_[0][:, 511:512], pb_[0][:, 510:512]
                else:
                    h1, h2 = halo_sb[:, col, 1:2], halo_sb[:, col, 0:2]
                k.act(cA[:, :], pb_[1][:, :], AF.Identity, bias=cb[:, col:col + 1], scale=w0_)
                k.stt(cB[:, 1:512], pb_[1][:, 0:511], w1_, cA[:, 1:512], ALU.mult, ALU.add)
                k.stt(cB[:, 0:1], h1, w1_, cA[:, 0:1], ALU.mult, ALU.add)
                k.stt(cA[:, 2:512], pb_[1][:, 0:510], w2_, cB[:, 2:512], ALU.mult, ALU.add)
                k.stt(cA[:, 0:2], h2, w2_, cB[:, 0:2], ALU.mult, ALU.add)
                if qi < 3 and not hq:
                    k.cp(halo_sb[:, col, :], pb_[1][:, 510:512])
                res.append(cA)
            k.act(c0[1][:, :], res[0][:, :], AF.Silu)
            k.tt(actT[:, fc, :], c0[1][:, :], res[1][:, :], ALU.mult)
        for c4 in range(4):
            ci = 1 + qi * 4 + c4
            h_ = hb[nidx[0] % 2]
            hname = "hb%d" % (nidx[0] % 2)
            k.dma(h_[:, :], hmid_d[ci * 128:(ci + 1) * 128, :], hname)
            nidx[0] += 1
            for half in range(2):
                b = bank()
                for fc in range(NFC):
                    k.mm(b[:, :], actT[:, fc, c4 * 128:(c4 + 1) * 128], w_dn[:, fc, half * 512:(half + 1) * 512],
                         start=(fc == 0), stop=(fc == NFC - 1))
                k.tt(h_[:, half * 512:(half + 1) * 512], b[:, :], h_[:, half * 512:(half + 1) * 512], ALU.add)
            r0 = (qi * 4 + c4) * 128
            k.dma(out_d[r0:r0 + 128, :], h_[:, :], hname)

    S.flush()
    print("[sched] modelled time (ns): phase-sum", getattr(S, "est", None))
    with nc.Block() as block:
        S.emit(block)
    es.close()
    return nc


def _t5_bucket(d):
    d = np.maximum(d, 0)
    df = np.maximum(d, 16).astype(np.float32)
    large = 16 + (np.log(df / np.float32(16)) / np.float32(np.log(128 / 16)) * np.float32(16)).astype(np.int32)
    large = np.minimum(large, 31)
    return np.where(d < 16, d, large)


_PROG = {}


def kernel(x, meta_tokens, rel_bias, norm1_w, w_in, shift_mu, decay_w0, decay_up, aaa_a0, aaa_up, gate_up,
           k_k, k_a, r_k, lnx_w, lnx_b, q_norm_w, k_norm_w, sinks, w_out, norm2_w, w_up, conv_w, conv_b, w_down):
    f = np.float32
    x = np.asarray(x, f)
    meta = np.asarray(meta_tokens, f)
    rel_bias = np.asarray(rel_bias, f)
    W = np.asarray(w_in, f)[0]
    cols = []
    for base in (0, 512, 1024):
        cols += list(range(base, base + 512))
    cols += list(range(1536, 1664))
    cols += list(range(1664, 1792))
    cols += list(range(1792, 2304))
    kbase = 2304
    cols += list(range(kbase, kbase + 64)) * 2
    cols += list(range(kbase + 64, kbase + 128)) * 2
    cols += list(range(2432, 2560))
    w_in_p = np.ascontiguousarray(W[:, cols])
    mu = np.asarray(shift_mu, f)[0]
    mu_p = mu[cols[:14 * 128]]

    def colmaj(v):
        return np.ascontiguousarray(np.asarray(v, f).reshape(-1, 128).T)
    pcols = np.zeros((128, 64), f)
    pcols[:, 0:14] = colmaj(mu_p)
    pcols[:, 14:18] = colmaj(k_k[0])
    pcols[:, 18:22] = colmaj(k_a[0])
    pcols[:, 22:26] = colmaj(decay_w0[0])
    pcols[:, 26:30] = colmaj(aaa_a0[0])
    pcols[:, 30:34] = colmaj(np.asarray(r_k, f)[0].reshape(-1))
    pcols[:, 34:38] = colmaj(lnx_w[0])
    pcols[:, 38:42] = colmaj(lnx_b[0])
    pcols[:, 42] = np.tile(np.asarray(q_norm_w, f)[0], 2)
    pcols[:, 43] = np.tile(np.asarray(k_norm_w, f)[0], 2)
    lut = np.concatenate([np.asarray(decay_up, f)[0], np.asarray(aaa_up, f)[0]], 0)
    gup = np.asarray(gate_up, f)[0]
    n1 = np.ascontiguousarray(np.broadcast_to(np.asarray(norm1_w, f)[0][None], (128, D)))
    n2 = np.ascontiguousarray(np.broadcast_to(np.asarray(norm2_w, f)[0][None], (128, D)))
    cwv = np.asarray(conv_w, f)[0]
    convw = np.zeros((128, 44 * 3), f)
    for j in range(3):
        convw[:, j::3] = colmaj(cwv[j])
    convb = colmaj(conv_b[0])
    consts = np.zeros((128, 448), f)
    consts[:, 0:128] = np.eye(128, dtype=f)
    bo = np.zeros((128, 128), f)
    bo[:64, :64] = 1
    bo[64:, 64:] = 1
    consts[:, 128:256] = bo
    consts[:, 384:448] = np.concatenate([np.eye(64, dtype=f)] * 2, 0)
    ii = np.arange(128)
    su = (ii[:, None] < ii[None, :]).astype(f)
    sl_ = (ii[:, None] > ii[None, :]).astype(f)
    ui = (ii[:, None] <= ii[None, :]).astype(f)
    masks = np.concatenate([np.tile(su, (1, 4)), np.tile(sl_, (1, 4)), np.tile(ui, (1, 4))], 1)
    sk = np.asarray(sinks, f)[0]
    sinkx = np.zeros((128, 4), f)
    for hp in range(4):
        sinkx[:64, hp] = sk[2 * hp]
        sinkx[64:, hp] = sk[2 * hp + 1]
    xmeta = np.zeros((128, D), f)
    xmeta[:16] = meta
    NEG = f(-30000.0)

    def bias_tables(q):
        ab = np.zeros((3, 2, 128, 1024), f)
        abm = np.zeros((3, 2, 16, 512), f)
        kk = ii[:, None]
        qq = ii[None, :]
        d_prev = qq + 128 - kk
        d_cur = qq - kk
        for var in range(3):
            for g in range(2):
                for hh in range(4):
                    h = 4 * g + hh
                    bp = np.where((d_prev < 128), rel_bias[_t5_bucket(d_prev), h], NEG)
                    bc = np.where((d_cur >= 0), rel_bias[_t5_bucket(d_cur), h], NEG)
                    bm = np.full((16, 128), rel_bias[31, h], f)
                    if q == 0:
                        if var == 0:
                            bp = np.full((128, 128), NEG, f)
                            bc = np.full((128, 128), NEG, f)
                            mq = qq - 112
                            mk = np.arange(16)[:, None]
                            dm = mq - mk
                            bm = np.where(dm >= 0, rel_bias[_t5_bucket(dm), h], NEG)
                            bm = np.where(mq < 0, f(0.0), bm).astype(f)
                        elif var == 1:
                            bp = np.full((128, 128), NEG, f)
                            dm = (16 + qq) - np.arange(16)[:, None]
                            bm = rel_bias[_t5_bucket(dm), h].astype(f)
                    ab[var, g, :, hh * 128:(hh + 1) * 128] = bp
                    ab[var, g, :, 512 + hh * 128:512 + (hh + 1) * 128] = bc
                    abm[var, g, :, hh * 128:(hh + 1) * 128] = bm
        return ab, abm

    if "nc" not in _PROG:
        _PROG["nc"] = build_program()
    nc = _PROG["nc"]
    shared = {
        "xmeta": xmeta, "w_in": w_in_p, "w_out": np.ascontiguousarray(np.asarray(w_out, f)[0]),
        "w_up": np.ascontiguousarray(np.concatenate([np.asarray(w_up, f)[0][:, :DFF].reshape(D, NFC, 128),
                                                         np.asarray(w_up, f)[0][:, DFF:].reshape(D, NFC, 128)], axis=2).transpose(1, 0, 2)), "w_down": np.ascontiguousarray(np.asarray(w_down, f)[0]),
        "lut": np.ascontiguousarray(lut), "gup": np.ascontiguousarray(gup), "pcols": pcols, "n1w": n1, "n2w": n2,
        "convw": convw, "convb": convb, "consts": consts, "masks": np.ascontiguousarray(masks), "sinkx": sinkx,
    }
    tabs = {q: bias_tables(q) for q in range(4)}
    in_maps = []
    for c in range(8):
        b, q = c // 4, c % 4
        xloc = np.zeros((NTOK, D), f)
        nreal = 2048 * (q + 1)
        off = NTOK - 16 - nreal
        xloc[off:off + 16] = meta
        xloc[off + 16:] = x[b, :nreal]
        m = dict(shared)
        m["xl"] = xloc
        m["abias"], m["abiasm"] = tabs[q]
        in_maps.append(m)
    if _PROG.get("maps_only"):
        return in_maps
    res = run_bass_kernel_spmd(nc, in_maps, core_ids=list(range(8)))
    out = np.zeros((2, 8192, D), f)
    for c in range(8):
        b, q = c // 4, c % 4
        out[b, 2048 * q:2048 * (q + 1)] = np.asarray(res.results[c]["out"], f)
    return out
```

```python
import contextlib
import numpy as np
import concourse.bass as bass
import concourse.mybir as mybir
from concourse.bass_utils import run_bass_kernel_spmd

F32 = mybir.dt.float32
BF16 = mybir.dt.bfloat16
ALU = mybir.AluOpType
AF = mybir.ActivationFunctionType

D = 1024
NCS = 2
SBW = NCS * 128
NCH = 68
NSB = NCH // NCS
NTOK = NCH * 128
HALO_CH = 51
KV_CH = 50
OWN0 = 52
NPROJ = 21
DFF = 2816
NFC = 22
CW = 0.6065306597126334


class TB:
    def __init__(self, h, name):
        self.h = h
        self.name = name
        self.last_w = None
        self.reads = {}

    def __getitem__(self, idx):
        return V(self.h[idx], self)


class V:
    def __init__(self, ap, buf):
        self.ap = ap
        self.buf = buf


class Op:
    __slots__ = ("eng", "fns", "reads", "writes", "dma", "cost", "preds", "succs", "idx", "prio", "npred", "fin", "pos", "func", "dr", "why", "tag", "alt", "dr2")

    def __init__(self, eng, fn, reads, writes, dma, cost, func=None):
        self.eng = eng
        self.fns = [fn]
        self.reads = reads
        self.writes = writes
        self.dma = dma
        self.cost = cost
        self.preds = set()
        self.succs = []
        self.func = func
        self.alt = None


import os as _osx
XLAT = float(_osx.environ.get('KXLAT', '600'))


class Sched:
    def __init__(self, nc, es):
        self.nc = nc
        self.es = es
        self.engs = ["pe", "act", "dve", "pool", "sp"]
        self.sem = {}
        self.cnt = {}
        self.known = {e: {} for e in self.engs}
        self.prog = {e: [] for e in self.engs}
        for e in self.engs:
            self.sem[e] = es.enter_context(nc.semaphore("s_" + e))
            self.cnt[e] = 0
        self.ops = []
        self.open_mm = None
        self.final = []

    def dma_sem(self, name):
        key = "d_" + name
        if key not in self.sem:
            self.sem[key] = self.es.enter_context(self.nc.semaphore(key))
            self.cnt[key] = 0
        return key

    def op(self, eng, fn, reads, writes, dma=None, cost=100.0, append_to=None, func=None):
        if append_to is not None:
            o = append_to
            o.fns.append(fn)
            o.cost += cost
            for b in reads:
                if b not in o.reads:
                    o.reads.append(b)
            for b in writes:
                if b not in o.writes:
                    o.writes.append(b)
        else:
            o = Op(eng, fn, list(reads), list(writes), dma, cost, func)
            o.idx = len(self.ops)
            self.ops.append(o)
        return o

    def _build_deps(self, ops):
        for o in ops:
            for b in o.reads:
                if b.last_w is not None and b.last_w is not o:
                    o.preds.add(b.last_w)
            for b in o.writes:
                if b.last_w is not None and b.last_w is not o:
                    o.preds.add(b.last_w)
                for r in b.reads.values():
                    if r is not o:
                        o.preds.add(r)
            for b in o.reads:
                b.reads[id(o)] = o
            for b in o.writes:
                b.last_w = o
                b.reads = {}

    def flush(self, barrier=True):
        import heapq
        ops = self.ops
        self.ops = []
        self.open_mm = None
        if not ops:
            return
        self._build_deps(ops)
        inset = set(id(o) for o in ops)
        for o in ops:
            o.preds = set(p for p in o.preds if id(p) in inset)
            o.npred = len(o.preds)
            for p in o.preds:
                p.succs.append(o)
        for o in reversed(ops):
            m = 0.0
            for s in o.succs:
                if s.prio > m:
                    m = s.prio
            o.prio = m + o.cost + (XLAT if o.succs else 0.0)
        ready = {e: [] for e in self.engs}
        def push_ready(o):
            if o.alt is None:
                dr = 0.0
                for p in o.preds:
                    t = p.fin + (0.0 if p.eng == o.eng else XLAT)
                    if t > dr:
                        dr = t
                o.dr = dr
                ready[o.eng].append(o)
            else:
                o.dr2 = {}
                for e2 in o.alt:
                    dr = 0.0
                    for p in o.preds:
                        t = p.fin + (0.0 if p.eng == e2 else XLAT)
                        if t > dr:
                            dr = t
                    o.dr2[e2] = dr
                    ready[e2].append(o)
        for o in ops:
            if o.npred == 0:
                push_ready(o)
        free = {e: 0.0 for e in self.engs}
        lastfunc = {"act": None}
        order = {e: [] for e in self.engs}
        nleft = len(ops)
        while nleft:
            best = None
            for e in self.engs:
                rl = ready[e]
                if not rl:
                    continue
                fe = free[e]
                mn = min((o.dr if o.alt is None else o.dr2[e]) for o in rl)
                if mn < fe:
                    mn = fe
                pick = None
                for o in rl:
                    d_ = o.dr if o.alt is None else o.dr2[e]
                    st_ = d_ if d_ > fe else fe
                    if st_ <= mn + 100.0:
                        pen = 0.0
                        if e == "act" and o.func is not None and lastfunc["act"] is not None and o.func != lastfunc["act"]:
                            pen = 1300.0
                        sc = o.prio - pen
                        if pick is None or sc > pick[2]:
                            pick = (st_, o, sc)
                if best is None or pick[0] < best[0]:
                    best = (pick[0], pick[1], e)
            st_, o, e = best
            ready[e].remove(o)
            if o.alt is not None:
                for e2 in o.alt:
                    if e2 != e:
                        ready[e2].remove(o)
                o.eng = e
                o.fns = [o.alt[e]]
                o.dr = o.dr2[e]
            if o.dr >= free[e]:
                bp = None
                for p in o.preds:
                    t = p.fin + (0.0 if p.eng == e else XLAT)
                    if bp is None or t > bp[0]:
                        bp = (t, p)
                o.why = ("dep", bp[1] if bp else None)
            else:
                o.why = ("eng", order[e][-1] if order[e] else None)
            issue = o.cost
            if o.dma is not None:
                o.fin = st_ + o.cost
                free[e] = st_ + 60.0 * len(o.fns)
            else:
                o.fin = st_ + issue
                free[e] = o.fin
            if e == "act" and o.func is not None:
                lastfunc["act"] = o.func
            order[e].append(o)
            nleft -= 1
            for s in o.succs:
                s.npred -= 1
                if s.npred == 0:
                    push_ready(s)
        self.est = max(free.values())
        self.busy = {e: sum(o.cost for o in order[e]) for e in self.engs}
        import os as _os
        if _os.environ.get("KCRIT"):
            last = max(ops, key=lambda o: o.fin)
            cnt_ = {}
            o = last
            n_ = 0
            while o is not None and n_ < 200000:
                kind, p = o.why
                key_ = (kind, o.eng, (o.writes[0].name if o.writes else "-")[:6])
                cnt_[key_] = cnt_.get(key_, 0.0) + o.cost + (XLAT if kind == "dep" and p is not None and p.eng != o.eng else 0.0)
                o = p
                n_ += 1
            for kk__, vv in sorted(cnt_.items(), key=lambda kv: -kv[1])[:40]:
                print("[crit]", kk__, int(vv))
        print('[sched] region est', int(self.est), 'busy', {e: int(v) for e, v in self.busy.items()}, 'nops', {e: len(order[e]) for e in self.engs})
        for e in self.engs:
            for o in order[e]:
                key = o.dma if o.dma is not None else e
                step = 16 * len(o.fns) if o.dma is not None else 1
                self.cnt[key] += step
                o.pos = (key, self.cnt[key])
        for e in self.engs:
            for o in order[e]:
                need = {}
                for p in o.preds:
                    k_, v_ = p.pos
                    if k_ == "pe" and e == "pe":
                        continue
                    if need.get(k_, 0) < v_:
                        need[k_] = v_
                for k_, v_ in need.items():
                    if self.known[e].get(k_, 0) >= v_:
                        continue
                    self.prog[e].append(("wait", k_, v_))
                    self.known[e][k_] = v_
                n = len(o.fns)
                for i_, fn in enumerate(o.fns):
                    if o.dma is not None:
                        self.prog[e].append(("op", fn, o.pos[0], 16))
                    else:
                        self.prog[e].append(("op", fn, o.pos[0] if i_ == n - 1 else None, 1))
        for o in ops:
            o.preds = None
            o.succs = None
        if barrier:
            for e in self.engs:
                for key, v in self.cnt.items():
                    if v > 0 and key != e and self.known[e].get(key, 0) < v:
                        self.prog[e].append(("wait", key, v))
                        self.known[e][key] = v
        for o in ops:
            for b in o.reads:
                b.reads = {}
            for b in o.writes:
                b.last_w = None
                b.reads = {}

    def emit(self, block):
        sems = self.sem

        def run(e, name):
            for it in self.prog[name]:
                if it[0] == "wait":
                    e.wait_ge(sems[it[1]], it[2])
                else:
                    ins = it[1](e)
                    if it[2] is not None:
                        ins.then_inc(sems[it[2]], it[3])
            if name == "sp":
                for k_, v_ in self.cnt.items():
                    if k_.startswith("d_") and v_ > 0:
                        e.wait_ge(sems[k_], v_)

        @block.tensor
        def _(e):
            run(e, "pe")

        @block.scalar
        def _(e):
            run(e, "act")

        @block.vector
        def _(e):
            run(e, "dve")

        @block.gpsimd
        def _(e):
            run(e, "pool")

        @block.sync
        def _(e):
            run(e, "sp")


class Sub:
    def __init__(self, tb, off, width):
        self.tb = tb
        self.off = off
        self.width = width
        self.name = tb.name

    def __getitem__(self, idx):
        rows, cols = idx
        a = 0 if cols.start is None else cols.start
        b_ = self.width if cols.stop is None else cols.stop
        return self.tb[rows, self.off + a:self.off + b_]


def bufs(*vs):
    out = []
    for v in vs:
        if isinstance(v, V) and v.buf not in out:
            out.append(v.buf)
    return out


def A(x):
    return x.ap if isinstance(x, V) else x


def fsz(v):
    n = 1
    for s in v.ap.shape[1:]:
        n *= int(s)
    return n


class K:
    def __init__(self, s):
        self.s = s

    def mm(self, out, lhsT, rhs, start=True, stop=True):
        n = fsz(out)
        c = max(130.0, 75.0 + n * 0.42)
        if lhsT.ap.dtype == F32:
            c *= 4
        fn = lambda e: e.matmul(out.ap, lhsT.ap, rhs.ap, start=start, stop=stop)
        if start:
            o = self.s.op("pe", fn, bufs(lhsT, rhs), bufs(out), cost=c)
            self.s.open_mm = None if stop else o
        else:
            o = self.s.op("pe", fn, bufs(lhsT, rhs), bufs(out), cost=c, append_to=self.s.open_mm)
            if stop:
                self.s.open_mm = None

    def tr(self, out, in_, ident):
        self.s.op("pe", lambda e: e.transpose(out.ap, in_.ap, ident.ap), bufs(in_, ident), bufs(out), cost=130.0)

    def act(self, out, in_, func, bias=0.0, scale=1.0, accum=None, eng="act"):
        kw = {}
        if accum is not None:
            kw["accum_out"] = accum.ap
        fam = {AF.Exp: "exp", AF.Tanh: "exp", AF.Sigmoid: "sig", AF.Sqrt: "sqrt", AF.Silu: "silu"}.get(func)
        self.s.op("act", lambda e: e.activation(out=out.ap, in_=in_.ap, func=func, bias=A(bias), scale=A(scale), **kw),
                  bufs(in_, bias, scale), bufs(out, accum), cost=200.0 + fsz(out) * 0.75, func=fam)

    def raw(self, eng, fn, reads, writes, n):
        self.s.op(eng, fn, reads, writes, cost=160.0 + n * 0.9)

    def tt(self, out, a, b, op, eng="dve"):
        self.s.op(eng, lambda e: e.tensor_tensor(out=out.ap, in0=a.ap, in1=b.ap, op=op), bufs(a, b), bufs(out),
                  cost=120.0 + fsz(out) * 0.85)

    def ts(self, out, a, s1, s2, op0, op1=None, eng="dve"):
        c = 120.0 + fsz(out) * 0.6
        if op1 is None:
            self.s.op(eng, lambda e: e.tensor_scalar(out=out.ap, in0=a.ap, scalar1=A(s1), scalar2=None, op0=op0),
                      bufs(a, s1), bufs(out), cost=c)
        else:
            self.s.op(eng, lambda e: e.tensor_scalar(out=out.ap, in0=a.ap, scalar1=A(s1), scalar2=A(s2), op0=op0, op1=op1),
                      bufs(a, s1, s2), bufs(out), cost=c)

    def stt(self, out, a, sc, b, op0, op1):
        self.s.op("dve", lambda e: e.scalar_tensor_tensor(out=out.ap, in0=a.ap, scalar=A(sc), in1=b.ap, op0=op0, op1=op1),
                  bufs(a, sc, b), bufs(out), cost=120.0 + fsz(out) * 0.85)

    def cp(self, out, in_, eng="dve"):
        if eng == "act":
            o = self.s.op("act", lambda e: e.activation(out=out.ap, in_=in_.ap, func=AF.Copy), bufs(in_), bufs(out),
                          cost=200.0 + fsz(out) * 0.75)
            o.alt = {"act": o.fns[0], "dve": (lambda e: e.tensor_copy(out=out.ap, in_=in_.ap))}
        else:
            self.s.op(eng, lambda e: e.tensor_copy(out=out.ap, in_=in_.ap), bufs(in_), bufs(out), cost=120.0 + fsz(out) * 0.6)

    def recip(self, out, in_):
        self.s.op("dve", lambda e: e.reciprocal(out=out.ap, in_=in_.ap), bufs(in_), bufs(out), cost=150.0 + fsz(out) * 5.5)

    def scan(self, out, ones, x):
        self.s.op("dve", lambda e: e.tensor_tensor_scan(out=out.ap, data0=ones.ap, data1=x.ap, initial=0.0,
                                                        op0=ALU.mult, op1=ALU.add), bufs(ones, x), bufs(out), cost=150.0 + fsz(out) * 2.1)

    def memset(self, out, val, eng="dve"):
        self.s.op(eng, lambda e: e.memset(out.ap, val), [], bufs(out), cost=100.0 + fsz(out) * 0.3)

    def raw_dma(self, out_tb, in_tb, fn, name, eng, nbytes):
        key = self.s.dma_sem(name)
        return self.s.op(eng, fn, [in_tb], [out_tb], dma=key, cost=2200.0 + nbytes / 60.0)

    def dma(self, out, in_, name, eng="sp", group=None):
        key = self.s.dma_sem(name)
        nbytes = 128 * fsz(out) * 4
        c = 2200.0 + nbytes / 60.0
        fn = lambda e: e.dma_start(out=out.ap, in_=in_.ap)
        if group is not None:
            return self.s.op(eng, fn, bufs(in_), bufs(out), dma=key, cost=nbytes / 120.0, append_to=group)
        return self.s.op(eng, fn, bufs(in_), bufs(out), dma=key, cost=c)


def build_program(debug=False, trunc=None):
    import os
    trunc = trunc or os.environ.get('KTRUNC', '')
    kstep = int(os.environ.get('KSTEP', '99'))
    kvar = os.environ.get('KVAR', '')
    nc = bass.Bass("TRN2", target_bir_lowering=False)
    es = contextlib.ExitStack()
    S = Sched(nc, es)
    k = K(S)
    W_ = SBW

    def dram(name, shape, dt=F32, kind="ExternalInput"):
        return TB(nc.dram_tensor(name, shape, dt, kind=kind).ap(), name)

    def sb(name, shape, dt=F32):
        return TB(es.enter_context(nc.sbuf_tensor(name, shape, dt)), name)

    def ps(name, shape, dt=F32):
        return TB(es.enter_context(nc.psum_tensor(name, shape, dt)), name)

    xl = dram("xl", [NTOK, D])
    xmeta = dram("xmeta", [128, D])
    w_in_d = dram("w_in", [D, NPROJ * 128])
    w_out_d = dram("w_out", [D, D])
    w_up_d = dram("w_up", [NFC, D, 256])
    w_down_d = dram("w_down", [DFF, D])
    lut_d = dram("lut", [128, 512])
    gup_d = dram("gup", [128, 512])
    pc_d = dram("pcols", [128, 64])
    n1_d = dram("n1w", [128, D])
    n2_d = dram("n2w", [128, D])
    cw_d = dram("convw", [128, 44 * 3])
    cb_d = dram("convb", [128, 44])
    cst_d = dram("consts", [128, 448])
    msk_d = dram("masks", [128, 3 * 512])
    bias_d = dram("abias", [3, 2, 128, 2 * 512])
    biasm_d = dram("abiasm", [3, 2, 16, 512])
    snk_d = dram("sinkx", [128, 4])
    out_d = dram("out", [2048, D], kind="ExternalOutput")
    hmid_d = dram("hmid", [17 * 128, D], kind="Internal")

    pc = sb("pc", [128, 64])
    pc2 = sb("pc2", [128, 32])
    nw = sb("nw_s", [128, D])
    ident = sb("ident", [128, 128], BF16)
    mhalf = sb("mhalf", [128, 1])
    bones = sb("bones", [128, 128])
    irep2 = sb("irep2", [128, 64])

    MU, KK_, KA, W0, A0, RK, LW, LB, QW, KW = 0, 14, 18, 22, 26, 30, 34, 38, 42, 43
    OMU, OMKA, ESK = 0, 14, 18

    NFB = 6
    pbanks = [ps("pb%d" % i, [128, 512]) for i in range(NFB)]
    pbh = [ps("pbf%d" % i, [128, 1024], BF16) for i in range(2)]
    rot = [0]

    NPB = 2

    npb_cur = [NPB]

    def bank():
        b = pbanks[rot[0] % npb_cur[0]]
        rot[0] += 1
        return b
    crot = [0]

    def cbank():
        b = pbanks[NPB + crot[0] % (NFB - NPB)]
        crot[0] += 1
        return b
    trot = [0]

    def tbank():
        b = pbh[trot[0] % 2]
        trot[0] += 1
        return b

    esP = contextlib.ExitStack()

    def sbp(name, shape, dt=F32):
        return TB(esP.enter_context(nc.sbuf_tensor(name, shape, dt)), name)
    w_in = sbp("w_in_bf", [128, 8, NPROJ * 128], BF16)
    yrw = sbp("yrw", [128, 17, 4, 128], BF16)
    ones64 = sbp("ones64", [128, 64], BF16)
    kmeta = sbp("kmeta", [128, 2, 16], BF16)
    vmeta = sbp("vmeta", [16, 128], BF16)
    esA = contextlib.ExitStack()

    def sba(name, shape, dt=F32):
        return TB(esA.enter_context(nc.sbuf_tensor(name, shape, dt)), name)

    lut = sba("lut_bf", [128, 512], BF16)
    gup = sba("gup_bf", [128, 512], BF16)
    irep8 = sba("irep8", [128, 512], BF16)
    msu = sba("msu", [128, 512], BF16)
    msl = sba("msl", [128, 512], BF16)
    mui = sba("mui", [128, 512], BF16)
    ones_t = sba("ones_t", [128, 128])
    rst = sba("rst", [128, 2 * W_])
    carry = sba("carry", [128, 14])
    Sst = [sba("S%d" % i, [128, 256], BF16) for i in range(2)]
    irep2x4 = sba("irep2x4", [128, 256])
    xb = [sba("xb%d" % i, [128, D]) for i in range(2)]
    utok = sba("utok", [128, D], BF16)
    st = [sba("st%d" % i, [128, 4]) for i in range(4)]
    uT = sba("uT", [128, 8, W_ + 1], BF16)
    tmp = [sba("tmp%d" % i, [128, W_]) for i in range(2)]
    def pp_set(alloc, sfx):
        wide = {nm: [alloc("pp%s%d%s" % (nm, g, sfx), [128, 2 * W_]) for g in range(2)] for nm in ("R", "K", "V")}
        lst = []
        for nm in ("R", "K", "V"):
            for hp in range(4):
                lst.append(Sub(wide[nm][hp // 2], (hp % 2) * W_, W_))
        lst.append(alloc("pp12" + sfx, [128, W_]))
        lst.append(alloc("pp13" + sfx, [128, W_]))
        return lst, wide
    pp, ppw = pp_set(sba, "")
    lo_bf = sba("lo_bf", [128, W_], BF16)
    snk = sba("snk", [128, 4])

    t1 = [sba("t1_%d" % i, [128, W_]) for i in range(3)]
    TW = {nm: sba(nm, [128, 2 * W_]) for nm in ("lwW", "alW", "kkW", "kfW", "clW", "g1W", "xA", "xB", "xC")}
    def wide_heads(alloc, nm, sfx):
        w_ = [alloc("%sW%d%s" % (nm, g, sfx), [128, 2 * W_], BF16) for g in range(2)]
        return [Sub(w_[hp // 2], (hp % 2) * W_, W_) for hp in range(4)], w_

    def sb_set(alloc, sfx):
        d_ = {}
        for nm in ("At", "Bt", "Kt", "vbf"):
            d_[nm], d_[nm + "W"] = wide_heads(alloc, nm, sfx)
        d_["gC"] = alloc("gC" + sfx, [128, 4, NCS])
        return d_
    SBS = [sb_set(sba, "")]
    CS = []
    for ci_ in range(NCS):
        s_ = {}
        for nm in ("A_tok", "Bh_tok", "Kh_tok", "V_tok", "TA", "W1", "Ul"):
            s_[nm] = sba("%s_%d" % (nm, ci_), [128, 512], BF16)
        for nm in ("N", "L", "TT"):
            s_[nm] = [[sba("%s%d_%d_%d" % (nm, ci_, i, j), [128, 512], BF16) for j in range(2)] for i in range(2)]
        for nm in ("AkT",):
            s_[nm] = [sba("%s%d_%d" % (nm, ci_, j), [128, 512], BF16) for j in range(2)]
        s_["McT"] = sba("McT_%d" % ci_, [128, 512], BF16)
        CS.append(s_)

    esA1 = contextlib.ExitStack()

    def sba1(name, shape, dt=F32):
        return TB(esA1.enter_context(nc.sbuf_tensor(name, shape, dt)), name)
    SBS.append(sb_set(sba1, "_b"))
    pp_b, ppw_b = pp_set(sba1, "_b")
    PPS = [(pp, ppw), (pp_b, ppw_b)]
    print('[mem] phase A1 free bytes/partition:', nc.sbuf_bytes_remaining)
    g_ = k.dma(lut[:, :], lut_d[:, :], "wc", eng="pool")
    k.dma(gup[:, :], gup_d[:, :], "wc", eng="pool", group=g_)
    import os as _os2
    nosplit = bool(_os2.environ.get("KNOSPLIT"))
    for kk_ in range(8):
        if nosplit:
            k.dma(w_in[:, kk_, :], w_in_d[kk_ * 128:(kk_ + 1) * 128, :], "wc", eng="pool", group=g_)
        else:
            k.dma(w_in[:, kk_, 512:1664], w_in_d[kk_ * 128:(kk_ + 1) * 128, 512:1664], "wc", eng="pool", group=g_)
    g_ = k.dma(pc[:, :], pc_d[:, :], "c0")
    k.dma(nw[:, :], n1_d[:, :], "c0", group=g_)
    k.dma(bones[:, :], cst_d[:, 128:256], "c0", group=g_)
    k.dma(irep2[:, :], cst_d[:, 384:448], "c0", group=g_)
    for i in range(4):
        k.dma(irep2x4[:, i * 64:(i + 1) * 64], cst_d[:, 384:448], "c0", group=g_)
    k.dma(snk[:, :], snk_d[:, :], "c0", group=g_)
    g_ = k.dma(ident[:, :], cst_d[:, 0:128], "c1", eng="pool")
    for i in range(4):
        k.dma(irep8[:, i * 128:(i + 1) * 128], cst_d[:, 0:128], "c1", eng="pool", group=g_)
    k.dma(msu[:, :], msk_d[:, 0:512], "c1", eng="pool", group=g_)
    k.dma(msl[:, :], msk_d[:, 512:1024], "c1", eng="pool", group=g_)
    k.dma(mui[:, :], msk_d[:, 1024:1536], "c1", eng="pool", group=g_)
    g_ = None
    for kk_ in range(8):
        if nosplit:
            break
        r_ = k.dma(w_in[:, kk_, 0:512], w_in_d[kk_ * 128:(kk_ + 1) * 128, 0:512], "wc_b", eng="pool", group=g_)
        g_ = g_ or r_
        k.dma(w_in[:, kk_, 1664:NPROJ * 128], w_in_d[kk_ * 128:(kk_ + 1) * 128, 1664:NPROJ * 128], "wc_b", eng="pool", group=g_)
    k.memset(ones_t[:, :], 1.0)
    k.memset(mhalf[:, :], -0.5)
    k.memset(rst[:, :], 1.0)
    for i in range(2 * W_ // 128):
        k.memset(rst[:, i * 128:i * 128 + 1], 0.0)
    k.memset(ones64[:, :], 1.0)
    k.raw("dve", lambda e: e.memset(uT.h[:, :, :], 0.0), [], [uT], 2056)
    k.memset(Sst[0][:, :], 0.0)
    for cs__ in CS:
        k.memset(cs__["McT"][:, :], 0.0)
    k.ts(pc2[:, OMU:OMU + 14], pc[:, MU:MU + 14], -1.0, 1.0, ALU.mult, ALU.add)
    k.ts(pc2[:, OMKA:OMKA + 4], pc[:, KA:KA + 4], -1.0, 1.0, ALU.mult, ALU.add)
    k.ts(pc2[:, 22:26], pc[:, W0:W0 + 4], 0.5, None, ALU.mult)
    k.ts(pc2[:, 26:30], pc[:, A0:A0 + 4], 0.5, None, ALU.mult)
    k.act(pc2[:, ESK:ESK + 4], snk[:, :], AF.Exp)
    def rstd_from_ss(out, ss_view, inv_n, eps):
        k.ts(out, ss_view, inv_n, eps, ALU.mult, ALU.add)
        k.act(out, out, AF.Sqrt)
        k.recip(out, out)

    def norm_block(x_t, ut, wt, si, stl):
        s_ = stl[si % 4]
        k.act(ut[:, :], x_t[:, :], AF.Square, accum=s_[:, 0:1])
        k.ts(s_[:, 2:3], s_[:, 0:1], 1.0 / D, 1e-6, ALU.mult, ALU.add)
        k.tt(s_[:, 1:2], s_[:, 2:3], mhalf[:, :], ALU.pow, eng="pool")
        k.stt(ut[:, :], x_t[:, :], s_[:, 1:2], wt[:, :], ALU.mult, ALU.mult)

    def transpose_block(ut, dst, col0, src_cols=None):
        for hlf in range(2):
            pt_ = tbank()
            for d4 in range(4):
                dk = hlf * 4 + d4
                k.tr(pt_[:, d4 * 128:(d4 + 1) * 128], ut[:, dk * 128:(dk + 1) * 128], ident[:, :])
            if src_cols is None:
                k.raw("act", lambda e, pt_=pt_, hlf=hlf: e.activation(
                    out=dst.h[:, hlf * 4:(hlf + 1) * 4, col0:col0 + 128],
                    in_=pt_.h[:, 0:512].rearrange("p (k t) -> p k t", k=4), func=AF.Copy), [pt_], [dst], 512)
            else:
                a, b_ = src_cols
                k.raw("act", lambda e, pt_=pt_, hlf=hlf, a=a, b_=b_: e.activation(
                    out=dst.h[:, hlf * 4:(hlf + 1) * 4, col0:col0 + (b_ - a)],
                    in_=pt_.h[:, 0:512].rearrange("p (k t) -> p k t", k=4)[:, :, a:b_], func=AF.Copy), [pt_], [dst], 16)

    def project(cc, n=W_, shift=False):
        b = bank()
        lo_ = 0 if shift else 1
        n_ = n + 1 if shift else n
        for dk in range(8):
            k.mm(b[:, 0:n_], w_in[:, dk, cc * 128:(cc + 1) * 128], uT[:, dk, lo_:lo_ + n_], start=(dk == 0), stop=(dk == 7))
        return b

    def kv_norm_k(psb, dst, n):
        k.act(t1[0][:, 0:n], psb[:, 0:n], AF.Square)
        b2 = bank()
        k.mm(b2[:, 0:n], bones[:, :], t1[0][:, 0:n])
        rstd_from_ss(t1[1][:, 0:n], b2[:, 0:n], 1.0 / 64, 1e-6)
        k.stt(dst, psb[:, 0:n], pc[:, KW:KW + 1], t1[1][:, 0:n], ALU.mult, ALU.mult)

    k.dma(xb[0][:, :], xmeta[:, :], "xb0")
    norm_block(xb[0], utok, nw, 0, st)
    transpose_block(utok, uT, 1)
    for g in range(2):
        b = project(18 + g, 128)
        kv_norm_k(b, t1[2][:, 0:128], 128)
        k.cp(kmeta[:, g, :], t1[2][:, 0:16])
    b = project(20, 128)
    k.cp(lo_bf[:, 0:128], b[:, 0:128], eng="act")
    pt_ = tbank()
    k.tr(pt_[:, 0:128], lo_bf[:, 0:128], ident[:, :])
    k.cp(vmeta[:, :], pt_[0:16, 0:128])

    xcnt = [1]
    nsb_run = NSB
    if trunc.startswith('nsb'):
        nsb_run = int(trunc[3:])
    if trunc == 'meta':
        nsb_run = 0
    for sbi in range(nsb_run):
        ch0 = sbi * NCS
        full = ch0 >= 48
        par = sbi % 2
        if ch0 == 48:
            S.flush()
            print('[sched] modelled phase A1 (ns):', S.est)
            esA1.close()

            def sba2(name, shape, dt=F32):
                return TB(esA.enter_context(nc.sbuf_tensor(name, shape, dt)), name)
            sg_bf = sba2("sg_bf", [128, W_], BF16)
            gq = [sba2("gq%d" % i, [128, W_], BF16) for i in range(4)]
            Rt, RtW = wide_heads(sba2, "Rt", "")
            bon, bonW = wide_heads(sba2, "bon", "")
            t2 = [sba2("t2_%d" % i, [128, 512]) for i in range(2)]
            GTm = [sba2("GT%d" % i, [128, 512], BF16) for i in range(2)]
            for i in range(2):
                k.memset(GTm[i][:, :], 0.0)
            yt = sba2("yt", [128, 512])
            for cs__ in CS:
                for nm in ("RbT", "RkT"):
                    cs__[nm] = [sba2("%s%s_%d" % (nm, cs__["McT"].name[-2:], j), [128, 512], BF16) for j in range(2)]
            print('[mem] phase A2 free bytes/partition:', nc.sbuf_bytes_remaining)
            SBS = SBS[:1]
            PPS = PPS[:1]
            print('[mem] phase A2 free after temps:', nc.sbuf_bytes_remaining)
        sbs_ = SBS[sbi % len(SBS)]
        At, Bt, Kt, vbf, gC = (sbs_[n_] for n_ in ("At", "Bt", "Kt", "vbf", "gC"))
        AtW, BtW, KtW, vbfW = (sbs_[n_ + "W"] for n_ in ("At", "Bt", "Kt", "vbf"))
        pp, ppw = PPS[sbi % len(PPS)]
        k.raw("dve", lambda e: e.tensor_copy(out=uT.h[:, :, 0:1], in_=uT.h[:, :, W_:W_ + 1]), [uT], [uT], 8)
        for c4 in range(NCS):
            ch = ch0 + c4
            xt_ = xb[xcnt[0] % 2]
            k.dma(xt_[:, :], xl[ch * 128:(ch + 1) * 128, :], "xb%d" % (xcnt[0] % 2))
            norm_block(xt_, utok, nw, xcnt[0], st)
            transpose_block(utok, uT, 1 + c4 * 128)
            xcnt[0] += 1
        if kstep < 1:
            continue
        chunks = list(range(0, 14)) if full else list(range(4, 13))
        for cc in chunks:
            b = project(cc, shift=True)
            tm = tmp[cc % len(tmp)]
            k.act(tm[:, 0:W_], b[:, 0:W_], AF.Copy, scale=pc[:, MU + cc:MU + cc + 1])
            k.stt(pp[cc][:, :], b[:, 1:W_ + 1], pc2[:, OMU + cc:OMU + cc + 1], tm[:, 0:W_], ALU.mult, ALU.add)
        if kstep < 2:
            continue
        k.act(lo_bf[0:64, :], pp[12][0:64, :], AF.Tanh)
        k.cp(lo_bf[64:128, :], pp[12][64:128, :])
        if full:
            k.act(t1[0][:, :], pp[13][:, :], AF.Tanh, scale=0.5)
            k.ts(sg_bf[:, :], t1[0][:, :], 0.5, 0.5, ALU.mult, ALU.add)
        lwW, alW, kkW, kfW, clW, g1W, xA, xB, xC = (TW[n_] for n_ in ("lwW", "alW", "kkW", "kfW", "clW", "g1W", "xA", "xB", "xC"))
        for g2 in range(2):
            kW, vW, rW = ppw["K"][g2], ppw["V"][g2], ppw["R"][g2]
            for i2 in range(2):
                hp = 2 * g2 + i2
                cs = slice(i2 * W_, (i2 + 1) * W_)
                b = bank()
                k.mm(b[:, 0:W_], lut[0:64, hp * 128:(hp + 1) * 128], lo_bf[0:64, :])
                k.act(lwW[:, cs], b[:, 0:W_], AF.Tanh, bias=pc2[:, 22 + hp:23 + hp], scale=0.5)
                b = bank()
                k.mm(b[:, 0:W_], lut[64:128, hp * 128:(hp + 1) * 128], lo_bf[64:128, :])
                k.act(alW[:, cs], b[:, 0:W_], AF.Tanh, bias=pc2[:, 26 + hp:27 + hp], scale=0.5)
                if full:
                    b = bank()
                    k.mm(b[:, 0:W_], gup[:, hp * 128:(hp + 1) * 128], sg_bf[:, :])
                    k.cp(gq[hp][:, :], b[:, 0:W_], eng="act")
                k.ts(xA[:, cs], kW[:, cs], pc[:, KK_ + hp:KK_ + hp + 1], None, ALU.mult)
            k.ts(lwW[:, :], lwW[:, :], 0.5, 0.5, ALU.mult, ALU.add)
            k.ts(alW[:, :], alW[:, :], 0.5, 0.5, ALU.mult, ALU.add)
            k.act(xB[:, :], xA[:, :], AF.Square)
            b = bank()
            k.mm(b[:, :], bones[:, :], xB[:, :])
            k.act(xB[:, :], b[:, :], AF.Sqrt)
            k.ts(xB[:, :], xB[:, :], 1e-12, None, ALU.max)
            k.recip(xB[:, :], xB[:, :])
            k.tt(kkW[:, :], xA[:, :], xB[:, :], ALU.mult)
            k.tt(xC[:, :], kkW[:, :], alW[:, :], ALU.mult)
            for i2 in range(2):
                hp = 2 * g2 + i2
                cs = slice(i2 * W_, (i2 + 1) * W_)
                k.ts(xA[:, cs], alW[:, cs], pc[:, KA + hp:KA + hp + 1], pc2[:, OMKA + hp:OMKA + hp + 1], ALU.mult, ALU.add)
            k.tt(kfW[:, :], kW[:, :], xA[:, :], ALU.mult)
            k.scan(clW[:, :], rst[:, :], lwW[:, :])
            k.tt(xA[:, :], clW[:, :], lwW[:, :], ALU.subtract)
            k.act(g1W[:, :], clW[:, :], AF.Exp, scale=-CW)
            k.act(xB[:, :], xA[:, :], AF.Exp, scale=-CW)
            k.act(alW[:, :], clW[:, :], AF.Exp, scale=CW)
            k.stt(AtW[g2][:, :], kkW[:, :], -1.0, xB[:, :], ALU.mult, ALU.mult)
            k.tt(BtW[g2][:, :], xC[:, :], alW[:, :], ALU.mult)
            k.tt(KtW[g2][:, :], kfW[:, :], alW[:, :], ALU.mult)
            k.raw("dve", lambda e, gC=gC, g1W=g1W, g2=g2: e.tensor_copy(
                out=gC.h[:, 2 * g2:2 * g2 + 2, :],
                in_=g1W.h[:, :].rearrange("p (a c) -> p a c", c=128)[:, :, 127:128].rearrange("p (i c4) o -> p i (c4 o)", i=2)),
                [g1W], [gC], 4)
            k.cp(vbfW[g2][:, :], vW[:, :], eng="act")
            if full:
                k.tt(RtW[g2][:, :], rW[:, :], g1W[:, :], ALU.mult)
                for i2 in range(2):
                    hp = 2 * g2 + i2
                    cs = slice(i2 * W_, (i2 + 1) * W_)
                    k.stt(xA[:, cs], rW[:, cs], pc[:, RK + hp:RK + hp + 1], kfW[:, cs], ALU.mult, ALU.mult)
                b = bank()
                k.mm(b[:, :], bones[:, :], xA[:, :])
                k.tt(bonW[g2][:, :], b[:, :], vW[:, :], ALU.mult)
        if kstep < 3:
            continue
        CH = [ch0 + c4 for c4 in range(NCS)]
        SL = [slice(c4 * 128, (c4 + 1) * 128) for c4 in range(NCS)]
        OUT = [ch >= HALO_CH for ch in CH]
        for c4 in range(NCS):
            cs_ = CS[c4]
            for src, dst in ((At, cs_["A_tok"]), (Bt, cs_["Bh_tok"]), (Kt, cs_["Kh_tok"]), (vbf, cs_["V_tok"])):
                pt_ = tbank()
                for hp in range(4):
                    k.tr(pt_[:, hp * 128:(hp + 1) * 128], src[hp][:, SL[c4]], ident[:, :])
                k.cp(dst[:, :], pt_[:, 0:512], eng="act")
        if kstep < 4:
            continue
        cur = [0] * NCS
        for c4 in range(NCS):
            cs_ = CS[c4]
            sl = SL[c4]
            pn = [cbank(), cbank()]
            pl = [cbank(), cbank()]
            for h in range(8):
                hp, hf = h // 2, h % 2
                rows = slice(64 * hf, 64 * hf + 64)
                cs = slice(hp * 128, hp * 128 + 128)
                k.mm(pn[hf][:, cs], Bt[hp][rows, sl], At[hp][rows, sl])
                k.mm(pl[hf][:, cs], At[hp][rows, sl], Bt[hp][rows, sl])
            for j in range(2):
                k.tt(cs_["N"][0][j][:, :], pn[j][:, :], msu[:, :], ALU.mult)
                k.tt(cs_["L"][0][j][:, :], pl[j][:, :], msl[:, :], ALU.mult)
                k.tt(cs_["TT"][0][j][:, :], cs_["N"][0][j][:, :], irep8[:, :], ALU.add)
            pa = [cbank(), cbank()]
            for h in range(8):
                hp, hf = h // 2, h % 2
                rows = slice(64 * hf, 64 * hf + 64)
                cs = slice(hp * 128, hp * 128 + 128)
                k.mm(pa[hf][:, cs], Kt[hp][rows, sl], At[hp][rows, sl])
            for j in range(2):
                k.tt(cs_["AkT"][j][:, :], pa[j][:, :], msu[:, :], ALU.mult)
            if OUT[c4]:
                for (lt, dst) in ((Bt, cs_["RbT"]), (Kt, cs_["RkT"])):
                    pr_ = [cbank(), cbank()]
                    for h in range(8):
                        hp, hf = h // 2, h % 2
                        rows = slice(64 * hf, 64 * hf + 64)
                        cs = slice(hp * 128, hp * 128 + 128)
                        k.mm(pr_[hf][:, cs], lt[hp][rows, sl], Rt[hp][rows, sl])
                    for j in range(2):
                        k.tt(dst[j][:, :], pr_[j][:, :], mui[:, :], ALU.mult)
        if kstep < 5:
            continue
        for lev in range(6):
            for c4 in range(NCS):
                cs_ = CS[c4]
                c_ = cur[c4]
                n_ = 1 - c_
                Nb_, Lb_, TT_ = cs_["N"], cs_["L"], cs_["TT"]
                for j in range(2):
                    b = cbank()
                    for i in range(4):
                        cs = slice(i * 128, (i + 1) * 128)
                        k.mm(b[:, cs], Nb_[c_][j][:, cs], Lb_[c_][j][:, cs])
                    k.cp(Lb_[n_][j][:, :], b[:, :], eng="act")
                if lev < 5:
                    for j in range(2):
                        b = cbank()
                        for i in range(4):
                            cs = slice(i * 128, (i + 1) * 128)
                            k.mm(b[:, cs], Lb_[c_][j][:, cs], Nb_[c_][j][:, cs])
                        k.cp(Nb_[n_][j][:, :], b[:, :], eng="act")
                for j in range(2):
                    b = cbank()
                    for i in range(4):
                        cs = slice(i * 128, (i + 1) * 128)
                        k.mm(b[:, cs], Lb_[n_][j][:, cs], TT_[c_][j][:, cs])
                    k.tt(TT_[n_][j][:, :], b[:, :], TT_[c_][j][:, :], ALU.add)
                cur[c4] = n_
        if kstep < 6:
            continue

        def hcs(h):
            return slice((h // 2) * 128, (h // 2) * 128 + 128)

        def h64(h):
            return slice(h * 64, (h + 1) * 64)
        for c4 in range(NCS):
            cs_ = CS[c4]
            TT = cs_["TT"][cur[c4]]
            A_tok, Bh_tok, V_tok = cs_["A_tok"], cs_["Bh_tok"], cs_["V_tok"]
            AkT, TA, W1, Ul, McT = cs_["AkT"], cs_["TA"], cs_["W1"], cs_["Ul"], cs_["McT"]
            b = cbank()
            for h in range(8):
                k.mm(b[:, h64(h)], TT[h % 2][:, hcs(h)], A_tok[:, h64(h)])
            k.cp(TA[:, :], b[:, :], eng="act")
            b = cbank()
            for h in range(8):
                k.mm(b[:, h64(h)], AkT[h % 2][:, hcs(h)], V_tok[:, h64(h)])
            k.cp(W1[:, :], b[:, :], eng="act")
            b = cbank()
            for h in range(8):
                k.mm(b[:, h64(h)], TT[h % 2][:, hcs(h)], W1[:, h64(h)])
            k.cp(Ul[:, :], b[:, :], eng="act")
            b = cbank()
            for h in range(8):
                hp, hf = h // 2, h % 2
                k.mm(b[64 * hf:64 * hf + 64, hp * 128 + 64 * hf:hp * 128 + 64 * hf + 64], TA[:, h64(h)], Bh_tok[:, h64(h)])
            for hf in range(2):
                rs_ = slice(64 * hf, 64 * hf + 64)
                k.raw("dve", lambda e, McT=McT, b=b, rs_=rs_, hf=hf: e.tensor_tensor(
                    out=McT.h[rs_, :].rearrange("p (a c) -> p a c", c=128)[:, :, 64 * hf:64 * hf + 64],
                    in0=b.h[rs_, :].rearrange("p (a c) -> p a c", c=128)[:, :, 64 * hf:64 * hf + 64],
                    in1=irep2x4.h[rs_, :].rearrange("p (a c) -> p a c", c=64), op=ALU.add), [b, irep2x4], [McT], 256)
        for c4 in range(NCS):
            cs_ = CS[c4]
            ch = CH[c4]
            sl = SL[c4]
            do_out = OUT[c4]
            A_tok, Bh_tok, Kh_tok, V_tok = cs_["A_tok"], cs_["Bh_tok"], cs_["Kh_tok"], cs_["V_tok"]
            RbT, RkT, TA, Ul, McT = cs_.get("RbT"), cs_.get("RkT"), cs_["TA"], cs_["Ul"], cs_["McT"]
            S0 = Sst[ch % 2]
            S1 = Sst[(ch + 1) % 2]
            if do_out:
                b = cbank()
                for h in range(8):
                    hp, hf = h // 2, h % 2
                    k.mm(b[64 * hf:64 * hf + 64, hp * 128:(hp + 1) * 128], TA[:, h64(h)], RbT[h % 2][:, hcs(h)])
                for hp in range(4):
                    for hf in range(2):
                        rs_ = slice(64 * hf, 64 * hf + 64)
                        k.tt(GTm[hf][rs_, hp * 128:(hp + 1) * 128], b[rs_, hp * 128:(hp + 1) * 128], Rt[hp][rs_, sl], ALU.add)
                by = cbank()
                for h in range(8):
                    hp, hf = h // 2, h % 2
                    o = by[64 * hf:64 * hf + 64, hp * 128:(hp + 1) * 128]
                    k.mm(o, S0[:, hp * 64:(hp + 1) * 64], GTm[hf][:, hp * 128:(hp + 1) * 128], start=True, stop=False)
                    k.mm(o, Ul[:, h64(h)], RbT[hf][:, hcs(h)], start=False, stop=False)
                    k.mm(o, V_tok[:, h64(h)], RkT[hf][:, hcs(h)], start=False, stop=True)
                k.cp(yt[:, :], by[:, :], eng="act")
            bs = cbank()
            for h in range(8):
                hp, hf = h // 2, h % 2
                o = bs[64 * hf:64 * hf + 64, hp * 64:(hp + 1) * 64]
                k.mm(o, McT[:, hp * 128 + 64 * hf:hp * 128 + 64 * hf + 64], S0[:, hp * 64:(hp + 1) * 64], start=True, stop=False)
                k.mm(o, Bh_tok[:, h64(h)], Ul[:, h64(h)], start=False, stop=False)
                k.mm(o, Kh_tok[:, h64(h)], V_tok[:, h64(h)], start=False, stop=True)
            for hp in range(4):
                cs = slice(hp * 64, (hp + 1) * 64)
                k.act(S1[:, cs], bs[:, cs], AF.Copy, scale=gC[:, hp, c4:c4 + 1])
            if not do_out:
                continue
            yc, ysq = t2[0], t2[1]
            b = bank()
            k.mm(b[:, :], bones[:, :], yt[:, :])
            k.stt(yc[:, :], b[:, :], -1.0 / 64, yt[:, :], ALU.mult, ALU.add)
            k.act(ysq[:, :], yc[:, :], AF.Square)
            b = bank()
            k.mm(b[:, :], bones[:, :], ysq[:, :])
            rstd_from_ss(ysq[:, :], b[:, :], 1.0 / 64, 64 * 1e-5)
            k.tt(yc[:, :], yc[:, :], ysq[:, :], ALU.mult)
            for hp in range(4):
                cs = slice(hp * 128, (hp + 1) * 128)
                k.ts(yc[:, cs], yc[:, cs], pc[:, LW + hp:LW + hp + 1], pc[:, LB + hp:LB + hp + 1], ALU.mult, ALU.add)
                k.tt(yc[:, cs], yc[:, cs], bon[hp][:, sl], ALU.add)
                k.tt(yrw[:, ch - HALO_CH, hp, :], yc[:, cs], gq[hp][:, sl], ALU.mult)

    S.flush()
    print('[sched] modelled phase A (ns):', S.est)
    esA.close()

    esC = contextlib.ExitStack()

    def sbc(name, shape, dt=F32):
        return TB(esC.enter_context(nc.sbuf_tensor(name, shape, dt)), name)
    npb_cur[0] = 2
    w_out = sbc("w_out_bf", [128, 8, D], BF16)
    xbc = [sbc("xbc%d" % i, [128, D]) for i in range(4)]
    xrc = [sbc("xrc%d" % i, [128, D]) for i in range(3)]
    utc = [sbc("utc%d" % i, [128, D], BF16) for i in range(2)]
    stc = [sbc("stc%d" % i, [128, 4]) for i in range(4)]
    uTc = [sbc("uTc%d" % i, [128, 8, W_], BF16) for i in range(2)]
    tc_ = [[sbc("tc%d_%d" % (i, j), [128, W_]) for j in range(2)] for i in range(3)]
    qn = [[[sbc("qn%d_%d_%d" % (r_, i, hf), [128, W_], BF16) for hf in range(2)] for i in range(4)] for r_ in range(2)]
    kd = [[sbc("kd%d_%d" % (g, i), [128, W_], BF16) for i in range(2)] for g in range(2)]
    vat = [sbc("vat%d" % i, [128, W_], BF16) for i in range(2)]
    vtok = [sbc("vtok%d" % i, [128, 128], BF16) for i in range(4)]
    ycat = [sbc("ycat%d" % i, [128, 4, 128], BF16) for i in range(2)]
    pT = [sbc("pT%d" % i, [128, 512], BF16) for i in range(6)]
    scf = [sbc("scf%d" % i, [128, 512]) for i in range(3)]
    rdn = [sbc("rdn%d" % i, [128, 512]) for i in range(2)]
    abias = [[sbc("abias_s%d_%d" % (v_, g), [128, 1024]) for g in range(2)] for v_ in range(2)]
    abiasm = [[sbc("abiasm_s%d_%d" % (v_, g), [16, 512]) for g in range(2)] for v_ in range(2)]
    print('[mem] attention pass free bytes/partition:', nc.sbuf_bytes_remaining)
    g_ = None
    for kk_ in range(8):
        r_ = k.dma(w_out[:, kk_, :], w_out_d[kk_ * 128:(kk_ + 1) * 128, :], "wo", eng="pool", group=g_)
        g_ = g_ or r_
    for r_ in range(2):
        for hp_ in range(4):
            for hf_ in range(2):
                k.memset(qn[r_][hp_][hf_][:, :], 0.0)
    for g in range(2):
        k.dma(abias[1][g][:, :], bias_d[2, g, :, :], "abg%d" % g)
        k.dma(abiasm[1][g][:, :], biasm_d[2, g, :, :], "abmg%d" % g)
    xc = [0]
    tcn = [0]

    def projc(uTt, cc):
        b = bank()
        for dk in range(8):
            k.mm(b[:, 0:W_], w_in[:, dk, cc * 128:(cc + 1) * 128], uTt[:, dk, 0:W_], start=(dk == 0), stop=(dk == 7))
        return b

    def ttmp():
        t_ = tc_[tcn[0] % 3]
        tcn[0] += 1
        return t_
    for sbi in range(KV_CH // NCS, NSB):
        ch0 = sbi * NCS
        par = sbi % 2
        uTt = uTc[par]
        xts = []
        for c4 in range(NCS):
            ch = ch0 + c4
            xt_ = xbc[xc[0] % 4]
            k.dma(xt_[:, :], xl[ch * 128:(ch + 1) * 128, :], "xbc%d" % (xc[0] % 4))
            ut = utc[xc[0] % 2]
            norm_block(xt_, ut, nw, xc[0], stc)
            transpose_block(ut, uTt, c4 * 128)
            xts.append(xt_)
            xc[0] += 1
        qs = qn[par]
        for hp in range(4):
            b = projc(uTt, 14 + hp)
            ta, tb = ttmp()
            k.act(ta[:, :], b[:, 0:W_], AF.Square)
            b2 = bank()
            k.mm(b2[:, 0:W_], bones[:, :], ta[:, :])
            rstd_from_ss(tb[:, :], b2[:, 0:W_], 1.0 / 64, 1e-6)
            for hf in range(2):
                rs = slice(64 * hf, 64 * hf + 64)
                k.stt(qs[hp][hf][rs, :], b[rs, 0:W_], pc[rs, QW:QW + 1], tb[rs, :], ALU.mult, ALU.mult)
        for g in range(2):
            b = projc(uTt, 18 + g)
            ta, tb = ttmp()
            k.act(ta[:, :], b[:, 0:W_], AF.Square)
            b2 = bank()
            k.mm(b2[:, 0:W_], bones[:, :], ta[:, :])
            rstd_from_ss(tb[:, :], b2[:, 0:W_], 1.0 / 64, 1e-6)
            k.stt(kd[g][par][:, :], b[:, 0:W_], pc[:, KW:KW + 1], tb[:, :], ALU.mult, ALU.mult)
        b = projc(uTt, 20)
        k.cp(vat[par][:, :], b[:, 0:W_], eng="act")
        for c4 in range(NCS):
            ch = ch0 + c4
            sl = slice(c4 * 128, (c4 + 1) * 128)
            pt_ = tbank()
            k.tr(pt_[:, 0:128], vat[par][:, sl], ident[:, :])
            k.cp(vtok[ch % 4][:, :], pt_[:, 0:128])
            if ch < HALO_CH:
                continue
            var = 0 if ch == HALO_CH else (1 if ch == OWN0 else 2)
            prev_par = par if c4 > 0 else 1 - par
            psl = slice(((c4 - 1) % NCS) * 128, ((c4 - 1) % NCS) * 128 + 128)
            bo = bd = None
            yc_ = ycat[ch % 2]
            rden = rdn[ch % 2]
            for g in range(2):
                if var == 2:
                    ab, abm = abias[1][g], abiasm[1][g]
                else:
                    ab, abm = abias[0][g], abiasm[0][g]
                    k.dma(ab[:, :], bias_d[var, g, :, :], "abs%d" % g)
                    k.dma(abm[:, :], biasm_d[var, g, :, :], "abms%d" % g)
                pts = []
                for grp in range(3):
                    bsc = cbank()
                    nk = 16 if grp == 2 else 128
                    for hh in range(4):
                        h = 4 * g + hh
                        hp, hf = h // 2, h % 2
                        if grp == 0:
                            kl = kd[g][prev_par][:, psl]
                        elif grp == 1:
                            kl = kd[g][par][:, sl]
                        else:
                            kl = kmeta[:, g, :]
                        k.mm(bsc[0:nk, hh * 128:(hh + 1) * 128], kl, qs[hp][hf][:, sl])
                    bsrc = abm[:, :] if grp == 2 else ab[:, grp * 512:(grp + 1) * 512]
                    sc_f = scf[grp]
                    k.stt(sc_f[0:nk, :], bsc[0:nk, :], 0.125, bsrc, ALU.mult, ALU.add)
                    pt = pT[g * 3 + grp]
                    k.act(pt[0:nk, :], sc_f[0:nk, :], AF.Exp)
                    pts.append((pt, nk))
                if bo is None:
                    bo = bank()
                    bd = bank()
                vprev = vtok[(ch - 1) % 4]
                vcur = vtok[ch % 4]
                for hh in range(4):
                    h = 4 * g + hh
                    hp, hf = h // 2, h % 2
                    o = bo[64 * hf:64 * hf + 64, hp * 128:(hp + 1) * 128]
                    od = bd[64 * hf:64 * hf + 64, hp * 128:(hp + 1) * 128]
                    cs = slice(hh * 128, (hh + 1) * 128)
                    vs = [vprev[:, g * 64:(g + 1) * 64], vcur[:, g * 64:(g + 1) * 64], vmeta[:, g * 64:(g + 1) * 64]]
                    for grp in range(3):
                        pt, nk = pts[grp]
                        k.mm(o, vs[grp], pt[0:nk, cs], start=(grp == 0), stop=(grp == 2))
                    for grp in range(3):
                        pt, nk = pts[grp]
                        k.mm(od, ones64[0:nk, :], pt[0:nk, cs], start=(grp == 0), stop=(grp == 2))
            for hp in range(4):
                cs = slice(hp * 128, (hp + 1) * 128)
                k.ts(rden[:, cs], bd[:, cs], pc2[:, ESK + hp:ESK + hp + 1], None, ALU.add)
            k.recip(rden[:, :], rden[:, :])
            for hp in range(4):
                k.tt(yc_[:, hp, :], bo[:, hp * 128:(hp + 1) * 128], rden[:, hp * 128:(hp + 1) * 128], ALU.mult)
            ci = ch - HALO_CH
            xr = xrc[ci % 3]
            for half in range(2):
                b = cbank()
                for kc in range(8):
                    lhs = yrw[:, ci, kc, :] if kc < 4 else yc_[:, kc - 4, :]
                    k.mm(b[:, :], lhs, w_out[:, kc, half * 512:(half + 1) * 512], start=(kc == 0), stop=(kc == 7))
                k.tt(xr[:, half * 512:(half + 1) * 512], b[:, :], xts[c4][:, half * 512:(half + 1) * 512], ALU.add)
            k.dma(hmid_d[ci * 128:(ci + 1) * 128, :], xr[:, :], "xrc%d" % (ci % 3))
    S.flush()
    print('[sched] modelled attention pass (ns):', S.est)
    esC.close()
    esP.close()
    if trunc:
        with nc.Block() as block:
            S.emit(block)
        es.close()
        return nc

    def sbb(name, shape, dt=F32):
        return TB(es.enter_context(nc.sbuf_tensor(name, shape, dt)), name)

    npb_cur[0] = NFB
    wu = [sbb("wu%d" % i, [128, 8, 256], BF16) for i in range(NFC)]
    halo_sb = sbb("halo_sb", [128, 44, 2])
    w_dn = sbb("w_dn_bf", [128, NFC, D], BF16)
    cw = sbb("cw", [128, 44 * 3])
    cb = sbb("cb", [128, 44])
    u2T = sbb("u2T", [128, 8, 514], BF16)
    u2c = sbb("u2c", [128, 8, 2], BF16)
    actT = sbb("actT", [128, NFC, 512], BF16)
    hb = [sbb("hb%d" % i, [128, D]) for i in range(2)]
    u2tok = sbb("u2tok", [128, D], BF16)
    st2 = [sbb("st2_%d" % i, [128, 4]) for i in range(4)]
    c0 = [sbb("c0_%d" % i, [128, 512]) for i in range(4)]
    wukey = [TB(None, "wukey%d" % i) for i in range(3)]
    for fc in range(NFC):
        g_ = None
        for kk_ in range(8):
            r_ = k.dma(wu[fc][:, kk_, :], w_up_d[fc, kk_ * 128:(kk_ + 1) * 128, :], "wu%d" % (fc % 3), eng="pool", group=g_)
            g_ = g_ or r_
        g_.writes.append(wukey[fc % 3])
    g_ = None
    for fc in range(NFC):
        r_ = k.dma(w_dn[:, fc, :], w_down_d[fc * 128:(fc + 1) * 128, :], "wd", eng="pool", group=g_)
        g_ = g_ or r_
    g_ = k.dma(nw[:, :], n2_d[:, :], "c2")
    k.dma(cw[:, :], cw_d[:, :], "c2", group=g_)
    k.dma(cb[:, :], cb_d[:, :], "c2", group=g_)

    print('[mem] phase B free bytes/partition:', nc.sbuf_bytes_remaining)
    nidx = [0]

    def norm2_block(ci, col0, src_cols=None):
        h_ = hb[nidx[0] % 2]
        k.dma(h_[:, :], hmid_d[ci * 128:(ci + 1) * 128, :], "hb%d" % (nidx[0] % 2))
        norm_block(h_, u2tok, nw, nidx[0], st2)
        transpose_block(u2tok, u2T, col0, src_cols)
        nidx[0] += 1

    norm2_block(0, 0, (126, 128))
    for qi in range(4):
        if qi > 0:
            k.raw("dve", lambda e: e.tensor_copy(out=u2T.h[:, :, 0:2], in_=u2c.h[:, :, :]), [u2c], [u2T], 16)
        for c4 in range(4):
            norm2_block(1 + qi * 4 + c4, 2 + c4 * 128)
        k.raw("dve", lambda e: e.tensor_copy(out=u2c.h[:, :, :], in_=u2T.h[:, :, 512:514]), [u2T], [u2c], 16)
        hq = True
        for fc in range(NFC):
            pg = [bank() if hq else None, bank()]
            pv = [bank() if hq else None, bank()]
            for (pb_, col, wo_) in ((pg, fc, 0), (pv, NFC + fc, 128)):
                if hq:
                    for dk in range(8):
                        k.mm(pb_[0][:, 510:512], wu[fc][:, dk, wo_:wo_ + 128], u2T[:, dk, 0:2], start=(dk == 0), stop=(dk == 7))
                for dk in range(8):
                    k.mm(pb_[1][:, :], wu[fc][:, dk, wo_:wo_ + 128], u2T[:, dk, 2:514], start=(dk == 0), stop=(dk == 7))
            res = []
            for idx, (pb_, col) in enumerate(((pg, fc), (pv, NFC + fc))):
                cA = c0[idx * 2]
                cB = c0[idx * 2 + 1]
                w0_, w1_, w2_ = (cw[:, col * 3 + j:col * 3 + j + 1] for j in range(3))
                if hq:
                    h1, h2 = pb_[0][:, 511:512], pb_[0][:, 510:512]
                else:
                    h1, h2 = halo_sb[:, col, 1:2], halo_sb[:, col, 0:2]
                k.act(cA[:, :], pb_[1][:, :], AF.Identity, bias=cb[:, col:col + 1], scale=w0_)
                k.stt(cB[:, 1:512], pb_[1][:, 0:511], w1_, cA[:, 1:512], ALU.mult, ALU.add)
                k.stt(cB[:, 0:1], h1, w1_, cA[:, 0:1], ALU.mult, ALU.add)
                k.stt(cA[:, 2:512], pb_[1][:, 0:510], w2_, cB[:, 2:512], ALU.mult, ALU.add)
                k.stt(cA[:, 0:2], h2, w2_, cB[:, 0:2], ALU.mult, ALU.add)
                if qi < 3 and not hq:
                    k.cp(halo_sb[:, col, :], pb_[1][:, 510:512])
                res.append(cA)
            k.act(c0[1][:, :], res[0][:, :], AF.Silu)
            k.tt(actT[:, fc, :], c0[1][:, :], res[1][:, :], ALU.mult)
        for c4 in range(4):
            ci = 1 + qi * 4 + c4
            h_ = hb[nidx[0] % 2]
            hname = "hb%d" % (nidx[0] % 2)
            k.dma(h_[:, :], hmid_d[ci * 128:(ci + 1) * 128, :], hname)
            nidx[0] += 1
            for half in range(2):
                b = bank()
                for fc in range(NFC):
                    k.mm(b[:, :], actT[:, fc, c4 * 128:(c4 + 1) * 128], w_dn[:, fc, half * 512:(half + 1) * 512],
                         start=(fc == 0), stop=(fc == NFC - 1))
                k.tt(h_[:, half * 512:(half + 1) * 512], b[:, :], h_[:, half * 512:(half + 1) * 512], ALU.add)
            r0 = (qi * 4 + c4) * 128
            k.dma(out_d[r0:r0 + 128, :], h_[:, :], hname)

    S.flush()
    print("[sched] modelled time (ns): phase-sum", getattr(S, "est", None))
    with nc.Block() as block:
        S.emit(block)
    es.close()
    return nc


def _t5_bucket(d):
    d = np.maximum(d, 0)
    df = np.maximum(d, 16).astype(np.float32)
    large = 16 + (np.log(df / np.float32(16)) / np.float32(np.log(128 / 16)) * np.float32(16)).astype(np.int32)
    large = np.minimum(large, 31)
    return np.where(d < 16, d, large)


_PROG = {}


def kernel(x, meta_tokens, rel_bias, norm1_w, w_in, shift_mu, decay_w0, decay_up, aaa_a0, aaa_up, gate_up,
           k_k, k_a, r_k, lnx_w, lnx_b, q_norm_w, k_norm_w, sinks, w_out, norm2_w, w_up, conv_w, conv_b, w_down):
    f = np.float32
    x = np.asarray(x, f)
    meta = np.asarray(meta_tokens, f)
    rel_bias = np.asarray(rel_bias, f)
    W = np.asarray(w_in, f)[0]
    cols = []
    for base in (0, 512, 1024):
        cols += list(range(base, base + 512))
    cols += list(range(1536, 1664))
    cols += list(range(1664, 1792))
    cols += list(range(1792, 2304))
    kbase = 2304
    cols += list(range(kbase, kbase + 64)) * 2
    cols += list(range(kbase + 64, kbase + 128)) * 2
    cols += list(range(2432, 2560))
    w_in_p = np.ascontiguousarray(W[:, cols])
    mu = np.asarray(shift_mu, f)[0]
    mu_p = mu[cols[:14 * 128]]

    def colmaj(v):
        return np.ascontiguousarray(np.asarray(v, f).reshape(-1, 128).T)
    pcols = np.zeros((128, 64), f)
    pcols[:, 0:14] = colmaj(mu_p)
    pcols[:, 14:18] = colmaj(k_k[0])
    pcols[:, 18:22] = colmaj(k_a[0])
    pcols[:, 22:26] = colmaj(decay_w0[0])
    pcols[:, 26:30] = colmaj(aaa_a0[0])
    pcols[:, 30:34] = colmaj(np.asarray(r_k, f)[0].reshape(-1))
    pcols[:, 34:38] = colmaj(lnx_w[0])
    pcols[:, 38:42] = colmaj(lnx_b[0])
    pcols[:, 42] = np.tile(np.asarray(q_norm_w, f)[0], 2)
    pcols[:, 43] = np.tile(np.asarray(k_norm_w, f)[0], 2)
    lut = np.concatenate([np.asarray(decay_up, f)[0], np.asarray(aaa_up, f)[0]], 0)
    gup = np.asarray(gate_up, f)[0]
    n1 = np.ascontiguousarray(np.broadcast_to(np.asarray(norm1_w, f)[0][None], (128, D)))
    n2 = np.ascontiguousarray(np.broadcast_to(np.asarray(norm2_w, f)[0][None], (128, D)))
    cwv = np.asarray(conv_w, f)[0]
    convw = np.zeros((128, 44 * 3), f)
    for j in range(3):
        convw[:, j::3] = colmaj(cwv[j])
    convb = colmaj(conv_b[0])
    consts = np.zeros((128, 448), f)
    consts[:, 0:128] = np.eye(128, dtype=f)
    bo = np.zeros((128, 128), f)
    bo[:64, :64] = 1
    bo[64:, 64:] = 1
    consts[:, 128:256] = bo
    consts[:, 384:448] = np.concatenate([np.eye(64, dtype=f)] * 2, 0)
    ii = np.arange(128)
    su = (ii[:, None] < ii[None, :]).astype(f)
    sl_ = (ii[:, None] > ii[None, :]).astype(f)
    ui = (ii[:, None] <= ii[None, :]).astype(f)
    masks = np.concatenate([np.tile(su, (1, 4)), np.tile(sl_, (1, 4)), np.tile(ui, (1, 4))], 1)
    sk = np.asarray(sinks, f)[0]
    sinkx = np.zeros((128, 4), f)
    for hp in range(4):
        sinkx[:64, hp] = sk[2 * hp]
        sinkx[64:, hp] = sk[2 * hp + 1]
    xmeta = np.zeros((128, D), f)
    xmeta[:16] = meta
    NEG = f(-30000.0)

    def bias_tables(q):
        ab = np.zeros((3, 2, 128, 1024), f)
        abm = np.zeros((3, 2, 16, 512), f)
        kk = ii[:, None]
        qq = ii[None, :]
        d_prev = qq + 128 - kk
        d_cur = qq - kk
        for var in range(3):
            for g in range(2):
                for hh in range(4):
                    h = 4 * g + hh
                    bp = np.where((d_prev < 128), rel_bias[_t5_bucket(d_prev), h], NEG)
                    bc = np.where((d_cur >= 0), rel_bias[_t5_bucket(d_cur), h], NEG)
                    bm = np.full((16, 128), rel_bias[31, h], f)
                    if q == 0:
                        if var == 0:
                            bp = np.full((128, 128), NEG, f)
                            bc = np.full((128, 128), NEG, f)
                            mq = qq - 112
                            mk = np.arange(16)[:, None]
                            dm = mq - mk
                            bm = np.where(dm >= 0, rel_bias[_t5_bucket(dm), h], NEG)
                            bm = np.where(mq < 0, f(0.0), bm).astype(f)
                        elif var == 1:
                            bp = np.full((128, 128), NEG, f)
                            dm = (16 + qq) - np.arange(16)[:, None]
                            bm = rel_bias[_t5_bucket(dm), h].astype(f)
                    ab[var, g, :, hh * 128:(hh + 1) * 128] = bp
                    ab[var, g, :, 512 + hh * 128:512 + (hh + 1) * 128] = bc
                    abm[var, g, :, hh * 128:(hh + 1) * 128] = bm
        return ab, abm

    if "nc" not in _PROG:
        _PROG["nc"] = build_program()
    nc = _PROG["nc"]
    shared = {
        "xmeta": xmeta, "w_in": w_in_p, "w_out": np.ascontiguousarray(np.asarray(w_out, f)[0]),
        "w_up": np.ascontiguousarray(np.concatenate([np.asarray(w_up, f)[0][:, :DFF].reshape(D, NFC, 128),
                                                         np.asarray(w_up, f)[0][:, DFF:].reshape(D, NFC, 128)], axis=2).transpose(1, 0, 2)), "w_down": np.ascontiguousarray(np.asarray(w_down, f)[0]),
        "lut": np.ascontiguousarray(lut), "gup": np.ascontiguousarray(gup), "pcols": pcols, "n1w": n1, "n2w": n2,
        "convw": convw, "convb": convb, "consts": consts, "masks": np.ascontiguousarray(masks), "sinkx": sinkx,
    }
    tabs = {q: bias_tables(q) for q in range(4)}
    in_maps = []
    for c in range(8):
        b, q = c // 4, c % 4
        xloc = np.zeros((NTOK, D), f)
        nreal = 2048 * (q + 1)
        off = NTOK - 16 - nreal
        xloc[off:off + 16] = meta
        xloc[off + 16:] = x[b, :nreal]
        m = dict(shared)
        m["xl"] = xloc
        m["abias"], m["abiasm"] = tabs[q]
        in_maps.append(m)
    if _PROG.get("maps_only"):
        return in_maps
    res = run_bass_kernel_spmd(nc, in_maps, core_ids=list(range(8)))
    out = np.zeros((2, 8192, D), f)
    for c in range(8):
        b, q = c // 4, c % 4
        out[b, 2048 * q:2048 * (q + 1)] = np.asarray(res.results[c]["out"], f)
    return out
```
